# Optimizing a Trainium2 kernel written in Bass

```python
import jax, jax.numpy as jnp
from jax import lax
import numpy as np

D_MODEL = 1024
BATCH = 8
SEQ = 2048
DEPTH = 2
DEC_BATCH = 128
DEC_SEQ = 8
PAST_LEN = 16384
PAGE_SIZE = 128

MIX_WIDTH = D_MODEL
POOL_WIDTH = MIX_WIDTH // 2
CONV_WIDTH = MIX_WIDTH - POOL_WIDTH
POOL_WINDOWS = (2, 4, 8, 16)
N_POOL_GROUPS = len(POOL_WINDOWS)
POOL_GC = POOL_WIDTH // N_POOL_GROUPS
POOL_BUF = max(POOL_WINDOWS) - 1
CONV_K = 3
CONV_BUF = CONV_K - 1
N_CONV_HEADS = 8
D_FF = 4 * D_MODEL
PLE_DIM = 256
IN_COLS = POOL_WIDTH + 3 * CONV_WIDTH
EPS = 1e-6

kernel_name = "hybrid_pool_shortconv_decoder_step"


def rmsnorm(x, g):
    xf = x.astype(jnp.float32)
    y = xf * lax.rsqrt(jnp.mean(xf * xf, axis=-1, keepdims=True) + EPS)
    return (y * g.astype(jnp.float32)).astype(x.dtype)


def pool_mixer(u, buf, start, pool_w, pool_scale):
    T = u.shape[1]
    ext = jnp.concatenate([buf, u], axis=1)
    extf = ext.astype(jnp.float32)
    cs = jnp.concatenate([jnp.zeros_like(extf[:, :1]), jnp.cumsum(extf, axis=1)], axis=1)
    pos = start + jnp.arange(T)
    L = POOL_BUF
    outs = []
    for g, w in enumerate(POOL_WINDOWS):
        sl = slice(g * POOL_GC, (g + 1) * POOL_GC)
        win_sum = cs[:, L + 1:L + 1 + T, sl] - cs[:, L + 1 - w:L + 1 - w + T, sl]
        cnt = jnp.minimum(w, pos + 1).astype(jnp.float32)
        d = win_sum / cnt[None, :, None] - extf[:, L:, sl]
        outs.append(jnp.einsum('btc,cd->btd', d.astype(u.dtype), pool_w[g]))
    out = jnp.concatenate(outs, axis=-1) * pool_scale
    return out, ext[:, -L:]


def short_conv(v, buf, w, b):
    T = v.shape[1]
    ext = jnp.concatenate([buf, v], axis=1)
    z = b + sum(w[k] * ext[:, k:k + T] for k in range(CONV_K))
    return z, ext[:, -CONV_BUF:]


def trunk(x, p, pool_bufs, conv_bufs, start,
          norm_mix, w_in, pool_w, pool_scale, conv_w, conv_b, w_out,
          norm_mlp, w_up, w_down, norm_ple, w_ple_gate, w_ple_proj, norm_f):
    h = x
    new_pool, new_conv = [], []
    for i in range(DEPTH):
        a = rmsnorm(h, norm_mix[i])
        proj = jnp.einsum('btd,de->bte', a, w_in[i])
        u = proj[..., :POOL_WIDTH]
        bg = proj[..., POOL_WIDTH:POOL_WIDTH + CONV_WIDTH]
        cg = proj[..., POOL_WIDTH + CONV_WIDTH:POOL_WIDTH + 2 * CONV_WIDTH]
        v = proj[..., POOL_WIDTH + 2 * CONV_WIDTH:]
        y_pool, pb = pool_mixer(u, pool_bufs[i], start, pool_w[i], pool_scale[i])
        z, cb = short_conv(cg * v, conv_bufs[i], conv_w[i], conv_b[i])
        y_conv = bg * z
        mix = jnp.concatenate([y_pool, y_conv], axis=-1)
        h = h + jnp.einsum('btm,md->btd', mix, w_out[i])
        m = rmsnorm(h, norm_mlp[i])
        f = jax.nn.relu(jnp.einsum('btd,df->btf', m, w_up[i]))
        h = h + jnp.einsum('btf,fd->btd', f * f, w_down[i])
        gate = jax.nn.sigmoid(jnp.einsum('btd,de->bte', rmsnorm(h, norm_ple[i]), w_ple_gate[i]))
        h = h + gate * jnp.einsum('btq,qd->btd', p[i], w_ple_proj[i])
        new_pool.append(pb)
        new_conv.append(cb)
    return rmsnorm(h, norm_f), jnp.stack(new_pool), jnp.stack(new_conv)


def setup_inputs(seed: int = 0) -> dict:
    key = jax.random.key(seed)
    ks = jax.random.split(key, 24)
    n = lambda k, s, sc=1.0: jax.random.normal(k, s, jnp.float32) * sc
    return {
        "x_prompt": n(ks[0], (BATCH, SEQ, D_MODEL)),
        "x_sample": n(ks[1], (DEC_BATCH, DEC_SEQ, D_MODEL)),
        "state_pool": n(ks[2], (DEPTH, DEC_BATCH, POOL_BUF, POOL_WIDTH)),
        "state_conv": n(ks[3], (DEPTH, DEC_BATCH, CONV_BUF, CONV_WIDTH)),
        "p_prompt": n(ks[4], (DEPTH, BATCH, SEQ, PLE_DIM)),
        "p_sample": n(ks[5], (DEPTH, DEC_BATCH, DEC_SEQ, PLE_DIM)),
        "norm_mix": 1.0 + n(ks[6], (DEPTH, D_MODEL), 0.05),
        "w_in": n(ks[7], (DEPTH, D_MODEL, IN_COLS), D_MODEL ** -0.5),
        "pool_w": n(ks[8], (DEPTH, N_POOL_GROUPS, POOL_GC, POOL_GC), POOL_GC ** -0.5),
        "pool_scale": 1.0 + n(ks[9], (DEPTH, POOL_WIDTH), 0.1),
        "conv_w": n(ks[10], (DEPTH, CONV_K, CONV_WIDTH), CONV_K ** -0.5),
        "conv_b": n(ks[11], (DEPTH, CONV_WIDTH), 0.02),
        "w_out": n(ks[12], (DEPTH, MIX_WIDTH, D_MODEL), MIX_WIDTH ** -0.5),
        "norm_mlp": 1.0 + n(ks[13], (DEPTH, D_MODEL), 0.05),
        "w_up": n(ks[14], (DEPTH, D_MODEL, D_FF), D_MODEL ** -0.5),
        "w_down": n(ks[15], (DEPTH, D_FF, D_MODEL), D_FF ** -0.5),
        "norm_ple": 1.0 + n(ks[16], (DEPTH, D_MODEL), 0.05),
        "w_ple_gate": n(ks[17], (DEPTH, D_MODEL, D_MODEL), D_MODEL ** -0.5),
        "w_ple_proj": n(ks[18], (DEPTH, PLE_DIM, D_MODEL), PLE_DIM ** -0.5),
        "norm_f": 1.0 + n(ks[19], (D_MODEL,), 0.05),
    }


def reference(x_prompt, x_sample, state_pool, state_conv, p_prompt, p_sample,
              norm_mix, w_in, pool_w, pool_scale, conv_w, conv_b, w_out,
              norm_mlp, w_up, w_down, norm_ple, w_ple_gate, w_ple_proj, norm_f):
    weights = (norm_mix, w_in, pool_w, pool_scale, conv_w, conv_b, w_out,
               norm_mlp, w_up, w_down, norm_ple, w_ple_gate, w_ple_proj, norm_f)
    pool0 = jnp.zeros((DEPTH, x_prompt.shape[0], POOL_BUF, POOL_WIDTH), x_prompt.dtype)
    conv0 = jnp.zeros((DEPTH, x_prompt.shape[0], CONV_BUF, CONV_WIDTH), x_prompt.dtype)
    y_prompt, new_pool_prompt, new_conv_prompt = trunk(x_prompt, p_prompt, pool0, conv0, 0, *weights)
    y_sample, new_pool_sample, new_conv_sample = trunk(x_sample, p_sample, state_pool, state_conv, PAST_LEN, *weights)
    return (y_prompt, y_sample, new_pool_prompt, new_conv_prompt, new_pool_sample, new_conv_sample)
```

```python
import numpy as np
from contextlib import ExitStack
import concourse.bass as bass
import concourse.mybir as mybir
from concourse.bass_utils import run_bass_kernel_spmd

F32 = mybir.dt.float32
BF16 = mybir.dt.bfloat16
AF = mybir.ActivationFunctionType
ALU = mybir.AluOpType

NCORES = 8
D = 1024
KT = 8
SEQ = 2048
NS = 128
NSEQ = 16
DSEQ = 8
NTOK = SEQ + NS
DEPTH = 2
DFF = 4096
PLE = 256
EPS = 1e-6
POOL_BUF = 15
CONV_BUF = 2
ER = POOL_BUF + DSEQ
CR = CONV_BUF + DSEQ

TILES = [(0, 512), (512, 512), (1024, 512), (1536, 512), (2048, 128)]
MT = [(0, 512), (512, 512), (1024, 512), (1536, 384), (1920, 256)]
NSUB = NTOK // 128
GROUPS = [
    dict(tiles=[0, 1], kind="p", first=True, last=False),
    dict(tiles=[2, 3], kind="p", first=False, last=True),
    dict(tiles=[4], kind="s", first=False, last=False),
]

V_NMIX, V_NMLP, V_NPLE, V_NF, V_PSC, V_CW, V_CB = 0, 16, 32, 48, 56, 64, 88
NVEC = 96


class Ch:
    def __init__(self, name):
        self.name = name
        self.count = 0
        self.sem = None


class Buf:
    __slots__ = ("w", "r", "arena")

    def __init__(self, arena=False):
        self.w = {}
        self.r = {}
        self.arena = arena


def _merge(dst, src):
    for c, v in src.items():
        if dst.get(c, 0) < v:
            dst[c] = v


class Prog:
    ENGS = ["pe", "act", "dve", "pool", "sp"]

    def __init__(self):
        self.engs = {n: Ch(n) for n in self.ENGS}
        self.progs = {n: [] for n in self.ENGS}
        self.waited = {n: {} for n in self.ENGS}
        self.chans = []
        self.barrier = {}
        self.arena_bufs = []

    def chan(self, name):
        c = Ch(name)
        self.chans.append(c)
        return c

    def abuf(self):
        b = Buf(arena=True)
        self.arena_bufs.append(b)
        return b

    def stage_barrier(self):
        for b in self.arena_bufs:
            _merge(self.barrier, b.w)
            _merge(self.barrier, b.r)

    def op(self, eng, fn, reads=(), writes=(), signal=True, chan=None):
        need = {}
        for b in reads:
            _merge(need, b.w)
        for b in writes:
            _merge(need, b.w)
            _merge(need, b.r)
            if b.arena:
                _merge(need, self.barrier)
        waits = []
        wd = self.waited[eng]
        pe_ch = self.engs["pe"]
        for c, v in need.items():
            if eng == "pe" and c is pe_ch:
                continue
            if wd.get(c, 0) < v:
                waits.append((c, v))
                wd[c] = v
        if chan is not None:
            chan.count += 16
            tok = {chan: chan.count}
            sig = (chan, 16)
        else:
            ch = self.engs[eng]
            tok = {ch: ch.count + 1}
            if signal:
                ch.count += 1
                sig = (ch, 1)
            else:
                sig = None
        self.progs[eng].append((waits, fn, sig))
        for b in reads:
            _merge(b.r, tok)
        for b in writes:
            b.w = dict(tok)
            b.r = {}
        return tok


def build():
    nc = bass.Bass("TRN2", target_bir_lowering=False)
    dt_in = lambda name, shape: nc.dram_tensor(name, shape, F32, kind="ExternalInput").ap()
    dt_out = lambda name, shape: nc.dram_tensor(name, shape, F32, kind="ExternalOutput").ap()
    x_p = dt_in("x_p", [SEQ, D])
    x_s = dt_in("x_s", [NS, D])
    sp_in = dt_in("sp_in", [DEPTH, NSEQ * POOL_BUF, 512])
    sc_in = dt_in("sc_in", [DEPTH, NSEQ * CONV_BUF, 512])
    pp = dt_in("pp", [DEPTH, SEQ, PLE])
    psm = dt_in("psm", [DEPTH, NS, PLE])
    vecs = dt_in("vecs", [NVEC, 128])
    w_in = dt_in("w_in", [DEPTH, D, 2048])
    pool_w = dt_in("pool_w", [DEPTH, 4, 128, 128])
    w_out = dt_in("w_out", [DEPTH, D, D])
    w_up = dt_in("w_up", [DEPTH, D, DFF])
    w_down = dt_in("w_down", [DEPTH, DFF, D])
    w_gate = dt_in("w_gate", [DEPTH, D, D])
    w_ple = dt_in("w_ple", [DEPTH, PLE, D])
    y_p = dt_out("y_p", [SEQ, D])
    y_s = dt_out("y_s", [NS, D])
    npp = dt_out("npp", [DEPTH, POOL_BUF, 512])
    ncp = dt_out("ncp", [DEPTH, CONV_BUF, 512])
    nps = dt_out("nps", [DEPTH, NSEQ, POOL_BUF, 512])
    ncs = dt_out("ncs", [DEPTH, NSEQ * CONV_BUF, 512])

    P = Prog()
    AW = 19904
    es = ExitStack()
    with es:
        sb = lambda name, shape, dt: es.enter_context(nc.sbuf_tensor(name, shape, dt))
        h = sb("h", [128, KT, NTOK], F32)
        ring = sb("ring", [128, 3, 8, 512], BF16)
        pT_t = sb("pT", [128, 2, NTOK], BF16)
        wple = sb("wple", [128, 2, D], BF16)
        poolw = sb("poolw", [128, 4, 128], BF16)
        sq = sb("sq", [128, 4, 512], BF16)
        rstd = sb("rstd", [128, 2, 512], F32)
        EH = sb("EH", [128, 4, NSEQ * POOL_BUF], F32)
        CVH = sb("CVH", [128, DEPTH, 4, NSEQ * CONV_BUF], F32)
        carryE = sb("carryE", [128, 4, 16], F32)
        carryCV = sb("carryCV", [128, 4, 2], F32)
        vec = sb("vec", [128, NVEC], F32)
        ident = sb("ident", [128, 128], F32)
        ones = sb("ones", [128, 128], BF16)
        invcnt = sb("invcnt", [128, 16], F32)
        so_stg = sb("so_stg", [128, 2, 512], F32)
        pstg = sb("pstg", [128, 4, PLE], F32)
        arena = sb("arena", [128, AW], F32)
        psum = [es.enter_context(nc.psum_tensor(f"ps{i}", [128, 512], F32)) for i in range(8)]

        def af(off, n):
            return arena[:, off:off + n]

        def ab(off, nwords):
            return arena[:, off:off + nwords].bitcast(BF16)

        hB = [[Buf() for _ in range(NSUB)] for _ in range(KT)]

        def hb(k, toff, tw):
            return hB[k][toff // 128:(toff + tw) // 128]

        psB = [Buf() for _ in range(8)]
        ringB = [Buf() for _ in range(3)]
        wpleB, poolwB = Buf(), Buf()
        sqB = [Buf() for _ in range(4)]
        rstdB = [Buf(), Buf()]
        EHB, CVHB, vecB, identB, onesB, invB = Buf(), Buf(), Buf(), Buf(), Buf(), Buf()
        carryEB = [Buf() for _ in range(4)]
        carryCVB = [Buf() for _ in range(4)]
        soB = [Buf(), Buf()]
        pstgB = [Buf() for _ in range(4)]
        pTB = [[Buf() for _ in range(NSUB)] for _ in range(2)]

        def ptb(kk, toff, tw):
            return pTB[kk][toff // 128:(toff + tw) // 128]


        st = dict(bank=0, sq=0, norm=0, r=0, g=0, pst=0, y=0)

        reserved = set()

        def next_bank():
            while True:
                b = st["bank"]
                st["bank"] = (b + 1) % 8
                if b not in reserved:
                    return b

        ring_ch = [P.chan(f"ring{i}") for i in range(3)]
        poolw_ch = P.chan("poolw")
        wple_ch = P.chan("wple")
        setup_ch = [P.chan(f"su{i}") for i in range(3)]
        sui = iter(setup_ch)
        ld_ch = [P.chan(f"ld{i}") for i in range(8)]
        pl_ch = [P.chan(f"pl{i}") for i in range(4)]
        st_ch = [P.chan(f"st{i}") for i in range(2)]
        so_ch = [P.chan(f"so{i}") for i in range(2)]
        d2d_ch = P.chan("d2d")

        def vcol(c):
            return vec[:, c:c + 1]

        def wview(w_l, c0):
            return w_l.rearrange("(k p) n -> p k n", p=128)[:, :, c0:c0 + 512]

        wq = []
        for l in range(DEPTH):
            for _g in GROUPS:
                wq += [wview(w_in[l], 0), wview(w_in[l], 1024), wview(w_in[l], 1536), wview(w_in[l], 512),
                       wview(w_out[l], 0), wview(w_out[l], 512)]
            for q in range(4):
                wq += [wview(w_up[l], q * 1024), wview(w_up[l], q * 1024 + 512),
                       wview(w_down[l][q * 1024:(q + 1) * 1024, :], 0), wview(w_down[l][q * 1024:(q + 1) * 1024, :], 512)]
            wq += [wview(w_gate[l], 0), wview(w_gate[l], 512)]
        ws = dict(issued=0, taken=0, free=[True] * 3, gate=())

        def ws_pump():
            while ws["issued"] < len(wq) and ws["free"][ws["issued"] % 3]:
                i = ws["issued"]
                s_ = i % 3
                P.op("pool", lambda e, s_=s_, src=wq[i]: e.dma_start(out=ring[:, s_], in_=src), reads=(ws["gate"] if i in (1, 2) else ()),
                     writes=[ringB[s_]], chan=ring_ch[s_])
                ws["free"][s_] = False
                ws["issued"] += 1

        def ws_take():
            i = ws["taken"]
            ws["taken"] += 1
            assert i < ws["issued"], "weight block not yet issued"
            return i % 3

        def ws_release(s_):
            ws["free"][s_] = True
            ws_pump()

        P.op("pool", lambda e: e.memset(ident[:], 0.0), writes=[identB])
        P.op("pool", lambda e: e.affine_select(out=ident[:], in_=ident[:], pattern=[[-1, 128]],
                                               compare_op=ALU.not_equal, fill=1.0, base=0,
                                               channel_multiplier=1), writes=[identB])
        P.op("pool", lambda e: e.memset(ones[:], 1.0), writes=[onesB])
        for t in range(16):
            P.op("pool", lambda e, t=t: e.memset(invcnt[:, t:t + 1], 1.0 / (t + 1)), writes=[invB])

        XSTG = 8192
        SPSTG = 16384
        vstg = pstg[:, 0, 0:128]
        vstgB = pstgB[0]
        P.op("sp", lambda e: e.dma_start(out=vstg[0:NVEC, :], in_=vecs), writes=[vstgB], chan=next(sui))
        b = next_bank()
        P.op("pe", lambda e, b=b: e.transpose(out=psum[b][:, 0:NVEC], in_=vstg[0:NVEC, :], identity=ident[0:NVEC, 0:NVEC]),
             reads=[vstgB, identB], writes=[psB[b]])
        P.op("act", lambda e, b=b: e.activation(out=vec[:], in_=psum[b][:, 0:NVEC], func=AF.Copy),
             reads=[psB[b]], writes=[vecB])

        pstg2 = pstg[:].rearrange("p a c -> p (a c)").rearrange("p (h c) -> p h c", h=2)

        def load_sample_state(l):
            for hf in range(2):
                stg = pstg2[:, hf, :]
                sBs = [pstgB[2 * hf], pstgB[2 * hf + 1]]
                P.op("sp", lambda e, stg=stg, hf=hf: e.dma_start(out=stg[0:120, :], in_=sp_in[l, hf * 120:(hf + 1) * 120, :]),
                     writes=sBs, chan=pl_ch[2 * hf])
                b = next_bank()
                for g in range(4):
                    P.op("pe", lambda e, b=b, g=g, stg=stg: e.transpose(out=psum[b][:, g * 120:(g + 1) * 120], in_=stg[0:120, g * 128:(g + 1) * 128],
                                                                        identity=ident[0:120, 0:120]),
                         reads=sBs + [identB], writes=[psB[b]], signal=(g == 3))
                P.op("act", lambda e, b=b, hf=hf: e.activation(out=EH[:, :, hf * 120:(hf + 1) * 120],
                                                               in_=psum[b][:, 0:480].rearrange("p (g c) -> p g c", g=4), func=AF.Copy),
                     reads=[psB[b]], writes=[EHB])

        load_sample_state(0)
        for l in range(DEPTH):
            stg = so_stg[:, l, :]
            sB_ = soB[l]
            P.op("sp", lambda e, stg=stg, l=l: e.dma_start(out=stg[0:32, :], in_=sc_in[l]), writes=[sB_], chan=next(sui))
            b = next_bank()
            for j in range(4):
                P.op("pe", lambda e, b=b, j=j, stg=stg: e.transpose(out=psum[b][:, j * 32:(j + 1) * 32], in_=stg[0:32, j * 128:(j + 1) * 128],
                                                                    identity=ident[0:32, 0:32]),
                     reads=[sB_, identB], writes=[psB[b]], signal=(j == 3))
            P.op("act", lambda e, b=b, l=l: e.activation(out=CVH[:, l], in_=psum[b][:, 0:128].rearrange("p (g c) -> p g c", g=4), func=AF.Copy),
                 reads=[psB[b]], writes=[CVHB])
            P.op("sp", lambda e, l=l: e.dma_start(out=nps[l, :, 0:POOL_BUF - DSEQ, :],
                                                  in_=sp_in[l].rearrange("(s r) c -> s r c", r=POOL_BUF)[:, DSEQ:POOL_BUF, :]),
                 chan=d2d_ch)

        xstgB = [P.abuf() for _ in range(8)]
        xstg_all = [af(XSTG + i * 1024, 1024) for i in range(8)]
        for ti, (toff, tw) in enumerate(TILES[0:2]):
            nsub = tw // 128
            xo = (ti % 2) * 4
            xstg = xstg_all[xo:xo + 4]
            xsB = xstgB[xo:xo + 4]
            for s_ in range(nsub):
                src = x_p[toff + s_ * 128: toff + (s_ + 1) * 128, :] if toff < SEQ else x_s
                P.op("sp", lambda e, s_=s_, src=src, xstg=xstg: e.dma_start(out=xstg[s_], in_=src), writes=[xsB[s_]], chan=ld_ch[xo + s_])
            if nsub == 4:
                for k in range(KT):
                    b = next_bank()
                    for s_ in range(4):
                        P.op("pe", lambda e, b=b, s_=s_, k=k, xstg=xstg: e.transpose(out=psum[b][:, s_ * 128:(s_ + 1) * 128], in_=xstg[s_][:, k * 128:(k + 1) * 128], identity=ident[:]),
                             reads=[xsB[s_], identB], writes=[psB[b]], signal=(s_ == 3))
                    if ti == 1 and k == 0:
                        gateB = Buf()
                        gateB.w = dict(psB[b].w)
                        ws["gate"] = (gateB,)
                    if k % 2 == 0:
                        P.op("act", lambda e, b=b, k=k, toff=toff: e.activation(out=h[:, k, toff:toff + 512], in_=psum[b][:], func=AF.Copy),
                             reads=[psB[b]], writes=hb(k, toff, 512))
                    else:
                        P.op("dve", lambda e, b=b, k=k, toff=toff: e.tensor_copy(out=h[:, k, toff:toff + 512], in_=psum[b][:]),
                             reads=[psB[b]], writes=hb(k, toff, 512))
            else:
                for kh in range(2):
                    b = next_bank()
                    for kk in range(4):
                        k = kh * 4 + kk
                        P.op("pe", lambda e, b=b, kk=kk, k=k, xstg=xstg: e.transpose(out=psum[b][:, kk * 128:(kk + 1) * 128], in_=xstg[0][:, k * 128:(k + 1) * 128], identity=ident[:]),
                             reads=[xsB[0], identB], writes=[psB[b]], signal=(kk == 3))
                    P.op("act", lambda e, b=b, kh=kh, toff=toff: e.activation(out=h[:, kh * 4:(kh + 1) * 4, toff:toff + 128],
                                                                              in_=psum[b][:].rearrange("p (k c) -> p k c", k=4), func=AF.Copy),
                         reads=[psB[b]], writes=[hb(kh * 4 + kk, toff, 128)[0] for kk in range(4)])

        ws_pump()

        xl = [pT_t[:, j, :].bitcast(F32)[:, 0:1024] for j in range(2)]
        xl_ch = [P.chan(f"xl{j}") for j in range(2)]
        late_list = []
        for ti in range(2, len(TILES)):
            toff, tw = TILES[ti]
            for s_ in range(tw // 128):
                src = x_p[toff + s_ * 128: toff + (s_ + 1) * 128, :] if toff < SEQ else x_s
                late_list.append((ti, toff + s_ * 128, src))

        def late_dma(n):
            ti, col0, src = late_list[n]
            j = n % 2
            P.op("sp", lambda e, j=j, src=src: e.dma_start(out=xl[j], in_=src), writes=list(pTB[j]), chan=xl_ch[j])

        def late_step(n):
            def f_():
                ti, col0, src = late_list[n]
                j = n % 2
                if n + 1 < len(late_list):
                    late_dma(n + 1)
                for hf in range(2):
                    b = next_bank()
                    for kk in range(4):
                        k = hf * 4 + kk
                        P.op("pe", lambda e, b=b, kk=kk, k=k: e.transpose(out=psum[b][:, kk * 128:(kk + 1) * 128], in_=xl[j][:, k * 128:(k + 1) * 128], identity=ident[:]),
                             reads=list(pTB[j]) + [identB], writes=[psB[b]], signal=(kk == 3))
                    P.op("act", lambda e, b=b, hf=hf: e.activation(out=h[:, hf * 4:(hf + 1) * 4, col0:col0 + 128],
                                                                   in_=psum[b][:].rearrange("p (k c) -> p k c", k=4), func=AF.Copy),
                         reads=[psB[b]], writes=[hb(hf * 4 + kk, col0, 128)[0] for kk in range(4)])
            return f_

        late_dma(0)
        late_hooks = [late_step(n) for n in range(len(late_list))]

        def norm_tile(ti, dst_ap_fn, dstB, gbase, T=TILES):
            toff, tw = T[ti]
            b = next_bank()
            for k in range(KT):
                q = st["sq"]
                st["sq"] = (q + 1) % 4
                P.op("act", lambda e, q=q, k=k: e.activation(out=sq[:, q, 0:tw], in_=h[:, k, toff:toff + tw], func=AF.Square),
                     reads=hb(k, toff, tw), writes=[sqB[q]])
                P.op("pe", lambda e, q=q, k=k, b=b: e.matmul(psum[b][:, 0:tw], lhsT=ones[:], rhs=sq[:, q, 0:tw], start=(k == 0), stop=(k == KT - 1)),
                     reads=[sqB[q], onesB], writes=[psB[b]], signal=True)
            n_ = st["norm"]
            st["norm"] = 1 - n_
            P.op("act", lambda e, b=b, n_=n_: e.activation(out=rstd[:, n_, 0:tw], in_=psum[b][:, 0:tw], func=AF.Ln, scale=1.0 / D, bias=EPS),
                 reads=[psB[b]], writes=[rstdB[n_]])
            P.op("act", lambda e, n_=n_: e.activation(out=rstd[:, n_, 0:tw], in_=rstd[:, n_, 0:tw], func=AF.Exp, scale=-0.5),
                 reads=[rstdB[n_]], writes=[rstdB[n_]])
            for k in range(KT):
                P.op("dve", lambda e, k=k, n_=n_: e.scalar_tensor_tensor(out=dst_ap_fn(k), in0=h[:, k, toff:toff + tw], scalar=vcol(gbase + k),
                                                                         in1=rstd[:, n_, 0:tw], op0=ALU.mult, op1=ALU.mult),
                     reads=hb(k, toff, tw) + [rstdB[n_], vecB], writes=[dstB[k]])

        def norm_steps(ti, ctx, T=TILES):
            toff, tw = T[ti]

            def squares(hf):
                for j in range(4):
                    k = hf * 4 + j
                    P.op("act", lambda e, j=j, k=k: e.activation(out=sq[:, j, 0:tw], in_=h[:, k, toff:toff + tw], func=AF.Square),
                         reads=hb(k, toff, tw), writes=[sqB[j]])

            def stats(hf):
                b = ctx["b"]
                for j in range(4):
                    k = hf * 4 + j
                    P.op("pe", lambda e, j=j, k=k, b=b: e.matmul(psum[b][:, 0:tw], lhsT=ones[:], rhs=sq[:, j, 0:tw], start=(k == 0), stop=(k == KT - 1)),
                         reads=[sqB[j], onesB], writes=[psB[b]], signal=True)

            def s1():
                ctx["b"] = next_bank()
                reserved.add(ctx["b"])
                squares(0)

            def s2():
                stats(0)
                squares(1)

            def s3():
                stats(1)
                b = ctx["b"]
                n_ = st["norm"]
                st["norm"] = 1 - n_
                ctx["n"] = n_
                P.op("act", lambda e: e.activation(out=rstd[:, n_, 0:tw], in_=psum[b][:, 0:tw], func=AF.Ln, scale=1.0 / D, bias=EPS),
                     reads=[psB[b]], writes=[rstdB[n_]])
                P.op("act", lambda e: e.activation(out=rstd[:, n_, 0:tw], in_=rstd[:, n_, 0:tw], func=AF.Exp, scale=-0.5),
                     reads=[rstdB[n_]], writes=[rstdB[n_]])
                reserved.discard(b)

            return [s1, s2, s3]

        def norm_apply(ti, n_, dst_ap_fn, dstB, gbase, T=TILES):
            toff, tw = T[ti]
            for k in range(KT):
                P.op("dve", lambda e, k=k: e.scalar_tensor_tensor(out=dst_ap_fn(k), in0=h[:, k, toff:toff + tw], scalar=vcol(gbase + k),
                                                                  in1=rstd[:, n_, 0:tw], op0=ALU.mult, op1=ALU.mult),
                     reads=hb(k, toff, tw) + [rstdB[n_], vecB], writes=[dstB[k]])

        def mm_group(lhs_list, rhs_list, width):
            b = next_bank()
            n = len(lhs_list)
            for i in range(n):
                (lap, lB), (rap, rB) = lhs_list[i], rhs_list[i]
                P.op("pe", lambda e, b=b, lap=lap, rap=rap, i=i: e.matmul(psum[b][:, 0:width], lhsT=lap, rhs=rap, start=(i == 0), stop=(i == n - 1)),
                     reads=(lB if isinstance(lB, list) else [lB]) + (rB if isinstance(rB, list) else [rB]), writes=[psB[b]], signal=(i == n - 1))
            return b

        A_OFF, MIX_OFF = 0, 4096
        E_OFF = [8192, 9232]
        T_OFF = [10272, 11312]
        D_OFF = [12352, 12864]
        CVE_OFF = [13376, 14404]
        Z_OFF = 15432
        C_OFF = [16460, 16972]
        FIX_OFF = 17484
        BS_OFF = [17800, 18824]
        UD_OFF = [17500, 17628]
        CD_OFF = 17756
        a_t = ab(A_OFF, 4096).rearrange("p (k n) -> p k n", k=KT)
        mix_t = ab(MIX_OFF, 4096).rearrange("p (k n) -> p k n", k=KT)
        aB = [[P.abuf() for _ in range(2)] for _ in range(KT)]
        mixB = [[P.abuf() for _ in range(2)] for _ in range(KT)]
        EtB = [[P.abuf() for _ in range(2)] for _ in range(2)]
        EhB = [P.abuf() for _ in range(2)]
        TB = [P.abuf() for _ in range(2)]
        DB = [[P.abuf() for _ in range(2)] for _ in range(2)]
        CtB = [[P.abuf() for _ in range(2)] for _ in range(2)]
        ChB = [P.abuf() for _ in range(2)]
        ZB = P.abuf()
        CsB = [P.abuf() for _ in range(2)]
        fixB = P.abuf()
        BsB = [[P.abuf() for _ in range(2)] for _ in range(2)]
        UdB = [P.abuf(), P.abuf()]
        CdB = P.abuf()
        M_OFF, F_OFF = 0, 8704
        R_OFF = [17408, 17920, 18432]
        m_t = ab(M_OFF, 8704).rearrange("p (k n) -> p k n", k=KT)
        f_t = ab(F_OFF, 8704).rearrange("p (k n) -> p k n", k=KT)
        mB = [[P.abuf() for _ in TILES] for _ in range(KT)]
        fB = [[P.abuf() for _ in TILES] for _ in range(KT)]
        rB = [P.abuf() for _ in range(3)]
        G_OFF = [8704, 9216]
        TMP_OFF = [9728, 10240]
        nB = mB
        n_t = m_t
        gB = [P.abuf() for _ in range(2)]
        tmpB = [P.abuf() for _ in range(2)]
        YFM_OFF = 10752
        YTM_OFF = [14848, 15872]
        yfmB = [P.abuf() for _ in range(KT)]
        ytmB = [P.abuf() for _ in range(2)]

        for l in range(DEPTH):
            P.stage_barrier()
            P.op("pool", lambda e, l=l: e.dma_start(out=poolw[:], in_=pool_w[l].rearrange("g c d -> c g d")), writes=[poolwB], chan=poolw_ch)
            P.op("pool", lambda e, l=l: e.dma_start(out=wple[:], in_=w_ple[l].rearrange("(k p) n -> p k n", p=128)), writes=[wpleB], chan=wple_ch)
            def do_group(l, grp):
                kind = grp["kind"]
                tl = grp["tiles"]
                ntl = len(tl)
                g0 = TILES[tl[0]][0]
                gw = sum(TILES[t][1] for t in tl)
                if kind == "p":
                    EW, CW = POOL_BUF + gw, CONV_BUF + gw
                else:
                    EW, CW = NSEQ * ER, NSEQ * CR

                def lo(il):
                    toff, tw = TILES[tl[il]]
                    return toff - g0, tw

                def etok(off, il):
                    o, tw = lo(il)
                    if kind == "p":
                        return af(off + POOL_BUF + o, tw)
                    return af(off, NSEQ * ER).rearrange("p (s r) -> p s r", r=ER)[:, :, POOL_BUF:ER]

                def ctok(off, il, hist):
                    o, tw = lo(il)
                    if kind == "p":
                        return af(off + hist + o, tw)
                    return af(off, NSEQ * CR).rearrange("p (s r) -> p s r", r=CR)[:, :, hist:hist + DSEQ]

                def pview(b, il):
                    o, tw = lo(il)
                    if kind == "p":
                        return psum[b][:, 0:tw]
                    return psum[b][:, 0:tw].rearrange("p (s t) -> p s t", t=DSEQ)

                def dense(ap2, il):
                    o, tw = lo(il)
                    v = ap2[:, o:o + tw]
                    if kind == "s":
                        return v.rearrange("p (s t) -> p s t", t=DSEQ)
                    return v

                S = {}

                nctx = [dict() for _ in tl]

                def g_steps():
                    out = []
                    for il, ti in enumerate(tl):
                        out += norm_steps(ti, nctx[il])
                    return out

                def g_apply():
                    for il, ti in enumerate(tl):
                        o, tw = lo(il)
                        norm_apply(ti, nctx[il]["n"], lambda k, o=o, tw=tw: a_t[:, k, o:o + tw], [aB[k][il] for k in range(KT)], V_NMIX + l * 8)

                def inproj(slot, c, il):
                    o, tw = lo(il)
                    lhs = [(ring[:, slot, k, c * 128:(c + 1) * 128], ringB[slot]) for k in range(KT)]
                    rhs = [(a_t[:, k, o:o + tw], aB[k][il]) for k in range(KT)]
                    return mm_group(lhs, rhs, tw)

                def u_chunk(g):
                    es_ = g % 2
                    eoff = E_OFF[es_]
                    if kind == "p":
                        if grp["first"]:
                            P.op("dve", lambda e: e.memset(af(eoff, POOL_BUF), 0.0), writes=[EhB[es_]])
                        else:
                            P.op("dve", lambda e: e.tensor_copy(out=af(eoff, POOL_BUF), in_=carryE[:, g, 0:POOL_BUF]),
                                 reads=[carryEB[g]], writes=[EhB[es_]])
                    else:
                        P.op("dve", lambda e: e.tensor_copy(
                            out=af(eoff, NSEQ * ER).rearrange("p (s r) -> p s r", r=ER)[:, :, 0:POOL_BUF],
                            in_=EH[:, g, :].rearrange("p (s r) -> p s r", r=POOL_BUF)),
                            reads=[EHB], writes=[EhB[es_]])
                    for il in range(ntl):
                        b = inproj(S["u"], g, il)
                        P.op("act", lambda e, b=b, il=il: e.activation(out=etok(eoff, il), in_=pview(b, il), func=AF.Copy),
                             reads=[psB[b]], writes=[EtB[es_][il]])
                        if kind == "s":
                            P.op("act", lambda e, b=b: e.activation(out=af(UD_OFF[es_], 128), in_=psum[b][:, 0:128], func=AF.Copy),
                                 reads=[psB[b]], writes=[UdB[es_]])

                def pool_ew(g):
                    es_ = g % 2
                    eoff = E_OFF[es_]
                    nlev = g + 1
                    Ereads = [EhB[es_]] + [EtB[es_][il] for il in range(ntl)]
                    src_off, srcB = eoff, Ereads
                    for L in range(1, nlev + 1):
                        sh = 1 << (L - 1)
                        v = (1 << L) - 1
                        ts_ = (L - 1) % 2
                        doff = T_OFF[ts_]
                        P.op("dve", lambda e, doff=doff, src_off=src_off, v=v, sh=sh: e.tensor_tensor(
                            out=af(doff + v, EW - v), in0=af(src_off + v, EW - v), in1=af(src_off + v - sh, EW - v), op=ALU.add),
                            reads=srcB, writes=[TB[ts_]])
                        src_off, srcB = doff, [TB[ts_]]
                    w_ = 1 << nlev
                    doff_ = D_OFF[es_]
                    dten = ab(doff_, 512)
                    for il in range(ntl):
                        P.op("dve", lambda e, il=il, src_off=src_off: e.scalar_tensor_tensor(
                            out=dense(dten, il), in0=etok(src_off, il), scalar=1.0 / w_, in1=etok(eoff, il), op0=ALU.mult, op1=ALU.subtract),
                            reads=srcB + [EtB[es_][il]], writes=[DB[es_][il]])
                    if kind == "p" and grp["first"]:
                        nfix = w_ - 1
                        P.op("dve", lambda e, src_off=src_off: e.tensor_tensor(
                            out=af(FIX_OFF, nfix), in0=af(src_off + POOL_BUF, nfix), in1=invcnt[:, 0:nfix], op=ALU.mult),
                            reads=srcB + [invB], writes=[fixB])
                        P.op("dve", lambda e: e.tensor_tensor(
                            out=dten[:, 0:nfix], in0=af(FIX_OFF, nfix), in1=af(eoff + POOL_BUF, nfix), op=ALU.subtract),
                            reads=[fixB, EtB[es_][0]], writes=[DB[es_][0]])
                    if kind == "p" and not grp["last"]:
                        P.op("dve", lambda e: e.tensor_copy(out=carryE[:, g, 0:POOL_BUF], in_=af(eoff + gw, POOL_BUF)),
                             reads=[EtB[es_][ntl - 1]], writes=[carryEB[g]])

                def pool_state(g):
                    es_ = g % 2
                    eoff = E_OFF[es_]
                    if kind == "p" and grp["last"]:
                        b = next_bank()
                        P.op("pe", lambda e, b=b: e.transpose(out=psum[b][0:POOL_BUF, 0:128], in_=af(eoff + gw, POOL_BUF), identity=ident[:]),
                             reads=[EtB[es_][ntl - 1], identB], writes=[psB[b]])
                        P.op("act", lambda e, b=b: e.activation(out=so_stg[0:POOL_BUF, 0, g * 128:(g + 1) * 128], in_=psum[b][0:POOL_BUF, 0:128], func=AF.Copy),
                             reads=[psB[b]], writes=[soB[0]])
                        if g == 3:
                            P.op("sp", lambda e: e.dma_start(out=npp[l], in_=so_stg[0:POOL_BUF, 0, :]), reads=[soB[0]], chan=so_ch[0])
                    elif kind == "s":
                        b = next_bank()
                        P.op("pe", lambda e, b=b: e.transpose(out=psum[b][:, 0:128], in_=af(UD_OFF[es_], 128), identity=ident[:]),
                             reads=[UdB[es_], identB], writes=[psB[b]])
                        P.op("act", lambda e, b=b: e.activation(out=so_stg[:, 0, g * 128:(g + 1) * 128], in_=psum[b][:, 0:128], func=AF.Copy),
                             reads=[psB[b]], writes=[soB[0]])
                        if g == 3:
                            for s_ in range(NSEQ):
                                P.op("sp", lambda e, s_=s_: e.dma_start(out=nps[l, s_, POOL_BUF - DSEQ:POOL_BUF, :], in_=so_stg[s_ * DSEQ:(s_ + 1) * DSEQ, 0, :]),
                                     reads=[soB[0]], chan=so_ch[0])

                def pool_mm(g):
                    es_ = g % 2
                    dten = ab(D_OFF[es_], 512)
                    for il in range(ntl):
                        o, tw = lo(il)
                        b = mm_group([(poolw[:, g, :], poolwB)], [(dten[:, o:o + tw], DB[es_][il])], tw)
                        P.op("act", lambda e, b=b, o=o, tw=tw: e.activation(out=mix_t[:, g, o:o + tw], in_=psum[b][:, 0:tw], func=AF.Identity,
                                                                            scale=vcol(V_PSC + l * 4 + g)),
                             reads=[psB[b], vecB], writes=[mixB[g][il]])

                def cv_chunk(j):
                    cs_ = j % 2
                    coff = CVE_OFF[cs_]
                    if kind == "p":
                        if grp["first"]:
                            P.op("dve", lambda e: e.memset(af(coff, CONV_BUF), 0.0), writes=[ChB[cs_]])
                        else:
                            P.op("dve", lambda e: e.tensor_copy(out=af(coff, CONV_BUF), in_=carryCV[:, j, :]),
                                 reads=[carryCVB[j]], writes=[ChB[cs_]])
                    else:
                        P.op("dve", lambda e: e.tensor_copy(
                            out=af(coff, NSEQ * CR).rearrange("p (s r) -> p s r", r=CR)[:, :, 0:CONV_BUF],
                            in_=CVH[:, l, j, :].rearrange("p (s r) -> p s r", r=CONV_BUF)),
                            reads=[CVHB], writes=[ChB[cs_]])
                    for il in range(ntl):
                        o, tw = lo(il)
                        bc = inproj(S["c"], j, il)
                        ci = st["r"] % 2
                        st["r"] += 1
                        P.op("act", lambda e, bc=bc, ci=ci, tw=tw: e.activation(out=af(C_OFF[ci], tw), in_=psum[bc][:, 0:tw], func=AF.Copy),
                             reads=[psB[bc]], writes=[CsB[ci]])
                        bv = inproj(S["v"], j, il)
                        P.op("dve", lambda e, bv=bv, ci=ci, il=il: e.tensor_tensor(out=ctok(coff, il, CONV_BUF), in0=dense(af(C_OFF[ci], 512), il) if kind == "s" else af(C_OFF[ci], lo(il)[1]),
                                                                                   in1=pview(bv, il), op=ALU.mult),
                             reads=[CsB[ci], psB[bv]], writes=[CtB[cs_][il]])
                    if kind == "p" and not grp["last"]:
                        P.op("dve", lambda e: e.tensor_copy(out=carryCV[:, j, :], in_=af(coff + gw, CONV_BUF)),
                             reads=[CtB[cs_][ntl - 1]], writes=[carryCVB[j]])

                def conv_state(j):
                    cs_ = j % 2
                    coff = CVE_OFF[cs_]
                    if kind == "p" and grp["last"]:
                        b = next_bank()
                        P.op("pe", lambda e, b=b: e.transpose(out=psum[b][0:CONV_BUF, 0:128], in_=af(coff + gw, CONV_BUF), identity=ident[:]),
                             reads=[CtB[cs_][ntl - 1], identB], writes=[psB[b]])
                        P.op("act", lambda e, b=b: e.activation(out=so_stg[0:CONV_BUF, 1, j * 128:(j + 1) * 128], in_=psum[b][0:CONV_BUF, 0:128], func=AF.Copy),
                             reads=[psB[b]], writes=[soB[1]])
                        if j == 3:
                            P.op("sp", lambda e: e.dma_start(out=ncp[l], in_=so_stg[0:CONV_BUF, 1, :]), reads=[soB[1]], chan=so_ch[1])
                    elif kind == "s":
                        b = next_bank()
                        P.op("dve", lambda e: e.tensor_copy(out=af(CD_OFF, NSEQ * CONV_BUF).rearrange("p (s r) -> p s r", r=CONV_BUF),
                                                            in_=af(coff, NSEQ * CR).rearrange("p (s r) -> p s r", r=CR)[:, :, DSEQ:CR]),
                             reads=[CtB[cs_][0]], writes=[CdB])
                        P.op("pe", lambda e, b=b: e.transpose(out=psum[b][0:NSEQ * CONV_BUF, 0:128],
                                                              in_=af(CD_OFF, NSEQ * CONV_BUF), identity=ident[:]),
                             reads=[CdB, identB], writes=[psB[b]])
                        P.op("act", lambda e, b=b: e.activation(out=so_stg[0:NSEQ * CONV_BUF, 1, j * 128:(j + 1) * 128], in_=psum[b][0:NSEQ * CONV_BUF, 0:128], func=AF.Copy),
                             reads=[psB[b]], writes=[soB[1]])
                        if j == 3:
                            P.op("sp", lambda e: e.dma_start(out=ncs[l], in_=so_stg[0:NSEQ * CONV_BUF, 1, :]), reads=[soB[1]], chan=so_ch[1])

                def conv_z(j):
                    cs_ = j % 2
                    coff = CVE_OFF[cs_]
                    Creads = [ChB[cs_]] + [CtB[cs_][il] for il in range(ntl)]
                    n_ = CW - 2
                    cwb = V_CW + l * 12 + j
                    P.op("act", lambda e: e.activation(out=af(Z_OFF, n_), in_=af(coff + 2, n_), func=AF.Identity, scale=vcol(cwb + 8), bias=vcol(V_CB + l * 4 + j)),
                         reads=Creads + [vecB], writes=[ZB])
                    P.op("dve", lambda e: e.scalar_tensor_tensor(out=af(Z_OFF, n_), in0=af(coff + 1, n_), scalar=vcol(cwb + 4), in1=af(Z_OFF, n_),
                                                                 op0=ALU.mult, op1=ALU.add),
                         reads=Creads + [vecB, ZB], writes=[ZB])
                    P.op("dve", lambda e: e.scalar_tensor_tensor(out=af(Z_OFF, n_), in0=af(coff, n_), scalar=vcol(cwb), in1=af(Z_OFF, n_),
                                                                 op0=ALU.mult, op1=ALU.add),
                         reads=Creads + [vecB, ZB], writes=[ZB])

                def b_chunk(j):
                    bs_ = j % 2
                    for il in range(ntl):
                        o, tw = lo(il)
                        bb = inproj(S["b"], j, il)
                        P.op("act", lambda e, bb=bb, o=o, tw=tw: e.activation(out=af(BS_OFF[bs_] + o, tw), in_=psum[bb][:, 0:tw], func=AF.Copy),
                             reads=[psB[bb]], writes=[BsB[bs_][il]])

                def y_chunk(j):
                    bs_ = j % 2
                    for il in range(ntl):
                        P.op("dve", lambda e, il=il: e.tensor_tensor(out=dense(mix_t[:, 4 + j, :], il), in0=ctok(Z_OFF, il, 0), in1=dense(af(BS_OFF[bs_], 1024), il), op=ALU.mult),
                             reads=[ZB, BsB[bs_][il]], writes=[mixB[4 + j][il]])

                def g_body(hooks=(), extra=False):
                    hooks = list(hooks)

                    def hk():
                        if hooks:
                            hooks.pop(0)()
                    S["u"] = ws_take()
                    S["c"] = ws_take()
                    S["v"] = ws_take()
                    u_chunk(0)
                    hk()
                    u_chunk(1)
                    hk()
                    pool_ew(0)
                    pool_state(0)
                    u_chunk(2)
                    hk()
                    pool_ew(1)
                    pool_state(1)
                    u_chunk(3)
                    hk()
                    ws_release(S["u"])
                    S["b"] = ws_take()
                    pool_mm(0)
                    pool_ew(2)
                    pool_state(2)
                    for j in range(4):
                        cv_chunk(j)
                        if j == 3:
                            ws_release(S["c"])
                            ws_release(S["v"])
                        hk()
                        conv_state(j)
                        if j == 0:
                            pool_mm(1)
                            pool_ew(3)
                            pool_state(3)
                        conv_z(j)
                        b_chunk(j)
                        if j == 3:
                            ws_release(S["b"])
                        if extra:
                            hk()
                        y_chunk(j)
                        if j >= 1 and j <= 2:
                            pool_mm(j + 1)
                    while hooks:
                        hk()

                def g_outproj():
                    slot_o = [ws_take(), ws_take()]
                    for c in range(KT):
                        for il in range(ntl):
                            o, tw = lo(il)
                            ti = tl[il]
                            toff = TILES[ti][0]
                            so_ = slot_o[c // 4]
                            lhs = [(ring[:, so_, k, (c % 4) * 128:(c % 4 + 1) * 128], ringB[so_]) for k in range(KT)]
                            rhs = [(mix_t[:, k, o:o + tw], mixB[k][il]) for k in range(KT)]
                            b = mm_group(lhs, rhs, tw)
                            P.op("dve", lambda e, b=b, c=c, toff=toff, tw=tw: e.tensor_tensor(out=h[:, c, toff:toff + tw], in0=h[:, c, toff:toff + tw], in1=psum[b][:, 0:tw], op=ALU.add),
                                 reads=[psB[b]] + hb(c, toff, tw), writes=hb(c, toff, tw))
                        if c == 3:
                            ws_release(slot_o[0])
                    ws_release(slot_o[1])

                return g_steps, g_apply, g_body, g_outproj

            psubs = {}

            def p_dma(ti):
                toff, tw = TILES[ti]
                nsub = tw // 128
                subs = []
                for s_ in range(nsub):
                    pi = st["pst"] % 4
                    st["pst"] += 1
                    src = pp[l, toff + s_ * 128: toff + (s_ + 1) * 128, :] if toff < SEQ else psm[l]
                    P.op("sp", lambda e, pi=pi, src=src: e.dma_start(out=pstg[:, pi, :], in_=src), writes=[pstgB[pi]], chan=pl_ch[pi])
                    subs.append(pi)
                psubs[ti] = subs

            def p_tile(ti):
                toff, tw = TILES[ti]
                nsub = tw // 128
                subs = psubs[ti]
                for kk in range(2):
                    b = next_bank()
                    for s_, pi in enumerate(subs):
                        P.op("pe", lambda e, b=b, s_=s_, pi=pi, kk=kk: e.transpose(out=psum[b][:, s_ * 128:(s_ + 1) * 128], in_=pstg[:, pi, kk * 128:(kk + 1) * 128], identity=ident[:]),
                             reads=[pstgB[pi], identB], writes=[psB[b]], signal=(s_ == nsub - 1))
                    P.op("act", lambda e, b=b, kk=kk, toff=toff, tw=tw: e.activation(out=pT_t[:, kk, toff:toff + tw], in_=psum[b][:, 0:tw], func=AF.Copy),
                         reads=[psB[b]], writes=ptb(kk, toff, tw))
                if ti + 1 < len(TILES):
                    p_dma(ti + 1)

            p_dma(0)

            mctx = [dict(), dict()]
            ms0 = norm_steps(0, mctx[0])
            ms1 = norm_steps(1, mctx[1])
            c_hooks = [ms0[0], (lambda: p_tile(0)), ms0[1], (lambda: p_tile(1)), ms0[2], (lambda: p_tile(2)),
                       ms1[0], (lambda: p_tile(3)), ms1[1], (lambda: p_tile(4)), ms1[2]]

            gs = [do_group(l, grp) for grp in GROUPS]
            if l == 0:
                for f_ in gs[0][0]():
                    f_()
                a_apply = gs[0][1]
            else:
                a_apply = pre_apply
            a_apply()
            if l == 0:
                gs[0][2](late_hooks + gs[1][0](), extra=True)
            else:
                gs[0][2](gs[1][0]())
            gs[1][1]()
            gs[0][3]()
            gs[1][2](gs[2][0]())
            gs[2][1]()
            gs[1][3]()
            gs[2][2](c_hooks)
            gs[2][3]()

            P.stage_barrier()
            NTL = len(MT)
            if l + 1 < DEPTH:
                load_sample_state(l + 1)

            def mlp_norm(ti):
                toff, tw = MT[ti]
                norm_tile(ti, lambda k, toff=toff, tw=tw: m_t[:, k, toff:toff + tw], [mB[k][ti] for k in range(KT)], V_NMLP + l * 8, T=MT)

            def up_unit(su, c, ti):
                toff, tw = MT[ti]
                s_ = su[c // 4]
                lhs = [(ring[:, s_, k, (c % 4) * 128:(c % 4 + 1) * 128], ringB[s_]) for k in range(KT)]
                rhs = [(m_t[:, k, toff:toff + tw], mB[k][ti]) for k in range(KT)]
                b = mm_group(lhs, rhs, tw)
                ri = st["r"] % 3
                st["r"] += 1
                P.op("act", lambda e, b=b, ri=ri, tw=tw: e.activation(out=af(R_OFF[ri], tw), in_=psum[b][:, 0:tw], func=AF.Relu),
                     reads=[psB[b]], writes=[rB[ri]])
                P.op("dve", lambda e, ri=ri, c=c, toff=toff, tw=tw: e.tensor_tensor(out=f_t[:, c, toff:toff + tw], in0=af(R_OFF[ri], tw), in1=af(R_OFF[ri], tw), op=ALU.mult),
                     reads=[rB[ri]], writes=[fB[c][ti]])

            def dn_unit(sd, c, ti):
                toff, tw = MT[ti]
                s_ = sd[c // 4]
                lhs = [(ring[:, s_, k, (c % 4) * 128:(c % 4 + 1) * 128], ringB[s_]) for k in range(KT)]
                rhs = [(f_t[:, k, toff:toff + tw], fB[k][ti]) for k in range(KT)]
                b = mm_group(lhs, rhs, tw)
                P.op("dve", lambda e, b=b, c=c, toff=toff, tw=tw: e.tensor_tensor(out=h[:, c, toff:toff + tw], in0=h[:, c, toff:toff + tw], in1=psum[b][:, 0:tw], op=ALU.add),
                     reads=[psB[b]] + hb(c, toff, tw), writes=hb(c, toff, tw))

            def ple_norm(ti):
                toff, tw = MT[ti]
                norm_tile(ti, lambda k, toff=toff, tw=tw: n_t[:, k, toff:toff + tw], [nB[k][ti] for k in range(KT)], V_NPLE + l * 8, T=MT)

            for q in range(4):
                su = [ws_take(), ws_take()]
                if q == 0:
                    for t01 in range(2):
                        toff_, tw_ = MT[t01]
                        norm_apply(t01, mctx[t01]["n"], lambda k, toff_=toff_, tw_=tw_: m_t[:, k, toff_:toff_ + tw_], [mB[k][t01] for k in range(KT)], V_NMLP + l * 8, T=MT)
                    for ti in range(NTL):
                        nx = ti + 2
                        cx = {}
                        stp = norm_steps(nx, cx, T=MT) if nx < NTL else []
                        for c in range(KT):
                            up_unit(su, c, ti)
                            if stp and c in (0, 2, 4):
                                stp.pop(0)()
                                if c == 4:
                                    toff_, tw_ = MT[nx]
                                    norm_apply(nx, cx["n"], lambda k, toff_=toff_, tw_=tw_: m_t[:, k, toff_:toff_ + tw_], [mB[k][nx] for k in range(KT)], V_NMLP + l * 8, T=MT)
                    ws_release(su[0])
                    ws_release(su[1])
                else:
                    for c in range(KT):
                        for ti in range(NTL):
                            up_unit(su, c, ti)
                        if c == 3:
                            ws_release(su[0])
                    ws_release(su[1])
                sd = [ws_take(), ws_take()]
                if q < 3:
                    for c in range(KT):
                        for ti in range(NTL):
                            dn_unit(sd, c, ti)
                        if c == 3:
                            ws_release(sd[0])
                    ws_release(sd[1])
                else:
                    for ti in range(NTL):
                        for c in range(KT):
                            dn_unit(sd, c, ti)
                        if ti >= 1:
                            ple_norm(ti - 1)
                    ws_release(sd[0])
                    ws_release(sd[1])

            P.stage_barrier()
            sg = [ws_take(), ws_take()]

            def ple_unit(c, ti):
                toff, tw = MT[ti]
                s_ = sg[c // 4]
                lhs = [(ring[:, s_, k, (c % 4) * 128:(c % 4 + 1) * 128], ringB[s_]) for k in range(KT)]
                rhs = [(n_t[:, k, toff:toff + tw], nB[k][ti]) for k in range(KT)]
                b1 = mm_group(lhs, rhs, tw)
                lhs2 = [(wple[:, kk, c * 128:(c + 1) * 128], wpleB) for kk in range(2)]
                rhs2 = [(pT_t[:, kk, toff:toff + tw], ptb(kk, toff, tw)) for kk in range(2)]
                b2 = mm_group(lhs2, rhs2, tw)
                gi_ = st["g"] % 2
                st["g"] += 1
                P.op("act", lambda e, b1=b1, gi_=gi_, tw=tw: e.activation(out=af(G_OFF[gi_], tw), in_=psum[b1][:, 0:tw], func=AF.Sigmoid),
                     reads=[psB[b1]], writes=[gB[gi_]])
                P.op("dve", lambda e, b2=b2, gi_=gi_, tw=tw: e.tensor_tensor(out=af(TMP_OFF[gi_], tw), in0=af(G_OFF[gi_], tw), in1=psum[b2][:, 0:tw], op=ALU.mult),
                     reads=[gB[gi_], psB[b2]], writes=[tmpB[gi_]])
                P.op("dve", lambda e, gi_=gi_, c=c, toff=toff, tw=tw: e.tensor_tensor(out=h[:, c, toff:toff + tw], in0=h[:, c, toff:toff + tw], in1=af(TMP_OFF[gi_], tw), op=ALU.add),
                     reads=[tmpB[gi_]] + hb(c, toff, tw), writes=hb(c, toff, tw))

            def final_norm(ti):
                toff, tw = MT[ti]
                norm_tile(ti, lambda k, tw=tw: af(YFM_OFF + k * 512, tw), yfmB, V_NF, T=MT)

            def final_tile(ti):
                toff, tw = MT[ti]
                for s_ in range(tw // 128):
                    ti_ = st["y"] % 2
                    st["y"] += 1
                    for hf in range(2):
                        b = next_bank()
                        for kk in range(4):
                            k = hf * 4 + kk
                            P.op("pe", lambda e, b=b, kk=kk, k=k, s_=s_: e.transpose(out=psum[b][:, kk * 128:(kk + 1) * 128],
                                                                                    in_=af(YFM_OFF + k * 512 + s_ * 128, 128), identity=ident[:]),
                                 reads=[yfmB[k], identB], writes=[psB[b]], signal=(kk == 3))
                        if hf == 0:
                            P.op("act", lambda e, b=b, ti_=ti_: e.activation(out=af(YTM_OFF[ti_], 512), in_=psum[b][:], func=AF.Copy),
                                 reads=[psB[b]], writes=[ytmB[ti_]])
                        else:
                            P.op("dve", lambda e, b=b, ti_=ti_: e.tensor_copy(out=af(YTM_OFF[ti_] + 512, 512), in_=psum[b][:]),
                                 reads=[psB[b], ytmB[ti_]], writes=[ytmB[ti_]])
                    col_ = toff + s_ * 128
                    dst = y_p[col_:col_ + 128, :] if col_ < SEQ else y_s
                    P.op("sp", lambda e, ti_=ti_, dst=dst: e.dma_start(out=dst, in_=af(YTM_OFF[ti_], 1024)), reads=[ytmB[ti_]], chan=st_ch[ti_])

            last = (l == DEPTH - 1)
            nhooks = []
            if not last:
                nA = do_group(l + 1, GROUPS[0])
                nhooks = nA[0]()
                pre_apply = nA[1]
            lctx = {}
            lsteps = norm_steps(NTL - 1, lctx, T=MT)
            for ti in range(NTL):
                fsteps, fctx = [], {}
                if last and ti >= 1:
                    fsteps = norm_steps(ti - 1, fctx, T=MT)
                for c in range(KT):
                    ple_unit(c, ti)
                    if ti == 0 and c in (0, 2, 4):
                        lsteps.pop(0)()
                        if c == 4:
                            toff_, tw_ = MT[NTL - 1]
                            norm_apply(NTL - 1, lctx["n"], lambda k, toff_=toff_, tw_=tw_: n_t[:, k, toff_:toff_ + tw_], [nB[k][NTL - 1] for k in range(KT)], V_NPLE + l * 8, T=MT)
                    if nhooks and ti >= 2 and c in (1, 3, 5):
                        nhooks.pop(0)()
                    if fsteps and c in (0, 2, 4):
                        fsteps.pop(0)()
                        if c == 4:
                            tw_ = MT[ti - 1][1]
                            norm_apply(ti - 1, fctx["n"], lambda k, tw_=tw_: af(YFM_OFF + k * 512, tw_), yfmB, V_NF, T=MT)
                if last and ti >= 1:
                    final_tile(ti - 1)
            ws_release(sg[0])
            ws_release(sg[1])
            while nhooks:
                nhooks.pop(0)()
            if last:
                final_norm(NTL - 1)
                final_tile(NTL - 1)

        assert ws["taken"] == len(wq) and ws["issued"] == len(wq), (ws["taken"], ws["issued"], len(wq))

        final_waits = [(c, c.count) for c in st_ch + so_ch + [d2d_ch] if c.count > 0]

        all_ch = list(P.engs.values()) + P.chans
        for c in all_ch:
            c.sem = es.enter_context(nc.semaphore(c.name))
        block = es.enter_context(nc.Block())

        def replay(name, e, tail=()):
            for waits, fn, sig in P.progs[name]:
                for c, v in waits:
                    e.wait_ge(c.sem, v)
                ins = fn(e)
                if sig is not None:
                    ins.then_inc(sig[0].sem, sig[1])
            for c, v in tail:
                e.wait_ge(c.sem, v)

        @block.tensor
        def _(e):
            replay("pe", e)

        @block.scalar
        def _(e):
            replay("act", e)

        @block.vector
        def _(e):
            replay("dve", e)

        @block.gpsimd
        def _(e):
            replay("pool", e)

        @block.sync
        def _(e):
            replay("sp", e, tail=final_waits)

    return nc


_NC = None


def kernel(x_prompt, x_sample, state_pool, state_conv, p_prompt, p_sample,
           norm_mix, w_in, pool_w, pool_scale, conv_w, conv_b, w_out,
           norm_mlp, w_up, w_down, norm_ple, w_ple_gate, w_ple_proj, norm_f):
    global _NC
    f = lambda a: np.ascontiguousarray(np.asarray(a, dtype=np.float32))
    if _NC is None:
        _NC = build()
    nc = _NC
    vecs = np.concatenate([
        f(norm_mix).reshape(16, 128), f(norm_mlp).reshape(16, 128), f(norm_ple).reshape(16, 128),
        f(norm_f).reshape(8, 128), f(pool_scale).reshape(8, 128), f(conv_w).reshape(24, 128),
        f(conv_b).reshape(8, 128)], axis=0)
    shared = dict(vecs=f(vecs), w_in=f(w_in), pool_w=f(pool_w), w_out=f(w_out), w_up=f(w_up),
                  w_down=f(w_down), w_gate=f(w_ple_gate), w_ple=f(w_ple_proj))
    x_prompt, x_sample = f(x_prompt), f(x_sample)
    state_pool, state_conv = f(state_pool), f(state_conv)
    p_prompt, p_sample = f(p_prompt), f(p_sample)
    in_maps = []
    for c in range(NCORES):
        ss = slice(c * NSEQ, (c + 1) * NSEQ)
        m = dict(shared)
        m["x_p"] = x_prompt[c]
        m["x_s"] = f(x_sample[ss].reshape(NS, D))
        m["sp_in"] = f(state_pool[:, ss].reshape(DEPTH, NSEQ * POOL_BUF, 512))
        m["sc_in"] = f(state_conv[:, ss].reshape(DEPTH, NSEQ * CONV_BUF, 512))
        m["pp"] = f(p_prompt[:, c])
        m["psm"] = f(p_sample[:, ss].reshape(DEPTH, NS, PLE))
        in_maps.append(m)
    res = run_bass_kernel_spmd(nc, in_maps, core_ids=list(range(NCORES)))
    R = res.results
    y_prompt = np.stack([R[c]["y_p"] for c in range(NCORES)], axis=0)
    y_sample = np.concatenate([R[c]["y_s"].reshape(NSEQ, DSEQ, D) for c in range(NCORES)], axis=0)
    new_pool_prompt = np.stack([R[c]["npp"] for c in range(NCORES)], axis=1)
    new_conv_prompt = np.stack([R[c]["ncp"] for c in range(NCORES)], axis=1)
    new_pool_sample = np.concatenate([R[c]["nps"] for c in range(NCORES)], axis=1)
    new_conv_sample = np.concatenate([R[c]["ncs"].reshape(DEPTH, NSEQ, CONV_BUF, 512) for c in range(NCORES)], axis=1)
    return (y_prompt.astype(np.float32), y_sample.astype(np.float32), new_pool_prompt.astype(np.float32),
            new_conv_prompt.astype(np.float32), new_pool_sample.astype(np.float32), new_conv_sample.astype(np.float32))
```

```python
import numpy as np
from contextlib import ExitStack
import concourse.bass as bass
import concourse.mybir as mybir
from concourse.bass_utils import run_bass_kernel_spmd

F32 = mybir.dt.float32
BF16 = mybir.dt.bfloat16
AF = mybir.ActivationFunctionType
ALU = mybir.AluOpType

NCORES = 8
D = 1024
KT = 8
SEQ = 2048
NS = 128
NSEQ = 16
DSEQ = 8
NTOK = SEQ + NS
DEPTH = 2
DFF = 4096
PLE = 256
EPS = 1e-6
POOL_BUF = 15
CONV_BUF = 2
ER = POOL_BUF + DSEQ
CR = CONV_BUF + DSEQ

TILES = [(0, 512), (512, 512), (1024, 512), (1536, 512), (2048, 128)]
MT = [(0, 512), (512, 512), (1024, 512), (1536, 384), (1920, 256)]
NSUB = NTOK // 128
GROUPS = [
    dict(tiles=[0, 1], kind="p", first=True, last=False),
    dict(tiles=[2, 3], kind="p", first=False, last=True),
    dict(tiles=[4], kind="s", first=False, last=False),
]

V_NMIX, V_NMLP, V_NPLE, V_NF, V_PSC, V_CW, V_CB = 0, 16, 32, 48, 56, 64, 88
NVEC = 96


class Ch:
    def __init__(self, name):
        self.name = name
        self.count = 0
        self.sem = None


class Buf:
    __slots__ = ("w", "r", "arena")

    def __init__(self, arena=False):
        self.w = {}
        self.r = {}
        self.arena = arena


def _merge(dst, src):
    for c, v in src.items():
        if dst.get(c, 0) < v:
            dst[c] = v


class Prog:
    ENGS = ["pe", "act", "dve", "pool", "sp"]

    def __init__(self):
        self.engs = {n: Ch(n) for n in self.ENGS}
        self.progs = {n: [] for n in self.ENGS}
        self.waited = {n: {} for n in self.ENGS}
        self.chans = []
        self.barrier = {}
        self.arena_bufs = []

    def chan(self, name):
        c = Ch(name)
        self.chans.append(c)
        return c

    def abuf(self):
        b = Buf(arena=True)
        self.arena_bufs.append(b)
        return b

    def stage_barrier(self):
        for b in self.arena_bufs:
            _merge(self.barrier, b.w)
            _merge(self.barrier, b.r)

    def op(self, eng, fn, reads=(), writes=(), signal=True, chan=None):
        need = {}
        for b in reads:
            _merge(need, b.w)
        for b in writes:
            _merge(need, b.w)
            _merge(need, b.r)
            if b.arena:
                _merge(need, self.barrier)
        waits = []
        wd = self.waited[eng]
        pe_ch = self.engs["pe"]
        for c, v in need.items():
            if eng == "pe" and c is pe_ch:
                continue
            if wd.get(c, 0) < v:
                waits.append((c, v))
                wd[c] = v
        if chan is not None:
            chan.count += 16
            tok = {chan: chan.count}
            sig = (chan, 16)
        else:
            ch = self.engs[eng]
            tok = {ch: ch.count + 1}
            if signal:
                ch.count += 1
                sig = (ch, 1)
            else:
                sig = None
        self.progs[eng].append((waits, fn, sig))
        for b in reads:
            _merge(b.r, tok)
        for b in writes:
            b.w = dict(tok)
            b.r = {}
        return tok


def build():
    nc = bass.Bass("TRN2", target_bir_lowering=False)
    dt_in = lambda name, shape: nc.dram_tensor(name, shape, F32, kind="ExternalInput").ap()
    dt_out = lambda name, shape: nc.dram_tensor(name, shape, F32, kind="ExternalOutput").ap()
    x_p = dt_in("x_p", [SEQ, D])
    x_s = dt_in("x_s", [NS, D])
    sp_in = dt_in("sp_in", [DEPTH, NSEQ * POOL_BUF, 512])
    sc_in = dt_in("sc_in", [DEPTH, NSEQ * CONV_BUF, 512])
    pp = dt_in("pp", [DEPTH, SEQ, PLE])
    psm = dt_in("psm", [DEPTH, NS, PLE])
    vecs = dt_in("vecs", [NVEC, 128])
    w_in = dt_in("w_in", [DEPTH, D, 2048])
    pool_w = dt_in("pool_w", [DEPTH, 4, 128, 128])
    w_out = dt_in("w_out", [DEPTH, D, D])
    w_up = dt_in("w_up", [DEPTH, D, DFF])
    w_down = dt_in("w_down", [DEPTH, DFF, D])
    w_gate = dt_in("w_gate", [DEPTH, D, D])
    w_ple = dt_in("w_ple", [DEPTH, PLE, D])
    y_p = dt_out("y_p", [SEQ, D])
    y_s = dt_out("y_s", [NS, D])
    npp = dt_out("npp", [DEPTH, POOL_BUF, 512])
    ncp = dt_out("ncp", [DEPTH, CONV_BUF, 512])
    nps = dt_out("nps", [DEPTH, NSEQ, POOL_BUF, 512])
    ncs = dt_out("ncs", [DEPTH, NSEQ * CONV_BUF, 512])

    P = Prog()
    AW = 19904
    es = ExitStack()
    with es:
        sb = lambda name, shape, dt: es.enter_context(nc.sbuf_tensor(name, shape, dt))
        h = sb("h", [128, KT, NTOK], F32)
        ring = sb("ring", [128, 6, 8, 256], BF16)
        pT_t = sb("pT", [128, 2, NTOK], BF16)
        wple = sb("wple", [128, 2, D], BF16)
        poolw = sb("poolw", [128, 4, 128], BF16)
        sq = sb("sq", [128, 4, 512], BF16)
        rstd = sb("rstd", [128, 2, 512], F32)
        EH = sb("EH", [128, 4, NSEQ * POOL_BUF], F32)
        CVH = sb("CVH", [128, DEPTH, 4, NSEQ * CONV_BUF], F32)
        carryE = sb("carryE", [128, 4, 16], F32)
        carryCV = sb("carryCV", [128, 4, 2], F32)
        vec = sb("vec", [128, NVEC], F32)
        ident = sb("ident", [128, 128], F32)
        ones = sb("ones", [128, 128], BF16)
        invcnt = sb("invcnt", [128, 16], F32)
        so_stg = sb("so_stg", [128, 2, 512], F32)
        pstg = sb("pstg", [128, 4, PLE], F32)
        arena = sb("arena", [128, AW], F32)
        psum = [es.enter_context(nc.psum_tensor(f"ps{i}", [128, 512], F32)) for i in range(8)]

        def af(off, n):
            return arena[:, off:off + n]

        def ab(off, nwords):
            return arena[:, off:off + nwords].bitcast(BF16)

        hB = [[Buf() for _ in range(NSUB)] for _ in range(KT)]

        def hb(k, toff, tw):
            return hB[k][toff // 128:(toff + tw) // 128]

        psB = [Buf() for _ in range(8)]
        ringB = [Buf() for _ in range(6)]
        wpleB, poolwB = Buf(), Buf()
        sqB = [Buf() for _ in range(4)]
        rstdB = [Buf(), Buf()]
        EHB, CVHB, vecB, identB, onesB, invB = Buf(), Buf(), Buf(), Buf(), Buf(), Buf()
        carryEB = [Buf() for _ in range(4)]
        carryCVB = [Buf() for _ in range(4)]
        soB = [Buf(), Buf()]
        pstgB = [Buf() for _ in range(4)]
        pTB = [[Buf() for _ in range(NSUB)] for _ in range(2)]

        def ptb(kk, toff, tw):
            return pTB[kk][toff // 128:(toff + tw) // 128]


        st = dict(bank=0, sq=0, norm=0, r=0, g=0, pst=0, y=0)

        reserved = set()

        def next_bank():
            while True:
                b = st["bank"]
                st["bank"] = (b + 1) % 8
                if b not in reserved:
                    return b

        ring_ch = [P.chan(f"ring{i}") for i in range(6)]
        poolw_ch = P.chan("poolw")
        wple_ch = P.chan("wple")
        setup_ch = [P.chan(f"su{i}") for i in range(3)]
        sui = iter(setup_ch)
        ld_ch = [P.chan(f"ld{i}") for i in range(8)]
        pl_ch = [P.chan(f"pl{i}") for i in range(4)]
        st_ch = [P.chan(f"st{i}") for i in range(2)]
        so_ch = [P.chan(f"so{i}") for i in range(2)]
        d2d_ch = P.chan("d2d")

        def vcol(c):
            return vec[:, c:c + 1]

        def wview(w_l, c0):
            v = w_l.rearrange("(k p) n -> p k n", p=128)
            return [v[:, :, c0:c0 + 256], v[:, :, c0 + 256:c0 + 512]]

        wq = []
        for l in range(DEPTH):
            for _g in GROUPS:
                wq += wview(w_in[l], 0) + wview(w_in[l], 1024) + wview(w_in[l], 1536) + wview(w_in[l], 512) \
                    + wview(w_out[l], 0) + wview(w_out[l], 512)
            for q in range(4):
                wq += wview(w_up[l], q * 1024) + wview(w_up[l], q * 1024 + 512) \
                    + wview(w_down[l][q * 1024:(q + 1) * 1024, :], 0) + wview(w_down[l][q * 1024:(q + 1) * 1024, :], 512)
            wq += wview(w_gate[l], 0) + wview(w_gate[l], 512)
        ws = dict(issued=0, taken=0, free=[True] * 6, gate=())

        def ws_pump():
            while ws["issued"] < len(wq) and ws["free"][ws["issued"] % 6]:
                i = ws["issued"]
                s_ = i % 6
                P.op("pool", lambda e, s_=s_, src=wq[i]: e.dma_start(out=ring[:, s_], in_=src), reads=(ws["gate"] if i in (2, 3, 4, 5) else ()),
                     writes=[ringB[s_]], chan=ring_ch[s_])
                ws["free"][s_] = False
                ws["issued"] += 1

        def ws_take():
            i = ws["taken"]
            ws["taken"] += 2
            assert i + 1 < ws["issued"], "weight block not yet issued"
            return dict(s=[i % 6, (i + 1) % 6], rel=[False, False])

        def ws_release_half(blk, hh):
            if not blk["rel"][hh]:
                blk["rel"][hh] = True
                ws["free"][blk["s"][hh]] = True
                ws_pump()

        def ws_release(blk):
            ws_release_half(blk, 0)
            ws_release_half(blk, 1)

        def wap(blk, c4, k):
            s_ = blk["s"][c4 // 2]
            return (ring[:, s_, k, (c4 % 2) * 128:(c4 % 2 + 1) * 128], ringB[s_])

        P.op("pool", lambda e: e.memset(ident[:], 0.0), writes=[identB])
        P.op("pool", lambda e: e.affine_select(out=ident[:], in_=ident[:], pattern=[[-1, 128]],
                                               compare_op=ALU.not_equal, fill=1.0, base=0,
                                               channel_multiplier=1), writes=[identB])
        P.op("pool", lambda e: e.memset(ones[:], 1.0), writes=[onesB])
        for t in range(16):
            P.op("pool", lambda e, t=t: e.memset(invcnt[:, t:t + 1], 1.0 / (t + 1)), writes=[invB])

        XSTG = 8192
        SPSTG = 16384
        vstg = pstg[:, 0, 0:128]
        vstgB = pstgB[0]
        P.op("sp", lambda e: e.dma_start(out=vstg[0:NVEC, :], in_=vecs), writes=[vstgB], chan=next(sui))
        b = next_bank()
        P.op("pe", lambda e, b=b: e.transpose(out=psum[b][:, 0:NVEC], in_=vstg[0:NVEC, :], identity=ident[0:NVEC, 0:NVEC]),
             reads=[vstgB, identB], writes=[psB[b]])
        P.op("act", lambda e, b=b: e.activation(out=vec[:], in_=psum[b][:, 0:NVEC], func=AF.Copy),
             reads=[psB[b]], writes=[vecB])

        pstg2 = pstg[:].rearrange("p a c -> p (a c)").rearrange("p (h c) -> p h c", h=2)

        def load_sample_state(l):
            for hf in range(2):
                stg = pstg2[:, hf, :]
                sBs = [pstgB[2 * hf], pstgB[2 * hf + 1]]
                P.op("sp", lambda e, stg=stg, hf=hf: e.dma_start(out=stg[0:120, :], in_=sp_in[l, hf * 120:(hf + 1) * 120, :]),
                     writes=sBs, chan=pl_ch[2 * hf])
                b = next_bank()
                for g in range(4):
                    P.op("pe", lambda e, b=b, g=g, stg=stg: e.transpose(out=psum[b][:, g * 120:(g + 1) * 120], in_=stg[0:120, g * 128:(g + 1) * 128],
                                                                        identity=ident[0:120, 0:120]),
                         reads=sBs + [identB], writes=[psB[b]], signal=(g == 3))
                P.op("act", lambda e, b=b, hf=hf: e.activation(out=EH[:, :, hf * 120:(hf + 1) * 120],
                                                               in_=psum[b][:, 0:480].rearrange("p (g c) -> p g c", g=4), func=AF.Copy),
                     reads=[psB[b]], writes=[EHB])

        load_sample_state(0)
        for l in range(DEPTH):
            stg = so_stg[:, l, :]
            sB_ = soB[l]
            P.op("sp", lambda e, stg=stg, l=l: e.dma_start(out=stg[0:32, :], in_=sc_in[l]), writes=[sB_], chan=next(sui))
            b = next_bank()
            for j in range(4):
                P.op("pe", lambda e, b=b, j=j, stg=stg: e.transpose(out=psum[b][:, j * 32:(j + 1) * 32], in_=stg[0:32, j * 128:(j + 1) * 128],
                                                                    identity=ident[0:32, 0:32]),
                     reads=[sB_, identB], writes=[psB[b]], signal=(j == 3))
            P.op("act", lambda e, b=b, l=l: e.activation(out=CVH[:, l], in_=psum[b][:, 0:128].rearrange("p (g c) -> p g c", g=4), func=AF.Copy),
                 reads=[psB[b]], writes=[CVHB])
            P.op("sp", lambda e, l=l: e.dma_start(out=nps[l, :, 0:POOL_BUF - DSEQ, :],
                                                  in_=sp_in[l].rearrange("(s r) c -> s r c", r=POOL_BUF)[:, DSEQ:POOL_BUF, :]),
                 chan=d2d_ch)

        xstgB = [P.abuf() for _ in range(8)]
        xstg_all = [af(XSTG + i * 1024, 1024) for i in range(8)]
        for ti, (toff, tw) in enumerate(TILES[0:2]):
            nsub = tw // 128
            xo = (ti % 2) * 4
            xstg = xstg_all[xo:xo + 4]
            xsB = xstgB[xo:xo + 4]
            for s_ in range(nsub):
                src = x_p[toff + s_ * 128: toff + (s_ + 1) * 128, :] if toff < SEQ else x_s
                P.op("sp", lambda e, s_=s_, src=src, xstg=xstg: e.dma_start(out=xstg[s_], in_=src), writes=[xsB[s_]], chan=ld_ch[xo + s_])
            if nsub == 4:
                for k in range(KT):
                    b = next_bank()
                    for s_ in range(4):
                        P.op("pe", lambda e, b=b, s_=s_, k=k, xstg=xstg: e.transpose(out=psum[b][:, s_ * 128:(s_ + 1) * 128], in_=xstg[s_][:, k * 128:(k + 1) * 128], identity=ident[:]),
                             reads=[xsB[s_], identB], writes=[psB[b]], signal=(s_ == 3))
                    if ti == 1 and k == 0:
                        gateB = Buf()
                        gateB.w = dict(psB[b].w)
                        ws["gate"] = (gateB,)
                    if k % 2 == 0:
                        P.op("act", lambda e, b=b, k=k, toff=toff: e.activation(out=h[:, k, toff:toff + 512], in_=psum[b][:], func=AF.Copy),
                             reads=[psB[b]], writes=hb(k, toff, 512))
                    else:
                        P.op("dve", lambda e, b=b, k=k, toff=toff: e.tensor_copy(out=h[:, k, toff:toff + 512], in_=psum[b][:]),
                             reads=[psB[b]], writes=hb(k, toff, 512))
            else:
                for kh in range(2):
                    b = next_bank()
                    for kk in range(4):
                        k = kh * 4 + kk
                        P.op("pe", lambda e, b=b, kk=kk, k=k, xstg=xstg: e.transpose(out=psum[b][:, kk * 128:(kk + 1) * 128], in_=xstg[0][:, k * 128:(k + 1) * 128], identity=ident[:]),
                             reads=[xsB[0], identB], writes=[psB[b]], signal=(kk == 3))
                    P.op("act", lambda e, b=b, kh=kh, toff=toff: e.activation(out=h[:, kh * 4:(kh + 1) * 4, toff:toff + 128],
                                                                              in_=psum[b][:].rearrange("p (k c) -> p k c", k=4), func=AF.Copy),
                         reads=[psB[b]], writes=[hb(kh * 4 + kk, toff, 128)[0] for kk in range(4)])

        ws_pump()

        xl = [pT_t[:, j, :].bitcast(F32)[:, 0:1024] for j in range(2)]
        xl_ch = [P.chan(f"xl{j}") for j in range(2)]
        late_list = []
        for ti in range(2, len(TILES)):
            toff, tw = TILES[ti]
            for s_ in range(tw // 128):
                src = x_p[toff + s_ * 128: toff + (s_ + 1) * 128, :] if toff < SEQ else x_s
                late_list.append((ti, toff + s_ * 128, src))

        def late_dma(n):
            ti, col0, src = late_list[n]
            j = n % 2
            P.op("sp", lambda e, j=j, src=src: e.dma_start(out=xl[j], in_=src), writes=list(pTB[j]), chan=xl_ch[j])

        def late_step(n):
            def f_():
                ti, col0, src = late_list[n]
                j = n % 2
                if n + 1 < len(late_list):
                    late_dma(n + 1)
                for hf in range(2):
                    b = next_bank()
                    for kk in range(4):
                        k = hf * 4 + kk
                        P.op("pe", lambda e, b=b, kk=kk, k=k: e.transpose(out=psum[b][:, kk * 128:(kk + 1) * 128], in_=xl[j][:, k * 128:(k + 1) * 128], identity=ident[:]),
                             reads=list(pTB[j]) + [identB], writes=[psB[b]], signal=(kk == 3))
                    P.op("act", lambda e, b=b, hf=hf: e.activation(out=h[:, hf * 4:(hf + 1) * 4, col0:col0 + 128],
                                                                   in_=psum[b][:].rearrange("p (k c) -> p k c", k=4), func=AF.Copy),
                         reads=[psB[b]], writes=[hb(hf * 4 + kk, col0, 128)[0] for kk in range(4)])
            return f_

        late_dma(0)
        late_hooks = [late_step(n) for n in range(len(late_list))]

        def norm_tile(ti, dst_ap_fn, dstB, gbase, T=TILES):
            toff, tw = T[ti]
            b = next_bank()
            for k in range(KT):
                q = st["sq"]
                st["sq"] = (q + 1) % 4
                P.op("act", lambda e, q=q, k=k: e.activation(out=sq[:, q, 0:tw], in_=h[:, k, toff:toff + tw], func=AF.Square),
                     reads=hb(k, toff, tw), writes=[sqB[q]])
                P.op("pe", lambda e, q=q, k=k, b=b: e.matmul(psum[b][:, 0:tw], lhsT=ones[:], rhs=sq[:, q, 0:tw], start=(k == 0), stop=(k == KT - 1)),
                     reads=[sqB[q], onesB], writes=[psB[b]], signal=True)
            n_ = st["norm"]
            st["norm"] = 1 - n_
            P.op("act", lambda e, b=b, n_=n_: e.activation(out=rstd[:, n_, 0:tw], in_=psum[b][:, 0:tw], func=AF.Ln, scale=1.0 / D, bias=EPS),
                 reads=[psB[b]], writes=[rstdB[n_]])
            P.op("act", lambda e, n_=n_: e.activation(out=rstd[:, n_, 0:tw], in_=rstd[:, n_, 0:tw], func=AF.Exp, scale=-0.5),
                 reads=[rstdB[n_]], writes=[rstdB[n_]])
            for k in range(KT):
                P.op("dve", lambda e, k=k, n_=n_: e.scalar_tensor_tensor(out=dst_ap_fn(k), in0=h[:, k, toff:toff + tw], scalar=vcol(gbase + k),
                                                                         in1=rstd[:, n_, 0:tw], op0=ALU.mult, op1=ALU.mult),
                     reads=hb(k, toff, tw) + [rstdB[n_], vecB], writes=[dstB[k]])

        def norm_steps(ti, ctx, T=TILES):
            toff, tw = T[ti]

            def squares(hf):
                for j in range(4):
                    k = hf * 4 + j
                    P.op("act", lambda e, j=j, k=k: e.activation(out=sq[:, j, 0:tw], in_=h[:, k, toff:toff + tw], func=AF.Square),
                         reads=hb(k, toff, tw), writes=[sqB[j]])

            def stats(hf):
                b = ctx["b"]
                for j in range(4):
                    k = hf * 4 + j
                    P.op("pe", lambda e, j=j, k=k, b=b: e.matmul(psum[b][:, 0:tw], lhsT=ones[:], rhs=sq[:, j, 0:tw], start=(k == 0), stop=(k == KT - 1)),
                         reads=[sqB[j], onesB], writes=[psB[b]], signal=True)

            def s1():
                ctx["b"] = next_bank()
                reserved.add(ctx["b"])
                squares(0)

            def s2():
                stats(0)
                squares(1)

            def s3():
                stats(1)
                b = ctx["b"]
                n_ = st["norm"]
                st["norm"] = 1 - n_
                ctx["n"] = n_
                P.op("act", lambda e: e.activation(out=rstd[:, n_, 0:tw], in_=psum[b][:, 0:tw], func=AF.Ln, scale=1.0 / D, bias=EPS),
                     reads=[psB[b]], writes=[rstdB[n_]])
                P.op("act", lambda e: e.activation(out=rstd[:, n_, 0:tw], in_=rstd[:, n_, 0:tw], func=AF.Exp, scale=-0.5),
                     reads=[rstdB[n_]], writes=[rstdB[n_]])
                reserved.discard(b)

            return [s1, s2, s3]

        def norm_apply(ti, n_, dst_ap_fn, dstB, gbase, T=TILES):
            toff, tw = T[ti]
            for k in range(KT):
                P.op("dve", lambda e, k=k: e.scalar_tensor_tensor(out=dst_ap_fn(k), in0=h[:, k, toff:toff + tw], scalar=vcol(gbase + k),
                                                                  in1=rstd[:, n_, 0:tw], op0=ALU.mult, op1=ALU.mult),
                     reads=hb(k, toff, tw) + [rstdB[n_], vecB], writes=[dstB[k]])

        def mm_group(lhs_list, rhs_list, width):
            b = next_bank()
            n = len(lhs_list)
            for i in range(n):
                (lap, lB), (rap, rB) = lhs_list[i], rhs_list[i]
                P.op("pe", lambda e, b=b, lap=lap, rap=rap, i=i: e.matmul(psum[b][:, 0:width], lhsT=lap, rhs=rap, start=(i == 0), stop=(i == n - 1)),
                     reads=(lB if isinstance(lB, list) else [lB]) + (rB if isinstance(rB, list) else [rB]), writes=[psB[b]], signal=(i == n - 1))
            return b

        A_OFF, MIX_OFF = 0, 4096
        E_OFF = [8192, 9232]
        T_OFF = [10272, 11312]
        D_OFF = [12352, 12864]
        CVE_OFF = [13376, 14404]
        Z_OFF = 15432
        C_OFF = [16460, 16972]
        FIX_OFF = 17484
        BS_OFF = [17800, 18824]
        UD_OFF = [17500, 17628]
        CD_OFF = 17756
        a_t = ab(A_OFF, 4096).rearrange("p (k n) -> p k n", k=KT)
        mix_t = ab(MIX_OFF, 4096).rearrange("p (k n) -> p k n", k=KT)
        aB = [[P.abuf() for _ in range(2)] for _ in range(KT)]
        mixB = [[P.abuf() for _ in range(2)] for _ in range(KT)]
        EtB = [[P.abuf() for _ in range(2)] for _ in range(2)]
        EhB = [P.abuf() for _ in range(2)]
        TB = [P.abuf() for _ in range(2)]
        DB = [[P.abuf() for _ in range(2)] for _ in range(2)]
        CtB = [[P.abuf() for _ in range(2)] for _ in range(2)]
        ChB = [P.abuf() for _ in range(2)]
        ZB = P.abuf()
        CsB = [P.abuf() for _ in range(2)]
        fixB = P.abuf()
        BsB = [[P.abuf() for _ in range(2)] for _ in range(2)]
        UdB = [P.abuf(), P.abuf()]
        CdB = P.abuf()
        M_OFF, F_OFF = 0, 8704
        R_OFF = [17408, 17920, 18432]
        m_t = ab(M_OFF, 8704).rearrange("p (k n) -> p k n", k=KT)
        f_t = ab(F_OFF, 8704).rearrange("p (k n) -> p k n", k=KT)
        mB = [[P.abuf() for _ in TILES] for _ in range(KT)]
        fB = [[P.abuf() for _ in TILES] for _ in range(KT)]
        rB = [P.abuf() for _ in range(3)]
        G_OFF = [8704, 9216]
        TMP_OFF = [9728, 10240]
        nB = mB
        n_t = m_t
        gB = [P.abuf() for _ in range(2)]
        tmpB = [P.abuf() for _ in range(2)]
        YFM_OFF = 10752
        YTM_OFF = [14848, 15872]
        yfmB = [P.abuf() for _ in range(KT)]
        ytmB = [P.abuf() for _ in range(2)]

        for l in range(DEPTH):
            P.stage_barrier()
            P.op("pool", lambda e, l=l: e.dma_start(out=poolw[:], in_=pool_w[l].rearrange("g c d -> c g d")), writes=[poolwB], chan=poolw_ch)
            P.op("pool", lambda e, l=l: e.dma_start(out=wple[:], in_=w_ple[l].rearrange("(k p) n -> p k n", p=128)), writes=[wpleB], chan=wple_ch)
            def do_group(l, grp):
                kind = grp["kind"]
                tl = grp["tiles"]
                ntl = len(tl)
                g0 = TILES[tl[0]][0]
                gw = sum(TILES[t][1] for t in tl)
                if kind == "p":
                    EW, CW = POOL_BUF + gw, CONV_BUF + gw
                else:
                    EW, CW = NSEQ * ER, NSEQ * CR

                def lo(il):
                    toff, tw = TILES[tl[il]]
                    return toff - g0, tw

                def etok(off, il):
                    o, tw = lo(il)
                    if kind == "p":
                        return af(off + POOL_BUF + o, tw)
                    return af(off, NSEQ * ER).rearrange("p (s r) -> p s r", r=ER)[:, :, POOL_BUF:ER]

                def ctok(off, il, hist):
                    o, tw = lo(il)
                    if kind == "p":
                        return af(off + hist + o, tw)
                    return af(off, NSEQ * CR).rearrange("p (s r) -> p s r", r=CR)[:, :, hist:hist + DSEQ]

                def pview(b, il):
                    o, tw = lo(il)
                    if kind == "p":
                        return psum[b][:, 0:tw]
                    return psum[b][:, 0:tw].rearrange("p (s t) -> p s t", t=DSEQ)

                def dense(ap2, il):
                    o, tw = lo(il)
                    v = ap2[:, o:o + tw]
                    if kind == "s":
                        return v.rearrange("p (s t) -> p s t", t=DSEQ)
                    return v

                S = {}

                nctx = [dict() for _ in tl]

                def g_steps():
                    out = []
                    for il, ti in enumerate(tl):
                        out += norm_steps(ti, nctx[il])
                    return out

                def g_apply():
                    for il, ti in enumerate(tl):
                        o, tw = lo(il)
                        norm_apply(ti, nctx[il]["n"], lambda k, o=o, tw=tw: a_t[:, k, o:o + tw], [aB[k][il] for k in range(KT)], V_NMIX + l * 8)

                def inproj(slot, c, il):
                    o, tw = lo(il)
                    lhs = [wap(slot, c, k) for k in range(KT)]
                    rhs = [(a_t[:, k, o:o + tw], aB[k][il]) for k in range(KT)]
                    return mm_group(lhs, rhs, tw)

                def u_chunk(g):
                    es_ = g % 2
                    eoff = E_OFF[es_]
                    if kind == "p":
                        if grp["first"]:
                            P.op("dve", lambda e: e.memset(af(eoff, POOL_BUF), 0.0), writes=[EhB[es_]])
                        else:
                            P.op("dve", lambda e: e.tensor_copy(out=af(eoff, POOL_BUF), in_=carryE[:, g, 0:POOL_BUF]),
                                 reads=[carryEB[g]], writes=[EhB[es_]])
                    else:
                        P.op("dve", lambda e: e.tensor_copy(
                            out=af(eoff, NSEQ * ER).rearrange("p (s r) -> p s r", r=ER)[:, :, 0:POOL_BUF],
                            in_=EH[:, g, :].rearrange("p (s r) -> p s r", r=POOL_BUF)),
                            reads=[EHB], writes=[EhB[es_]])
                    for il in range(ntl):
                        b = inproj(S["u"], g, il)
                        P.op("act", lambda e, b=b, il=il: e.activation(out=etok(eoff, il), in_=pview(b, il), func=AF.Copy),
                             reads=[psB[b]], writes=[EtB[es_][il]])
                        if kind == "s":
                            P.op("act", lambda e, b=b: e.activation(out=af(UD_OFF[es_], 128), in_=psum[b][:, 0:128], func=AF.Copy),
                                 reads=[psB[b]], writes=[UdB[es_]])

                def pool_ew(g):
                    es_ = g % 2
                    eoff = E_OFF[es_]
                    nlev = g + 1
                    Ereads = [EhB[es_]] + [EtB[es_][il] for il in range(ntl)]
                    src_off, srcB = eoff, Ereads
                    for L in range(1, nlev + 1):
                        sh = 1 << (L - 1)
                        v = (1 << L) - 1
                        ts_ = (L - 1) % 2
                        doff = T_OFF[ts_]
                        P.op("dve", lambda e, doff=doff, src_off=src_off, v=v, sh=sh: e.tensor_tensor(
                            out=af(doff + v, EW - v), in0=af(src_off + v, EW - v), in1=af(src_off + v - sh, EW - v), op=ALU.add),
                            reads=srcB, writes=[TB[ts_]])
                        src_off, srcB = doff, [TB[ts_]]
                    w_ = 1 << nlev
                    doff_ = D_OFF[es_]
                    dten = ab(doff_, 512)
                    for il in range(ntl):
                        P.op("dve", lambda e, il=il, src_off=src_off: e.scalar_tensor_tensor(
                            out=dense(dten, il), in0=etok(src_off, il), scalar=1.0 / w_, in1=etok(eoff, il), op0=ALU.mult, op1=ALU.subtract),
                            reads=srcB + [EtB[es_][il]], writes=[DB[es_][il]])
                    if kind == "p" and grp["first"]:
                        nfix = w_ - 1
                        P.op("dve", lambda e, src_off=src_off: e.tensor_tensor(
                            out=af(FIX_OFF, nfix), in0=af(src_off + POOL_BUF, nfix), in1=invcnt[:, 0:nfix], op=ALU.mult),
                            reads=srcB + [invB], writes=[fixB])
                        P.op("dve", lambda e: e.tensor_tensor(
                            out=dten[:, 0:nfix], in0=af(FIX_OFF, nfix), in1=af(eoff + POOL_BUF, nfix), op=ALU.subtract),
                            reads=[fixB, EtB[es_][0]], writes=[DB[es_][0]])
                    if kind == "p" and not grp["last"]:
                        P.op("dve", lambda e: e.tensor_copy(out=carryE[:, g, 0:POOL_BUF], in_=af(eoff + gw, POOL_BUF)),
                             reads=[EtB[es_][ntl - 1]], writes=[carryEB[g]])

                def pool_state(g):
                    es_ = g % 2
                    eoff = E_OFF[es_]
                    if kind == "p" and grp["last"]:
                        b = next_bank()
                        P.op("pe", lambda e, b=b: e.transpose(out=psum[b][0:POOL_BUF, 0:128], in_=af(eoff + gw, POOL_BUF), identity=ident[:]),
                             reads=[EtB[es_][ntl - 1], identB], writes=[psB[b]])
                        P.op("act", lambda e, b=b: e.activation(out=so_stg[0:POOL_BUF, 0, g * 128:(g + 1) * 128], in_=psum[b][0:POOL_BUF, 0:128], func=AF.Copy),
                             reads=[psB[b]], writes=[soB[0]])
                        if g == 3:
                            P.op("sp", lambda e: e.dma_start(out=npp[l], in_=so_stg[0:POOL_BUF, 0, :]), reads=[soB[0]], chan=so_ch[0])
                    elif kind == "s":
                        b = next_bank()
                        P.op("pe", lambda e, b=b: e.transpose(out=psum[b][:, 0:128], in_=af(UD_OFF[es_], 128), identity=ident[:]),
                             reads=[UdB[es_], identB], writes=[psB[b]])
                        P.op("act", lambda e, b=b: e.activation(out=so_stg[:, 0, g * 128:(g + 1) * 128], in_=psum[b][:, 0:128], func=AF.Copy),
                             reads=[psB[b]], writes=[soB[0]])
                        if g == 3:
                            for s_ in range(NSEQ):
                                P.op("sp", lambda e, s_=s_: e.dma_start(out=nps[l, s_, POOL_BUF - DSEQ:POOL_BUF, :], in_=so_stg[s_ * DSEQ:(s_ + 1) * DSEQ, 0, :]),
                                     reads=[soB[0]], chan=so_ch[0])

                def pool_mm(g):
                    es_ = g % 2
                    dten = ab(D_OFF[es_], 512)
                    for il in range(ntl):
                        o, tw = lo(il)
                        b = mm_group([(poolw[:, g, :], poolwB)], [(dten[:, o:o + tw], DB[es_][il])], tw)
                        P.op("act", lambda e, b=b, o=o, tw=tw: e.activation(out=mix_t[:, g, o:o + tw], in_=psum[b][:, 0:tw], func=AF.Identity,
                                                                            scale=vcol(V_PSC + l * 4 + g)),
                             reads=[psB[b], vecB], writes=[mixB[g][il]])

                def cv_chunk(j):
                    cs_ = j % 2
                    coff = CVE_OFF[cs_]
                    if kind == "p":
                        if grp["first"]:
                            P.op("dve", lambda e: e.memset(af(coff, CONV_BUF), 0.0), writes=[ChB[cs_]])
                        else:
                            P.op("dve", lambda e: e.tensor_copy(out=af(coff, CONV_BUF), in_=carryCV[:, j, :]),
                                 reads=[carryCVB[j]], writes=[ChB[cs_]])
                    else:
                        P.op("dve", lambda e: e.tensor_copy(
                            out=af(coff, NSEQ * CR).rearrange("p (s r) -> p s r", r=CR)[:, :, 0:CONV_BUF],
                            in_=CVH[:, l, j, :].rearrange("p (s r) -> p s r", r=CONV_BUF)),
                            reads=[CVHB], writes=[ChB[cs_]])
                    for il in range(ntl):
                        o, tw = lo(il)
                        bc = inproj(S["c"], j, il)
                        ci = st["r"] % 2
                        st["r"] += 1
                        P.op("act", lambda e, bc=bc, ci=ci, tw=tw: e.activation(out=af(C_OFF[ci], tw), in_=psum[bc][:, 0:tw], func=AF.Copy),
                             reads=[psB[bc]], writes=[CsB[ci]])
                        bv = inproj(S["v"], j, il)
                        P.op("dve", lambda e, bv=bv, ci=ci, il=il: e.tensor_tensor(out=ctok(coff, il, CONV_BUF), in0=dense(af(C_OFF[ci], 512), il) if kind == "s" else af(C_OFF[ci], lo(il)[1]),
                                                                                   in1=pview(bv, il), op=ALU.mult),
                             reads=[CsB[ci], psB[bv]], writes=[CtB[cs_][il]])
                    if kind == "p" and not grp["last"]:
                        P.op("dve", lambda e: e.tensor_copy(out=carryCV[:, j, :], in_=af(coff + gw, CONV_BUF)),
                             reads=[CtB[cs_][ntl - 1]], writes=[carryCVB[j]])

                def conv_state(j):
                    cs_ = j % 2
                    coff = CVE_OFF[cs_]
                    if kind == "p" and grp["last"]:
                        b = next_bank()
                        P.op("pe", lambda e, b=b: e.transpose(out=psum[b][0:CONV_BUF, 0:128], in_=af(coff + gw, CONV_BUF), identity=ident[:]),
                             reads=[CtB[cs_][ntl - 1], identB], writes=[psB[b]])
                        P.op("act", lambda e, b=b: e.activation(out=so_stg[0:CONV_BUF, 1, j * 128:(j + 1) * 128], in_=psum[b][0:CONV_BUF, 0:128], func=AF.Copy),
                             reads=[psB[b]], writes=[soB[1]])
                        if j == 3:
                            P.op("sp", lambda e: e.dma_start(out=ncp[l], in_=so_stg[0:CONV_BUF, 1, :]), reads=[soB[1]], chan=so_ch[1])
                    elif kind == "s":
                        b = next_bank()
                        P.op("dve", lambda e: e.tensor_copy(out=af(CD_OFF, NSEQ * CONV_BUF).rearrange("p (s r) -> p s r", r=CONV_BUF),
                                                            in_=af(coff, NSEQ * CR).rearrange("p (s r) -> p s r", r=CR)[:, :, DSEQ:CR]),
                             reads=[CtB[cs_][0]], writes=[CdB])
                        P.op("pe", lambda e, b=b: e.transpose(out=psum[b][0:NSEQ * CONV_BUF, 0:128],
                                                              in_=af(CD_OFF, NSEQ * CONV_BUF), identity=ident[:]),
                             reads=[CdB, identB], writes=[psB[b]])
                        P.op("act", lambda e, b=b: e.activation(out=so_stg[0:NSEQ * CONV_BUF, 1, j * 128:(j + 1) * 128], in_=psum[b][0:NSEQ * CONV_BUF, 0:128], func=AF.Copy),
                             reads=[psB[b]], writes=[soB[1]])
                        if j == 3:
                            P.op("sp", lambda e: e.dma_start(out=ncs[l], in_=so_stg[0:NSEQ * CONV_BUF, 1, :]), reads=[soB[1]], chan=so_ch[1])

                def conv_z(j):
                    cs_ = j % 2
                    coff = CVE_OFF[cs_]
                    Creads = [ChB[cs_]] + [CtB[cs_][il] for il in range(ntl)]
                    n_ = CW - 2
                    cwb = V_CW + l * 12 + j
                    P.op("act", lambda e: e.activation(out=af(Z_OFF, n_), in_=af(coff + 2, n_), func=AF.Identity, scale=vcol(cwb + 8), bias=vcol(V_CB + l * 4 + j)),
                         reads=Creads + [vecB], writes=[ZB])
                    P.op("dve", lambda e: e.scalar_tensor_tensor(out=af(Z_OFF, n_), in0=af(coff + 1, n_), scalar=vcol(cwb + 4), in1=af(Z_OFF, n_),
                                                                 op0=ALU.mult, op1=ALU.add),
                         reads=Creads + [vecB, ZB], writes=[ZB])
                    P.op("dve", lambda e: e.scalar_tensor_tensor(out=af(Z_OFF, n_), in0=af(coff, n_), scalar=vcol(cwb), in1=af(Z_OFF, n_),
                                                                 op0=ALU.mult, op1=ALU.add),
                         reads=Creads + [vecB, ZB], writes=[ZB])

                def b_chunk(j):
                    bs_ = j % 2
                    for il in range(ntl):
                        o, tw = lo(il)
                        bb = inproj(S["b"], j, il)
                        P.op("act", lambda e, bb=bb, o=o, tw=tw: e.activation(out=af(BS_OFF[bs_] + o, tw), in_=psum[bb][:, 0:tw], func=AF.Copy),
                             reads=[psB[bb]], writes=[BsB[bs_][il]])

                def y_chunk(j):
                    bs_ = j % 2
                    for il in range(ntl):
                        P.op("dve", lambda e, il=il: e.tensor_tensor(out=dense(mix_t[:, 4 + j, :], il), in0=ctok(Z_OFF, il, 0), in1=dense(af(BS_OFF[bs_], 1024), il), op=ALU.mult),
                             reads=[ZB, BsB[bs_][il]], writes=[mixB[4 + j][il]])

                def g_body(hooks=(), extra=False):
                    hooks = list(hooks)

                    def hk():
                        if hooks:
                            hooks.pop(0)()
                    S["u"] = ws_take()
                    S["c"] = ws_take()
                    S["v"] = ws_take()
                    u_chunk(0)
                    hk()
                    u_chunk(1)
                    ws_release_half(S["u"], 0)
                    hk()
                    pool_ew(0)
                    pool_state(0)
                    u_chunk(2)
                    hk()
                    pool_ew(1)
                    pool_state(1)
                    u_chunk(3)
                    hk()
                    ws_release(S["u"])
                    S["b"] = ws_take()
                    pool_mm(0)
                    pool_ew(2)
                    pool_state(2)
                    for j in range(4):
                        cv_chunk(j)
                        if j == 1:
                            ws_release_half(S["c"], 0)
                            ws_release_half(S["v"], 0)
                        hk()
                        conv_state(j)
                        if j == 0:
                            pool_mm(1)
                            pool_ew(3)
                            pool_state(3)
                        conv_z(j)
                        b_chunk(j)
                        if j == 1:
                            ws_release_half(S["b"], 0)
                        if extra:
                            hk()
                        y_chunk(j)
                        if j >= 1 and j <= 2:
                            pool_mm(j + 1)
                    ws_release(S["c"])
                    ws_release(S["v"])
                    ws_release(S["b"])
                    while hooks:
                        hk()

                def g_outproj():
                    slot_o = [ws_take(), ws_take()]
                    for c in range(KT):
                        for il in range(ntl):
                            o, tw = lo(il)
                            ti = tl[il]
                            toff = TILES[ti][0]
                            so_ = slot_o[c // 4]
                            lhs = [wap(so_, c % 4, k) for k in range(KT)]
                            rhs = [(mix_t[:, k, o:o + tw], mixB[k][il]) for k in range(KT)]
                            b = mm_group(lhs, rhs, tw)
                            P.op("dve", lambda e, b=b, c=c, toff=toff, tw=tw: e.tensor_tensor(out=h[:, c, toff:toff + tw], in0=h[:, c, toff:toff + tw], in1=psum[b][:, 0:tw], op=ALU.add),
                                 reads=[psB[b]] + hb(c, toff, tw), writes=hb(c, toff, tw))
                        if c == 1:
                            ws_release_half(slot_o[0], 0)
                        if c == 3:
                            ws_release(slot_o[0])
                        if c == 5:
                            ws_release_half(slot_o[1], 0)
                    ws_release(slot_o[1])

                return g_steps, g_apply, g_body, g_outproj

            psubs = {}

            def p_dma(ti):
                toff, tw = TILES[ti]
                nsub = tw // 128
                subs = []
                for s_ in range(nsub):
                    pi = st["pst"] % 4
                    st["pst"] += 1
                    src = pp[l, toff + s_ * 128: toff + (s_ + 1) * 128, :] if toff < SEQ else psm[l]
                    P.op("sp", lambda e, pi=pi, src=src: e.dma_start(out=pstg[:, pi, :], in_=src), writes=[pstgB[pi]], chan=pl_ch[pi])
                    subs.append(pi)
                psubs[ti] = subs

            def p_tile(ti):
                toff, tw = TILES[ti]
                nsub = tw // 128
                subs = psubs[ti]
                for kk in range(2):
                    b = next_bank()
                    for s_, pi in enumerate(subs):
                        P.op("pe", lambda e, b=b, s_=s_, pi=pi, kk=kk: e.transpose(out=psum[b][:, s_ * 128:(s_ + 1) * 128], in_=pstg[:, pi, kk * 128:(kk + 1) * 128], identity=ident[:]),
                             reads=[pstgB[pi], identB], writes=[psB[b]], signal=(s_ == nsub - 1))
                    P.op("act", lambda e, b=b, kk=kk, toff=toff, tw=tw: e.activation(out=pT_t[:, kk, toff:toff + tw], in_=psum[b][:, 0:tw], func=AF.Copy),
                         reads=[psB[b]], writes=ptb(kk, toff, tw))
                if ti + 1 < len(TILES):
                    p_dma(ti + 1)

            p_dma(0)

            mctx = [dict(), dict()]
            ms0 = norm_steps(0, mctx[0])
            ms1 = norm_steps(1, mctx[1])
            c_hooks = [ms0[0], (lambda: p_tile(0)), ms0[1], (lambda: p_tile(1)), ms0[2], (lambda: p_tile(2)),
                       ms1[0], (lambda: p_tile(3)), ms1[1], (lambda: p_tile(4)), ms1[2]]

            gs = [do_group(l, grp) for grp in GROUPS]
            if l == 0:
                for f_ in gs[0][0]():
                    f_()
                a_apply = gs[0][1]
            else:
                a_apply = pre_apply
            a_apply()
            if l == 0:
                gs[0][2](late_hooks + gs[1][0](), extra=True)
            else:
                gs[0][2](gs[1][0]())
            gs[1][1]()
            gs[0][3]()
            gs[1][2](gs[2][0]())
            gs[2][1]()
            gs[1][3]()
            gs[2][2](c_hooks)
            gs[2][3]()

            P.stage_barrier()
            NTL = len(MT)
            if l + 1 < DEPTH:
                load_sample_state(l + 1)

            def mlp_norm(ti):
                toff, tw = MT[ti]
                norm_tile(ti, lambda k, toff=toff, tw=tw: m_t[:, k, toff:toff + tw], [mB[k][ti] for k in range(KT)], V_NMLP + l * 8, T=MT)

            def up_unit(su, c, ti):
                toff, tw = MT[ti]
                s_ = su[c // 4]
                lhs = [wap(s_, c % 4, k) for k in range(KT)]
                rhs = [(m_t[:, k, toff:toff + tw], mB[k][ti]) for k in range(KT)]
                b = mm_group(lhs, rhs, tw)
                ri = st["r"] % 3
                st["r"] += 1
                P.op("act", lambda e, b=b, ri=ri, tw=tw: e.activation(out=af(R_OFF[ri], tw), in_=psum[b][:, 0:tw], func=AF.Relu),
                     reads=[psB[b]], writes=[rB[ri]])
                P.op("dve", lambda e, ri=ri, c=c, toff=toff, tw=tw: e.tensor_tensor(out=f_t[:, c, toff:toff + tw], in0=af(R_OFF[ri], tw), in1=af(R_OFF[ri], tw), op=ALU.mult),
                     reads=[rB[ri]], writes=[fB[c][ti]])

            def dn_unit(sd, c, ti):
                toff, tw = MT[ti]
                s_ = sd[c // 4]
                lhs = [wap(s_, c % 4, k) for k in range(KT)]
                rhs = [(f_t[:, k, toff:toff + tw], fB[k][ti]) for k in range(KT)]
                b = mm_group(lhs, rhs, tw)
                P.op("dve", lambda e, b=b, c=c, toff=toff, tw=tw: e.tensor_tensor(out=h[:, c, toff:toff + tw], in0=h[:, c, toff:toff + tw], in1=psum[b][:, 0:tw], op=ALU.add),
                     reads=[psB[b]] + hb(c, toff, tw), writes=hb(c, toff, tw))

            def ple_norm(ti):
                toff, tw = MT[ti]
                norm_tile(ti, lambda k, toff=toff, tw=tw: n_t[:, k, toff:toff + tw], [nB[k][ti] for k in range(KT)], V_NPLE + l * 8, T=MT)

            for q in range(4):
                su = [ws_take(), ws_take()]
                if q == 0:
                    for t01 in range(2):
                        toff_, tw_ = MT[t01]
                        norm_apply(t01, mctx[t01]["n"], lambda k, toff_=toff_, tw_=tw_: m_t[:, k, toff_:toff_ + tw_], [mB[k][t01] for k in range(KT)], V_NMLP + l * 8, T=MT)
                    for ti in range(NTL):
                        nx = ti + 2
                        cx = {}
                        stp = norm_steps(nx, cx, T=MT) if nx < NTL else []
                        for c in range(KT):
                            up_unit(su, c, ti)
                            if stp and c in (0, 2, 4):
                                stp.pop(0)()
                                if c == 4:
                                    toff_, tw_ = MT[nx]
                                    norm_apply(nx, cx["n"], lambda k, toff_=toff_, tw_=tw_: m_t[:, k, toff_:toff_ + tw_], [mB[k][nx] for k in range(KT)], V_NMLP + l * 8, T=MT)
                    ws_release(su[0])
                    ws_release(su[1])
                else:
                    for c in range(KT):
                        for ti in range(NTL):
                            up_unit(su, c, ti)
                        if c == 1:
                            ws_release_half(su[0], 0)
                        if c == 3:
                            ws_release(su[0])
                        if c == 5:
                            ws_release_half(su[1], 0)
                    ws_release(su[1])
                sd = [ws_take(), ws_take()]
                if q < 3:
                    for c in range(KT):
                        for ti in range(NTL):
                            dn_unit(sd, c, ti)
                        if c == 1:
                            ws_release_half(sd[0], 0)
                        if c == 3:
                            ws_release(sd[0])
                        if c == 5:
                            ws_release_half(sd[1], 0)
                    ws_release(sd[1])
                else:
                    for ti in range(NTL):
                        for c in range(KT):
                            dn_unit(sd, c, ti)
                        if ti >= 1:
                            ple_norm(ti - 1)
                    ws_release(sd[0])
                    ws_release(sd[1])

            P.stage_barrier()
            sg = [ws_take(), ws_take()]

            def ple_unit(c, ti):
                toff, tw = MT[ti]
                s_ = sg[c // 4]
                lhs = [wap(s_, c % 4, k) for k in range(KT)]
                rhs = [(n_t[:, k, toff:toff + tw], nB[k][ti]) for k in range(KT)]
                b1 = mm_group(lhs, rhs, tw)
                lhs2 = [(wple[:, kk, c * 128:(c + 1) * 128], wpleB) for kk in range(2)]
                rhs2 = [(pT_t[:, kk, toff:toff + tw], ptb(kk, toff, tw)) for kk in range(2)]
                b2 = mm_group(lhs2, rhs2, tw)
                gi_ = st["g"] % 2
                st["g"] += 1
                P.op("act", lambda e, b1=b1, gi_=gi_, tw=tw: e.activation(out=af(G_OFF[gi_], tw), in_=psum[b1][:, 0:tw], func=AF.Sigmoid),
                     reads=[psB[b1]], writes=[gB[gi_]])
                P.op("dve", lambda e, b2=b2, gi_=gi_, tw=tw: e.tensor_tensor(out=af(TMP_OFF[gi_], tw), in0=af(G_OFF[gi_], tw), in1=psum[b2][:, 0:tw], op=ALU.mult),
                     reads=[gB[gi_], psB[b2]], writes=[tmpB[gi_]])
                P.op("dve", lambda e, gi_=gi_, c=c, toff=toff, tw=tw: e.tensor_tensor(out=h[:, c, toff:toff + tw], in0=h[:, c, toff:toff + tw], in1=af(TMP_OFF[gi_], tw), op=ALU.add),
                     reads=[tmpB[gi_]] + hb(c, toff, tw), writes=hb(c, toff, tw))

            def final_norm(ti):
                toff, tw = MT[ti]
                norm_tile(ti, lambda k, tw=tw: af(YFM_OFF + k * 512, tw), yfmB, V_NF, T=MT)

            def final_tile(ti):
                toff, tw = MT[ti]
                for s_ in range(tw // 128):
                    ti_ = st["y"] % 2
                    st["y"] += 1
                    for hf in range(2):
                        b = next_bank()
                        for kk in range(4):
                            k = hf * 4 + kk
                            P.op("pe", lambda e, b=b, kk=kk, k=k, s_=s_: e.transpose(out=psum[b][:, kk * 128:(kk + 1) * 128],
                                                                                    in_=af(YFM_OFF + k * 512 + s_ * 128, 128), identity=ident[:]),
                                 reads=[yfmB[k], identB], writes=[psB[b]], signal=(kk == 3))
                        if hf == 0:
                            P.op("act", lambda e, b=b, ti_=ti_: e.activation(out=af(YTM_OFF[ti_], 512), in_=psum[b][:], func=AF.Copy),
                                 reads=[psB[b]], writes=[ytmB[ti_]])
                        else:
                            P.op("dve", lambda e, b=b, ti_=ti_: e.tensor_copy(out=af(YTM_OFF[ti_] + 512, 512), in_=psum[b][:]),
                                 reads=[psB[b], ytmB[ti_]], writes=[ytmB[ti_]])
                    col_ = toff + s_ * 128
                    dst = y_p[col_:col_ + 128, :] if col_ < SEQ else y_s
                    P.op("sp", lambda e, ti_=ti_, dst=dst: e.dma_start(out=dst, in_=af(YTM_OFF[ti_], 1024)), reads=[ytmB[ti_]], chan=st_ch[ti_])

            last = (l == DEPTH - 1)
            nhooks = []
            if not last:
                nA = do_group(l + 1, GROUPS[0])
                nhooks = nA[0]()
                pre_apply = nA[1]
            lctx = {}
            lsteps = norm_steps(NTL - 1, lctx, T=MT)
            for ti in range(NTL):
                fsteps, fctx = [], {}
                if last and ti >= 1:
                    fsteps = norm_steps(ti - 1, fctx, T=MT)
                for c in range(KT):
                    ple_unit(c, ti)
                    if ti == 0 and c in (0, 2, 4):
                        lsteps.pop(0)()
                        if c == 4:
                            toff_, tw_ = MT[NTL - 1]
                            norm_apply(NTL - 1, lctx["n"], lambda k, toff_=toff_, tw_=tw_: n_t[:, k, toff_:toff_ + tw_], [nB[k][NTL - 1] for k in range(KT)], V_NPLE + l * 8, T=MT)
                    if nhooks and ti >= 2 and c in (1, 3, 5):
                        nhooks.pop(0)()
                    if fsteps and c in (0, 2, 4):
                        fsteps.pop(0)()
                        if c == 4:
                            tw_ = MT[ti - 1][1]
                            norm_apply(ti - 1, fctx["n"], lambda k, tw_=tw_: af(YFM_OFF + k * 512, tw_), yfmB, V_NF, T=MT)
                if last and ti >= 1:
                    final_tile(ti - 1)
            ws_release(sg[0])
            ws_release(sg[1])
            while nhooks:
                nhooks.pop(0)()
            if last:
                final_norm(NTL - 1)
                final_tile(NTL - 1)

        assert ws["taken"] == len(wq) and ws["issued"] == len(wq), (ws["taken"], ws["issued"], len(wq))

        final_waits = [(c, c.count) for c in st_ch + so_ch + [d2d_ch] if c.count > 0]

        all_ch = list(P.engs.values()) + P.chans
        for c in all_ch:
            c.sem = es.enter_context(nc.semaphore(c.name))
        block = es.enter_context(nc.Block())

        def replay(name, e, tail=()):
            for waits, fn, sig in P.progs[name]:
                for c, v in waits:
                    e.wait_ge(c.sem, v)
                ins = fn(e)
                if sig is not None:
                    ins.then_inc(sig[0].sem, sig[1])
            for c, v in tail:
                e.wait_ge(c.sem, v)

        @block.tensor
        def _(e):
            replay("pe", e)

        @block.scalar
        def _(e):
            replay("act", e)

        @block.vector
        def _(e):
            replay("dve", e)

        @block.gpsimd
        def _(e):
            replay("pool", e)

        @block.sync
        def _(e):
            replay("sp", e, tail=final_waits)

    return nc


_NC = None


def kernel(x_prompt, x_sample, state_pool, state_conv, p_prompt, p_sample,
           norm_mix, w_in, pool_w, pool_scale, conv_w, conv_b, w_out,
           norm_mlp, w_up, w_down, norm_ple, w_ple_gate, w_ple_proj, norm_f):
    global _NC
    f = lambda a: np.ascontiguousarray(np.asarray(a, dtype=np.float32))
    if _NC is None:
        _NC = build()
    nc = _NC
    vecs = np.concatenate([
        f(norm_mix).reshape(16, 128), f(norm_mlp).reshape(16, 128), f(norm_ple).reshape(16, 128),
        f(norm_f).reshape(8, 128), f(pool_scale).reshape(8, 128), f(conv_w).reshape(24, 128),
        f(conv_b).reshape(8, 128)], axis=0)
    shared = dict(vecs=f(vecs), w_in=f(w_in), pool_w=f(pool_w), w_out=f(w_out), w_up=f(w_up),
                  w_down=f(w_down), w_gate=f(w_ple_gate), w_ple=f(w_ple_proj))
    x_prompt, x_sample = f(x_prompt), f(x_sample)
    state_pool, state_conv = f(state_pool), f(state_conv)
    p_prompt, p_sample = f(p_prompt), f(p_sample)
    in_maps = []
    for c in range(NCORES):
        ss = slice(c * NSEQ, (c + 1) * NSEQ)
        m = dict(shared)
        m["x_p"] = x_prompt[c]
        m["x_s"] = f(x_sample[ss].reshape(NS, D))
        m["sp_in"] = f(state_pool[:, ss].reshape(DEPTH, NSEQ * POOL_BUF, 512))
        m["sc_in"] = f(state_conv[:, ss].reshape(DEPTH, NSEQ * CONV_BUF, 512))
        m["pp"] = f(p_prompt[:, c])
        m["psm"] = f(p_sample[:, ss].reshape(DEPTH, NS, PLE))
        in_maps.append(m)
    res = run_bass_kernel_spmd(nc, in_maps, core_ids=list(range(NCORES)))
    R = res.results
    y_prompt = np.stack([R[c]["y_p"] for c in range(NCORES)], axis=0)
    y_sample = np.concatenate([R[c]["y_s"].reshape(NSEQ, DSEQ, D) for c in range(NCORES)], axis=0)
    new_pool_prompt = np.stack([R[c]["npp"] for c in range(NCORES)], axis=1)
    new_conv_prompt = np.stack([R[c]["ncp"] for c in range(NCORES)], axis=1)
    new_pool_sample = np.concatenate([R[c]["nps"] for c in range(NCORES)], axis=1)
    new_conv_sample = np.concatenate([R[c]["ncs"].reshape(DEPTH, NSEQ, CONV_BUF, 512) for c in range(NCORES)], axis=1)
    return (y_prompt.astype(np.float32), y_sample.astype(np.float32), new_pool_prompt.astype(np.float32),
            new_conv_prompt.astype(np.float32), new_pool_sample.astype(np.float32), new_conv_sample.astype(np.float32))
```

```python
import numpy as np
from contextlib import ExitStack
import concourse.bass as bass
import concourse.mybir as mybir
from concourse.bass_utils import run_bass_kernel_spmd

F32 = mybir.dt.float32
BF16 = mybir.dt.bfloat16
AF = mybir.ActivationFunctionType
ALU = mybir.AluOpType

NCORES = 8
D = 1024
KT = 8
SEQ = 2048
NS = 128
NSEQ = 16
DSEQ = 8
NTOK = SEQ + NS
DEPTH = 2
DFF = 4096
PLE = 256
EPS = 1e-6
POOL_BUF = 15
CONV_BUF = 2
ER = POOL_BUF + DSEQ
CR = CONV_BUF + DSEQ

TILES = [(0, 512), (512, 512), (1024, 512), (1536, 512), (2048, 128)]
MT = [(0, 512), (512, 512), (1024, 512), (1536, 384), (1920, 256)]
NSUB = NTOK // 128
GROUPS = [
    dict(tiles=[0, 1], kind="p", first=True, last=False),
    dict(tiles=[2, 3], kind="p", first=False, last=True),
    dict(tiles=[4], kind="s", first=False, last=False),
]

V_NMIX, V_NMLP, V_NPLE, V_NF, V_PSC, V_CW, V_CB = 0, 16, 32, 48, 56, 64, 88
NVEC = 96


class Ch:
    def __init__(self, name):
        self.name = name
        self.count = 0
        self.sem = None


class Buf:
    __slots__ = ("w", "r", "arena")

    def __init__(self, arena=False):
        self.w = {}
        self.r = {}
        self.arena = arena


def _merge(dst, src):
    for c, v in src.items():
        if dst.get(c, 0) < v:
            dst[c] = v


class Prog:
    ENGS = ["pe", "act", "dve", "pool", "sp"]

    def __init__(self):
        self.engs = {n: Ch(n) for n in self.ENGS}
        self.progs = {n: [] for n in self.ENGS}
        self.waited = {n: {} for n in self.ENGS}
        self.chans = []
        self.barrier = {}
        self.arena_bufs = []

    def chan(self, name):
        c = Ch(name)
        self.chans.append(c)
        return c

    def abuf(self):
        b = Buf(arena=True)
        self.arena_bufs.append(b)
        return b

    def stage_barrier(self):
        for b in self.arena_bufs:
            _merge(self.barrier, b.w)
            _merge(self.barrier, b.r)

    def op(self, eng, fn, reads=(), writes=(), signal=True, chan=None):
        need = {}
        for b in reads:
            _merge(need, b.w)
        for b in writes:
            _merge(need, b.w)
            _merge(need, b.r)
            if b.arena:
                _merge(need, self.barrier)
        waits = []
        wd = self.waited[eng]
        pe_ch = self.engs["pe"]
        for c, v in need.items():
            if eng == "pe" and c is pe_ch:
                continue
            if wd.get(c, 0) < v:
                waits.append((c, v))
                wd[c] = v
        if chan is not None:
            chan.count += 16
            tok = {chan: chan.count}
            sig = (chan, 16)
        else:
            ch = self.engs[eng]
            tok = {ch: ch.count + 1}
            if signal:
                ch.count += 1
                sig = (ch, 1)
            else:
                sig = None
        self.progs[eng].append((waits, fn, sig))
        for b in reads:
            _merge(b.r, tok)
        for b in writes:
            b.w = dict(tok)
            b.r = {}
        return tok


def build():
    nc = bass.Bass("TRN2", target_bir_lowering=False)
    dt_in = lambda name, shape: nc.dram_tensor(name, shape, F32, kind="ExternalInput").ap()
    dt_out = lambda name, shape: nc.dram_tensor(name, shape, F32, kind="ExternalOutput").ap()
    x_p = dt_in("x_p", [SEQ, D])
    x_s = dt_in("x_s", [NS, D])
    sp_in = dt_in("sp_in", [DEPTH, NSEQ * POOL_BUF, 512])
    sc_in = dt_in("sc_in", [DEPTH, NSEQ * CONV_BUF, 512])
    pp = dt_in("pp", [DEPTH, SEQ, PLE])
    psm = dt_in("psm", [DEPTH, NS, PLE])
    vecs = dt_in("vecs", [NVEC, 128])
    w_in = dt_in("w_in", [DEPTH, D, 2048])
    pool_w = dt_in("pool_w", [DEPTH, 4, 128, 128])
    w_out = dt_in("w_out", [DEPTH, D, D])
    w_up = dt_in("w_up", [DEPTH, D, DFF])
    w_down = dt_in("w_down", [DEPTH, DFF, D])
    w_gate = dt_in("w_gate", [DEPTH, D, D])
    w_ple = dt_in("w_ple", [DEPTH, PLE, D])
    y_p = dt_out("y_p", [SEQ, D])
    y_s = dt_out("y_s", [NS, D])
    npp = dt_out("npp", [DEPTH, POOL_BUF, 512])
    ncp = dt_out("ncp", [DEPTH, CONV_BUF, 512])
    nps = dt_out("nps", [DEPTH, NSEQ, POOL_BUF, 512])
    ncs = dt_out("ncs", [DEPTH, NSEQ * CONV_BUF, 512])

    P = Prog()
    AW = 19904
    es = ExitStack()
    with es:
        sb = lambda name, shape, dt: es.enter_context(nc.sbuf_tensor(name, shape, dt))
        h = sb("h", [128, KT, NTOK], F32)
        ring = sb("ring", [128, 6, 8, 256], BF16)
        pT_t = sb("pT", [128, 2, NTOK], BF16)
        wple = sb("wple", [128, 2, D], BF16)
        poolw = sb("poolw", [128, 4, 128], BF16)
        sq = sb("sq", [128, 4, 512], BF16)
        rstd = sb("rstd", [128, 2, 512], F32)
        EH = sb("EH", [128, 4, NSEQ * POOL_BUF], F32)
        CVH = sb("CVH", [128, DEPTH, 4, NSEQ * CONV_BUF], F32)
        carryE = sb("carryE", [128, 4, 16], F32)
        carryCV = sb("carryCV", [128, 4, 2], F32)
        vec = sb("vec", [128, NVEC], F32)
        ident = sb("ident", [128, 128], F32)
        ones = sb("ones", [128, 128], BF16)
        invcnt = sb("invcnt", [128, 16], F32)
        so_stg = sb("so_stg", [128, 2, 512], F32)
        pstg = sb("pstg", [128, 4, PLE], F32)
        arena = sb("arena", [128, AW], F32)
        psum = [es.enter_context(nc.psum_tensor(f"ps{i}", [128, 512], F32)) for i in range(8)]

        def af(off, n):
            return arena[:, off:off + n]

        def ab(off, nwords):
            return arena[:, off:off + nwords].bitcast(BF16)

        hB = [[Buf() for _ in range(NSUB)] for _ in range(KT)]

        def hb(k, toff, tw):
            return hB[k][toff // 128:(toff + tw) // 128]

        psB = [Buf() for _ in range(8)]
        ringB = [Buf() for _ in range(6)]
        wpleB, poolwB = Buf(), Buf()
        sqB = [Buf() for _ in range(4)]
        rstdB = [Buf(), Buf()]
        EHB, CVHB, vecB, identB, onesB, invB = Buf(), Buf(), Buf(), Buf(), Buf(), Buf()
        carryEB = [Buf() for _ in range(4)]
        carryCVB = [Buf() for _ in range(4)]
        soB = [Buf(), Buf()]
        pstgB = [Buf() for _ in range(4)]
        pTB = [[Buf() for _ in range(NSUB)] for _ in range(2)]

        def ptb(kk, toff, tw):
            return pTB[kk][toff // 128:(toff + tw) // 128]


        st = dict(bank=0, sq=0, norm=0, r=0, g=0, pst=0, y=0)

        reserved = set()

        def next_bank():
            while True:
                b = st["bank"]
                st["bank"] = (b + 1) % 8
                if b not in reserved:
                    return b

        ring_ch = [P.chan(f"ring{i}") for i in range(6)]
        poolw_ch = P.chan("poolw")
        wple_ch = P.chan("wple")
        setup_ch = [P.chan(f"su{i}") for i in range(3)]
        sui = iter(setup_ch)
        ld_ch = [P.chan(f"ld{i}") for i in range(8)]
        pl_ch = [P.chan(f"pl{i}") for i in range(4)]
        st_ch = [P.chan(f"st{i}") for i in range(2)]
        so_ch = [P.chan(f"so{i}") for i in range(2)]
        d2d_ch = P.chan("d2d")

        def vcol(c):
            return vec[:, c:c + 1]

        def wview(w_l, c0):
            v = w_l.rearrange("(k p) n -> p k n", p=128)
            return [v[:, :, c0:c0 + 256], v[:, :, c0 + 256:c0 + 512]]

        wq = []
        for l in range(DEPTH):
            for _g in GROUPS:
                wq += wview(w_in[l], 0) + wview(w_in[l], 1024) + wview(w_in[l], 1536) + wview(w_in[l], 512) \
                    + wview(w_out[l], 0) + wview(w_out[l], 512)
            for q in range(4):
                wq += wview(w_up[l], q * 1024) + wview(w_up[l], q * 1024 + 512) \
                    + wview(w_down[l][q * 1024:(q + 1) * 1024, :], 0) + wview(w_down[l][q * 1024:(q + 1) * 1024, :], 512)
            wq += wview(w_gate[l], 0) + wview(w_gate[l], 512)
        ws = dict(issued=0, taken=0, free=[True] * 6, gate=())

        def ws_pump():
            while ws["issued"] < len(wq) and ws["free"][ws["issued"] % 6]:
                i = ws["issued"]
                s_ = i % 6
                P.op("pool", lambda e, s_=s_, src=wq[i]: e.dma_start(out=ring[:, s_], in_=src), reads=(ws["gate"] if i in (2, 3, 4, 5) else ()),
                     writes=[ringB[s_]], chan=ring_ch[s_])
                ws["free"][s_] = False
                ws["issued"] += 1

        def ws_take():
            i = ws["taken"]
            ws["taken"] += 2
            assert i + 1 < ws["issued"], "weight block not yet issued"
            return dict(s=[i % 6, (i + 1) % 6], rel=[False, False])

        def ws_release_half(blk, hh):
            if not blk["rel"][hh]:
                blk["rel"][hh] = True
                ws["free"][blk["s"][hh]] = True
                ws_pump()

        def ws_release(blk):
            ws_release_half(blk, 0)
            ws_release_half(blk, 1)

        def wap(blk, c4, k):
            s_ = blk["s"][c4 // 2]
            return (ring[:, s_, k, (c4 % 2) * 128:(c4 % 2 + 1) * 128], ringB[s_])

        P.op("pool", lambda e: e.memset(ident[:], 0.0), writes=[identB])
        P.op("pool", lambda e: e.affine_select(out=ident[:], in_=ident[:], pattern=[[-1, 128]],
                                               compare_op=ALU.not_equal, fill=1.0, base=0,
                                               channel_multiplier=1), writes=[identB])
        P.op("pool", lambda e: e.memset(ones[:], 1.0), writes=[onesB])
        for t in range(16):
            P.op("pool", lambda e, t=t: e.memset(invcnt[:, t:t + 1], 1.0 / (t + 1)), writes=[invB])

        XSTG = 8192
        SPSTG = 16384
        vstg = pstg[:, 0, 0:128]
        vstgB = pstgB[0]
        P.op("sp", lambda e: e.dma_start(out=vstg[0:NVEC, :], in_=vecs), writes=[vstgB], chan=next(sui))
        b = next_bank()
        P.op("pe", lambda e, b=b: e.transpose(out=psum[b][:, 0:NVEC], in_=vstg[0:NVEC, :], identity=ident[0:NVEC, 0:NVEC]),
             reads=[vstgB, identB], writes=[psB[b]])
        P.op("act", lambda e, b=b: e.activation(out=vec[:], in_=psum[b][:, 0:NVEC], func=AF.Copy),
             reads=[psB[b]], writes=[vecB])

        pstg2 = pstg[:].rearrange("p a c -> p (a c)").rearrange("p (h c) -> p h c", h=2)

        def load_sample_state(l):
            for hf in range(2):
                stg = pstg2[:, hf, :]
                sBs = [pstgB[2 * hf], pstgB[2 * hf + 1]]
                P.op("sp", lambda e, stg=stg, hf=hf: e.dma_start(out=stg[0:120, :], in_=sp_in[l, hf * 120:(hf + 1) * 120, :]),
                     writes=sBs, chan=pl_ch[2 * hf])
                b = next_bank()
                for g in range(4):
                    P.op("pe", lambda e, b=b, g=g, stg=stg: e.transpose(out=psum[b][:, g * 120:(g + 1) * 120], in_=stg[0:120, g * 128:(g + 1) * 128],
                                                                        identity=ident[0:120, 0:120]),
                         reads=sBs + [identB], writes=[psB[b]], signal=(g == 3))
                P.op("act", lambda e, b=b, hf=hf: e.activation(out=EH[:, :, hf * 120:(hf + 1) * 120],
                                                               in_=psum[b][:, 0:480].rearrange("p (g c) -> p g c", g=4), func=AF.Copy),
                     reads=[psB[b]], writes=[EHB])

        load_sample_state(0)
        for l in range(DEPTH):
            stg = so_stg[:, l, :]
            sB_ = soB[l]
            P.op("sp", lambda e, stg=stg, l=l: e.dma_start(out=stg[0:32, :], in_=sc_in[l]), writes=[sB_], chan=next(sui))
            b = next_bank()
            for j in range(4):
                P.op("pe", lambda e, b=b, j=j, stg=stg: e.transpose(out=psum[b][:, j * 32:(j + 1) * 32], in_=stg[0:32, j * 128:(j + 1) * 128],
                                                                    identity=ident[0:32, 0:32]),
                     reads=[sB_, identB], writes=[psB[b]], signal=(j == 3))
            P.op("act", lambda e, b=b, l=l: e.activation(out=CVH[:, l], in_=psum[b][:, 0:128].rearrange("p (g c) -> p g c", g=4), func=AF.Copy),
                 reads=[psB[b]], writes=[CVHB])
            P.op("sp", lambda e, l=l: e.dma_start(out=nps[l, :, 0:POOL_BUF - DSEQ, :],
                                                  in_=sp_in[l].rearrange("(s r) c -> s r c", r=POOL_BUF)[:, DSEQ:POOL_BUF, :]),
                 chan=d2d_ch)

        xstgB = [P.abuf() for _ in range(8)]
        xstg_all = [af(XSTG + i * 1024, 1024) for i in range(8)]
        for ti, (toff, tw) in enumerate(TILES[0:2]):
            nsub = tw // 128
            xo = (ti % 2) * 4
            xstg = xstg_all[xo:xo + 4]
            xsB = xstgB[xo:xo + 4]
            for s_ in range(nsub):
                src = x_p[toff + s_ * 128: toff + (s_ + 1) * 128, :] if toff < SEQ else x_s
                P.op("sp", lambda e, s_=s_, src=src, xstg=xstg: e.dma_start(out=xstg[s_], in_=src), writes=[xsB[s_]], chan=ld_ch[xo + s_])
            if nsub == 4:
                for k in range(KT):
                    b = next_bank()
                    for s_ in range(4):
                        P.op("pe", lambda e, b=b, s_=s_, k=k, xstg=xstg: e.transpose(out=psum[b][:, s_ * 128:(s_ + 1) * 128], in_=xstg[s_][:, k * 128:(k + 1) * 128], identity=ident[:]),
                             reads=[xsB[s_], identB], writes=[psB[b]], signal=(s_ == 3))
                    if ti == 1 and k == 0:
                        gateB = Buf()
                        gateB.w = dict(psB[b].w)
                        ws["gate"] = (gateB,)
                    if k % 2 == 0:
                        P.op("act", lambda e, b=b, k=k, toff=toff: e.activation(out=h[:, k, toff:toff + 512], in_=psum[b][:], func=AF.Copy),
                             reads=[psB[b]], writes=hb(k, toff, 512))
                    else:
                        P.op("dve", lambda e, b=b, k=k, toff=toff: e.tensor_copy(out=h[:, k, toff:toff + 512], in_=psum[b][:]),
                             reads=[psB[b]], writes=hb(k, toff, 512))
            else:
                for kh in range(2):
                    b = next_bank()
                    for kk in range(4):
                        k = kh * 4 + kk
                        P.op("pe", lambda e, b=b, kk=kk, k=k, xstg=xstg: e.transpose(out=psum[b][:, kk * 128:(kk + 1) * 128], in_=xstg[0][:, k * 128:(k + 1) * 128], identity=ident[:]),
                             reads=[xsB[0], identB], writes=[psB[b]], signal=(kk == 3))
                    P.op("act", lambda e, b=b, kh=kh, toff=toff: e.activation(out=h[:, kh * 4:(kh + 1) * 4, toff:toff + 128],
                                                                              in_=psum[b][:].rearrange("p (k c) -> p k c", k=4), func=AF.Copy),
                         reads=[psB[b]], writes=[hb(kh * 4 + kk, toff, 128)[0] for kk in range(4)])

        ws_pump()

        xl = [pT_t[:, j, :].bitcast(F32)[:, 0:1024] for j in range(2)]
        xl_ch = [P.chan(f"xl{j}") for j in range(2)]
        late_list = []
        for ti in range(2, len(TILES)):
            toff, tw = TILES[ti]
            for s_ in range(tw // 128):
                src = x_p[toff + s_ * 128: toff + (s_ + 1) * 128, :] if toff < SEQ else x_s
                late_list.append((ti, toff + s_ * 128, src))

        def late_dma(n):
            ti, col0, src = late_list[n]
            j = n % 2
            P.op("sp", lambda e, j=j, src=src: e.dma_start(out=xl[j], in_=src), writes=list(pTB[j]), chan=xl_ch[j])

        def late_step(n):
            def f_():
                ti, col0, src = late_list[n]
                j = n % 2
                if n + 1 < len(late_list):
                    late_dma(n + 1)
                for hf in range(2):
                    b = next_bank()
                    for kk in range(4):
                        k = hf * 4 + kk
                        P.op("pe", lambda e, b=b, kk=kk, k=k: e.transpose(out=psum[b][:, kk * 128:(kk + 1) * 128], in_=xl[j][:, k * 128:(k + 1) * 128], identity=ident[:]),
                             reads=list(pTB[j]) + [identB], writes=[psB[b]], signal=(kk == 3))
                    P.op("act", lambda e, b=b, hf=hf: e.activation(out=h[:, hf * 4:(hf + 1) * 4, col0:col0 + 128],
                                                                   in_=psum[b][:].rearrange("p (k c) -> p k c", k=4), func=AF.Copy),
                         reads=[psB[b]], writes=[hb(hf * 4 + kk, col0, 128)[0] for kk in range(4)])
            return f_

        late_dma(0)
        late_hooks = [late_step(n) for n in range(len(late_list))]

        def norm_tile(ti, dst_ap_fn, dstB, gbase, T=TILES):
            toff, tw = T[ti]
            b = next_bank()
            for k in range(KT):
                q = st["sq"]
                st["sq"] = (q + 1) % 4
                P.op("act", lambda e, q=q, k=k: e.activation(out=sq[:, q, 0:tw], in_=h[:, k, toff:toff + tw], func=AF.Square),
                     reads=hb(k, toff, tw), writes=[sqB[q]])
                P.op("pe", lambda e, q=q, k=k, b=b: e.matmul(psum[b][:, 0:tw], lhsT=ones[:], rhs=sq[:, q, 0:tw], start=(k == 0), stop=(k == KT - 1)),
                     reads=[sqB[q], onesB], writes=[psB[b]], signal=True)
            n_ = st["norm"]
            st["norm"] = 1 - n_
            P.op("act", lambda e, b=b, n_=n_: e.activation(out=rstd[:, n_, 0:tw], in_=psum[b][:, 0:tw], func=AF.Ln, scale=1.0 / D, bias=EPS),
                 reads=[psB[b]], writes=[rstdB[n_]])
            P.op("act", lambda e, n_=n_: e.activation(out=rstd[:, n_, 0:tw], in_=rstd[:, n_, 0:tw], func=AF.Exp, scale=-0.5),
                 reads=[rstdB[n_]], writes=[rstdB[n_]])
            for k in range(KT):
                P.op("dve", lambda e, k=k, n_=n_: e.scalar_tensor_tensor(out=dst_ap_fn(k), in0=h[:, k, toff:toff + tw], scalar=vcol(gbase + k),
                                                                         in1=rstd[:, n_, 0:tw], op0=ALU.mult, op1=ALU.mult),
                     reads=hb(k, toff, tw) + [rstdB[n_], vecB], writes=[dstB[k]])

        def norm_steps(ti, ctx, T=TILES):
            toff, tw = T[ti]

            def squares(hf):
                for j in range(4):
                    k = hf * 4 + j
                    P.op("act", lambda e, j=j, k=k: e.activation(out=sq[:, j, 0:tw], in_=h[:, k, toff:toff + tw], func=AF.Square),
                         reads=hb(k, toff, tw), writes=[sqB[j]])

            def stats(hf):
                b = ctx["b"]
                for j in range(4):
                    k = hf * 4 + j
                    P.op("pe", lambda e, j=j, k=k, b=b: e.matmul(psum[b][:, 0:tw], lhsT=ones[:], rhs=sq[:, j, 0:tw], start=(k == 0), stop=(k == KT - 1)),
                         reads=[sqB[j], onesB], writes=[psB[b]], signal=True)

            def s1():
                ctx["b"] = next_bank()
                reserved.add(ctx["b"])
                squares(0)

            def s2():
                stats(0)
                squares(1)

            def s3():
                stats(1)
                b = ctx["b"]
                n_ = st["norm"]
                st["norm"] = 1 - n_
                ctx["n"] = n_
                P.op("act", lambda e: e.activation(out=rstd[:, n_, 0:tw], in_=psum[b][:, 0:tw], func=AF.Ln, scale=1.0 / D, bias=EPS),
                     reads=[psB[b]], writes=[rstdB[n_]])
                P.op("act", lambda e: e.activation(out=rstd[:, n_, 0:tw], in_=rstd[:, n_, 0:tw], func=AF.Exp, scale=-0.5),
                     reads=[rstdB[n_]], writes=[rstdB[n_]])
                reserved.discard(b)

            return [s1, s2, s3]

        def norm_apply(ti, n_, dst_ap_fn, dstB, gbase, T=TILES):
            toff, tw = T[ti]
            for k in range(KT):
                P.op("dve", lambda e, k=k: e.scalar_tensor_tensor(out=dst_ap_fn(k), in0=h[:, k, toff:toff + tw], scalar=vcol(gbase + k),
                                                                  in1=rstd[:, n_, 0:tw], op0=ALU.mult, op1=ALU.mult),
                     reads=hb(k, toff, tw) + [rstdB[n_], vecB], writes=[dstB[k]])

        def mm_group(lhs_list, rhs_list, width):
            b = next_bank()
            n = len(lhs_list)
            for i in range(n):
                (lap, lB), (rap, rB) = lhs_list[i], rhs_list[i]
                P.op("pe", lambda e, b=b, lap=lap, rap=rap, i=i: e.matmul(psum[b][:, 0:width], lhsT=lap, rhs=rap, start=(i == 0), stop=(i == n - 1)),
                     reads=(lB if isinstance(lB, list) else [lB]) + (rB if isinstance(rB, list) else [rB]), writes=[psB[b]], signal=(i == n - 1))
            return b

        A_OFF, MIX_OFF = 0, 4096
        E_OFF = [8192, 9232]
        T_OFF = [10272, 11312]
        D_OFF = [12352, 12864]
        CVE_OFF = [13376, 14404]
        Z_OFF = 15432
        C_OFF = [16460, 16972]
        FIX_OFF = 17484
        BS_OFF = [17800, 18824]
        UD_OFF = [17500, 17628]
        CD_OFF = 17756
        a_t = ab(A_OFF, 4096).rearrange("p (k n) -> p k n", k=KT)
        mix_t = ab(MIX_OFF, 4096).rearrange("p (k n) -> p k n", k=KT)
        aB = [[P.abuf() for _ in range(2)] for _ in range(KT)]
        mixB = [[P.abuf() for _ in range(2)] for _ in range(KT)]
        EtB = [[P.abuf() for _ in range(2)] for _ in range(2)]
        EhB = [P.abuf() for _ in range(2)]
        TB = [P.abuf() for _ in range(2)]
        DB = [[P.abuf() for _ in range(2)] for _ in range(2)]
        CtB = [[P.abuf() for _ in range(2)] for _ in range(2)]
        ChB = [P.abuf() for _ in range(2)]
        ZB = P.abuf()
        CsB = [P.abuf() for _ in range(2)]
        fixB = P.abuf()
        BsB = [[P.abuf() for _ in range(2)] for _ in range(2)]
        UdB = [P.abuf(), P.abuf()]
        CdB = P.abuf()
        M_OFF, F_OFF = 0, 8704
        R_OFF = [17408, 17920, 18432]
        m_t = ab(M_OFF, 8704).rearrange("p (k n) -> p k n", k=KT)
        f_t = ab(F_OFF, 8704).rearrange("p (k n) -> p k n", k=KT)
        mB = [[P.abuf() for _ in TILES] for _ in range(KT)]
        fB = [[P.abuf() for _ in TILES] for _ in range(KT)]
        rB = [P.abuf() for _ in range(3)]
        G_OFF = [8704, 9216]
        TMP_OFF = [9728, 10240]
        nB = mB
        n_t = m_t
        gB = [P.abuf() for _ in range(2)]
        tmpB = [P.abuf() for _ in range(2)]
        YFM_OFF = 10752
        YTM_OFF = [14848, 15872]
        yfmB = [P.abuf() for _ in range(KT)]
        ytmB = [P.abuf() for _ in range(2)]

        for l in range(DEPTH):
            P.stage_barrier()
            P.op("pool", lambda e, l=l: e.dma_start(out=poolw[:], in_=pool_w[l].rearrange("g c d -> c g d")), writes=[poolwB], chan=poolw_ch)
            P.op("pool", lambda e, l=l: e.dma_start(out=wple[:], in_=w_ple[l].rearrange("(k p) n -> p k n", p=128)), writes=[wpleB], chan=wple_ch)
            def do_group(l, grp):
                kind = grp["kind"]
                tl = grp["tiles"]
                ntl = len(tl)
                g0 = TILES[tl[0]][0]
                gw = sum(TILES[t][1] for t in tl)
                if kind == "p":
                    EW, CW = POOL_BUF + gw, CONV_BUF + gw
                else:
                    EW, CW = NSEQ * ER, NSEQ * CR

                def lo(il):
                    toff, tw = TILES[tl[il]]
                    return toff - g0, tw

                def etok(off, il):
                    o, tw = lo(il)
                    if kind == "p":
                        return af(off + POOL_BUF + o, tw)
                    return af(off, NSEQ * ER).rearrange("p (s r) -> p s r", r=ER)[:, :, POOL_BUF:ER]

                def ctok(off, il, hist):
                    o, tw = lo(il)
                    if kind == "p":
                        return af(off + hist + o, tw)
                    return af(off, NSEQ * CR).rearrange("p (s r) -> p s r", r=CR)[:, :, hist:hist + DSEQ]

                def pview(b, il):
                    o, tw = lo(il)
                    if kind == "p":
                        return psum[b][:, 0:tw]
                    return psum[b][:, 0:tw].rearrange("p (s t) -> p s t", t=DSEQ)

                def dense(ap2, il):
                    o, tw = lo(il)
                    v = ap2[:, o:o + tw]
                    if kind == "s":
                        return v.rearrange("p (s t) -> p s t", t=DSEQ)
                    return v

                S = {}

                nctx = [dict() for _ in tl]

                def g_steps():
                    out = []
                    for il, ti in enumerate(tl):
                        out += norm_steps(ti, nctx[il])
                    return out

                def g_apply():
                    for il, ti in enumerate(tl):
                        o, tw = lo(il)
                        norm_apply(ti, nctx[il]["n"], lambda k, o=o, tw=tw: a_t[:, k, o:o + tw], [aB[k][il] for k in range(KT)], V_NMIX + l * 8)

                def inproj(slot, c, il):
                    o, tw = lo(il)
                    lhs = [wap(slot, c, k) for k in range(KT)]
                    rhs = [(a_t[:, k, o:o + tw], aB[k][il]) for k in range(KT)]
                    return mm_group(lhs, rhs, tw)

                def u_chunk(g):
                    es_ = g % 2
                    eoff = E_OFF[es_]
                    if kind == "p":
                        if grp["first"]:
                            P.op("dve", lambda e: e.memset(af(eoff, POOL_BUF), 0.0), writes=[EhB[es_]])
                        else:
                            P.op("dve", lambda e: e.tensor_copy(out=af(eoff, POOL_BUF), in_=carryE[:, g, 0:POOL_BUF]),
                                 reads=[carryEB[g]], writes=[EhB[es_]])
                    else:
                        P.op("dve", lambda e: e.tensor_copy(
                            out=af(eoff, NSEQ * ER).rearrange("p (s r) -> p s r", r=ER)[:, :, 0:POOL_BUF],
                            in_=EH[:, g, :].rearrange("p (s r) -> p s r", r=POOL_BUF)),
                            reads=[EHB], writes=[EhB[es_]])
                    for il in range(ntl):
                        b = inproj(S["u"], g, il)
                        P.op("act", lambda e, b=b, il=il: e.activation(out=etok(eoff, il), in_=pview(b, il), func=AF.Copy),
                             reads=[psB[b]], writes=[EtB[es_][il]])
                        if kind == "s":
                            P.op("act", lambda e, b=b: e.activation(out=af(UD_OFF[es_], 128), in_=psum[b][:, 0:128], func=AF.Copy),
                                 reads=[psB[b]], writes=[UdB[es_]])

                def pool_ew(g):
                    es_ = g % 2
                    eoff = E_OFF[es_]
                    nlev = g + 1
                    Ereads = [EhB[es_]] + [EtB[es_][il] for il in range(ntl)]
                    src_off, srcB = eoff, Ereads
                    for L in range(1, nlev + 1):
                        sh = 1 << (L - 1)
                        v = (1 << L) - 1
                        ts_ = (L - 1) % 2
                        doff = T_OFF[ts_]
                        P.op("dve", lambda e, doff=doff, src_off=src_off, v=v, sh=sh: e.tensor_tensor(
                            out=af(doff + v, EW - v), in0=af(src_off + v, EW - v), in1=af(src_off + v - sh, EW - v), op=ALU.add),
                            reads=srcB, writes=[TB[ts_]])
                        src_off, srcB = doff, [TB[ts_]]
                    w_ = 1 << nlev
                    doff_ = D_OFF[es_]
                    dten = ab(doff_, 512)
                    for il in range(ntl):
                        P.op("dve", lambda e, il=il, src_off=src_off: e.scalar_tensor_tensor(
                            out=dense(dten, il), in0=etok(src_off, il), scalar=1.0 / w_, in1=etok(eoff, il), op0=ALU.mult, op1=ALU.subtract),
                            reads=srcB + [EtB[es_][il]], writes=[DB[es_][il]])
                    if kind == "p" and grp["first"]:
                        nfix = w_ - 1
                        P.op("dve", lambda e, src_off=src_off: e.tensor_tensor(
                            out=af(FIX_OFF, nfix), in0=af(src_off + POOL_BUF, nfix), in1=invcnt[:, 0:nfix], op=ALU.mult),
                            reads=srcB + [invB], writes=[fixB])
                        P.op("dve", lambda e: e.tensor_tensor(
                            out=dten[:, 0:nfix], in0=af(FIX_OFF, nfix), in1=af(eoff + POOL_BUF, nfix), op=ALU.subtract),
                            reads=[fixB, EtB[es_][0]], writes=[DB[es_][0]])
                    if kind == "p" and not grp["last"]:
                        P.op("dve", lambda e: e.tensor_copy(out=carryE[:, g, 0:POOL_BUF], in_=af(eoff + gw, POOL_BUF)),
                             reads=[EtB[es_][ntl - 1]], writes=[carryEB[g]])

                def pool_state(g):
                    es_ = g % 2
                    eoff = E_OFF[es_]
                    if kind == "p" and grp["last"]:
                        b = next_bank()
                        P.op("pe", lambda e, b=b: e.transpose(out=psum[b][0:POOL_BUF, 0:128], in_=af(eoff + gw, POOL_BUF), identity=ident[:]),
                             reads=[EtB[es_][ntl - 1], identB], writes=[psB[b]])
                        P.op("act", lambda e, b=b: e.activation(out=so_stg[0:POOL_BUF, 0, g * 128:(g + 1) * 128], in_=psum[b][0:POOL_BUF, 0:128], func=AF.Copy),
                             reads=[psB[b]], writes=[soB[0]])
                        if g == 3:
                            P.op("sp", lambda e: e.dma_start(out=npp[l], in_=so_stg[0:POOL_BUF, 0, :]), reads=[soB[0]], chan=so_ch[0])
                    elif kind == "s":
                        b = next_bank()
                        P.op("pe", lambda e, b=b: e.transpose(out=psum[b][:, 0:128], in_=af(UD_OFF[es_], 128), identity=ident[:]),
                             reads=[UdB[es_], identB], writes=[psB[b]])
                        P.op("act", lambda e, b=b: e.activation(out=so_stg[:, 0, g * 128:(g + 1) * 128], in_=psum[b][:, 0:128], func=AF.Copy),
                             reads=[psB[b]], writes=[soB[0]])
                        if g == 3:
                            for s_ in range(NSEQ):
                                P.op("sp", lambda e, s_=s_: e.dma_start(out=nps[l, s_, POOL_BUF - DSEQ:POOL_BUF, :], in_=so_stg[s_ * DSEQ:(s_ + 1) * DSEQ, 0, :]),
                                     reads=[soB[0]], chan=so_ch[0])

                def pool_mm(g):
                    es_ = g % 2
                    dten = ab(D_OFF[es_], 512)
                    for il in range(ntl):
                        o, tw = lo(il)
                        b = mm_group([(poolw[:, g, :], poolwB)], [(dten[:, o:o + tw], DB[es_][il])], tw)
                        P.op("act", lambda e, b=b, o=o, tw=tw: e.activation(out=mix_t[:, g, o:o + tw], in_=psum[b][:, 0:tw], func=AF.Identity,
                                                                            scale=vcol(V_PSC + l * 4 + g)),
                             reads=[psB[b], vecB], writes=[mixB[g][il]])

                def cv_chunk(j):
                    cs_ = j % 2
                    coff = CVE_OFF[cs_]
                    if kind == "p":
                        if grp["first"]:
                            P.op("dve", lambda e: e.memset(af(coff, CONV_BUF), 0.0), writes=[ChB[cs_]])
                        else:
                            P.op("dve", lambda e: e.tensor_copy(out=af(coff, CONV_BUF), in_=carryCV[:, j, :]),
                                 reads=[carryCVB[j]], writes=[ChB[cs_]])
                    else:
                        P.op("dve", lambda e: e.tensor_copy(
                            out=af(coff, NSEQ * CR).rearrange("p (s r) -> p s r", r=CR)[:, :, 0:CONV_BUF],
                            in_=CVH[:, l, j, :].rearrange("p (s r) -> p s r", r=CONV_BUF)),
                            reads=[CVHB], writes=[ChB[cs_]])
                    for il in range(ntl):
                        o, tw = lo(il)
                        bc = inproj(S["c"], j, il)
                        ci = st["r"] % 2
                        st["r"] += 1
                        P.op("act", lambda e, bc=bc, ci=ci, tw=tw: e.activation(out=af(C_OFF[ci], tw), in_=psum[bc][:, 0:tw], func=AF.Copy),
                             reads=[psB[bc]], writes=[CsB[ci]])
                        bv = inproj(S["v"], j, il)
                        P.op("dve", lambda e, bv=bv, ci=ci, il=il: e.tensor_tensor(out=ctok(coff, il, CONV_BUF), in0=dense(af(C_OFF[ci], 512), il) if kind == "s" else af(C_OFF[ci], lo(il)[1]),
                                                                                   in1=pview(bv, il), op=ALU.mult),
                             reads=[CsB[ci], psB[bv]], writes=[CtB[cs_][il]])
                    if kind == "p" and not grp["last"]:
                        P.op("dve", lambda e: e.tensor_copy(out=carryCV[:, j, :], in_=af(coff + gw, CONV_BUF)),
                             reads=[CtB[cs_][ntl - 1]], writes=[carryCVB[j]])

                def conv_state(j):
                    cs_ = j % 2
                    coff = CVE_OFF[cs_]
                    if kind == "p" and grp["last"]:
                        b = next_bank()
                        P.op("pe", lambda e, b=b: e.transpose(out=psum[b][0:CONV_BUF, 0:128], in_=af(coff + gw, CONV_BUF), identity=ident[:]),
                             reads=[CtB[cs_][ntl - 1], identB], writes=[psB[b]])
                        P.op("act", lambda e, b=b: e.activation(out=so_stg[0:CONV_BUF, 1, j * 128:(j + 1) * 128], in_=psum[b][0:CONV_BUF, 0:128], func=AF.Copy),
                             reads=[psB[b]], writes=[soB[1]])
                        if j == 3:
                            P.op("sp", lambda e: e.dma_start(out=ncp[l], in_=so_stg[0:CONV_BUF, 1, :]), reads=[soB[1]], chan=so_ch[1])
                    elif kind == "s":
                        b = next_bank()
                        P.op("dve", lambda e: e.tensor_copy(out=af(CD_OFF, NSEQ * CONV_BUF).rearrange("p (s r) -> p s r", r=CONV_BUF),
                                                            in_=af(coff, NSEQ * CR).rearrange("p (s r) -> p s r", r=CR)[:, :, DSEQ:CR]),
                             reads=[CtB[cs_][0]], writes=[CdB])
                        P.op("pe", lambda e, b=b: e.transpose(out=psum[b][0:NSEQ * CONV_BUF, 0:128],
                                                              in_=af(CD_OFF, NSEQ * CONV_BUF), identity=ident[:]),
                             reads=[CdB, identB], writes=[psB[b]])
                        P.op("act", lambda e, b=b: e.activation(out=so_stg[0:NSEQ * CONV_BUF, 1, j * 128:(j + 1) * 128], in_=psum[b][0:NSEQ * CONV_BUF, 0:128], func=AF.Copy),
                             reads=[psB[b]], writes=[soB[1]])
                        if j == 3:
                            P.op("sp", lambda e: e.dma_start(out=ncs[l], in_=so_stg[0:NSEQ * CONV_BUF, 1, :]), reads=[soB[1]], chan=so_ch[1])

                def conv_z(j):
                    cs_ = j % 2
                    coff = CVE_OFF[cs_]
                    Creads = [ChB[cs_]] + [CtB[cs_][il] for il in range(ntl)]
                    n_ = CW - 2
                    cwb = V_CW + l * 12 + j
                    P.op("act", lambda e: e.activation(out=af(Z_OFF, n_), in_=af(coff + 2, n_), func=AF.Identity, scale=vcol(cwb + 8), bias=vcol(V_CB + l * 4 + j)),
                         reads=Creads + [vecB], writes=[ZB])
                    P.op("dve", lambda e: e.scalar_tensor_tensor(out=af(Z_OFF, n_), in0=af(coff + 1, n_), scalar=vcol(cwb + 4), in1=af(Z_OFF, n_),
                                                                 op0=ALU.mult, op1=ALU.add),
                         reads=Creads + [vecB, ZB], writes=[ZB])
                    P.op("dve", lambda e: e.scalar_tensor_tensor(out=af(Z_OFF, n_), in0=af(coff, n_), scalar=vcol(cwb), in1=af(Z_OFF, n_),
                                                                 op0=ALU.mult, op1=ALU.add),
                         reads=Creads + [vecB, ZB], writes=[ZB])

                def b_chunk(j):
                    bs_ = j % 2
                    for il in range(ntl):
                        o, tw = lo(il)
                        bb = inproj(S["b"], j, il)
                        P.op("act", lambda e, bb=bb, o=o, tw=tw: e.activation(out=af(BS_OFF[bs_] + o, tw), in_=psum[bb][:, 0:tw], func=AF.Copy),
                             reads=[psB[bb]], writes=[BsB[bs_][il]])

                def y_chunk(j):
                    bs_ = j % 2
                    for il in range(ntl):
                        P.op("dve", lambda e, il=il: e.tensor_tensor(out=dense(mix_t[:, 4 + j, :], il), in0=ctok(Z_OFF, il, 0), in1=dense(af(BS_OFF[bs_], 1024), il), op=ALU.mult),
                             reads=[ZB, BsB[bs_][il]], writes=[mixB[4 + j][il]])

                def g_body(hooks=(), extra=False):
                    hooks = list(hooks)

                    def hk():
                        if hooks:
                            hooks.pop(0)()
                    S["u"] = ws_take()
                    S["c"] = ws_take()
                    S["v"] = ws_take()
                    u_chunk(0)
                    hk()
                    u_chunk(1)
                    ws_release_half(S["u"], 0)
                    hk()
                    pool_ew(0)
                    pool_state(0)
                    u_chunk(2)
                    hk()
                    pool_ew(1)
                    pool_state(1)
                    u_chunk(3)
                    hk()
                    ws_release(S["u"])
                    S["b"] = ws_take()
                    pool_mm(0)
                    pool_ew(2)
                    pool_state(2)
                    for j in range(4):
                        cv_chunk(j)
                        if j == 1:
                            ws_release_half(S["c"], 0)
                            ws_release_half(S["v"], 0)
                        hk()
                        conv_state(j)
                        if j == 0:
                            pool_mm(1)
                            pool_ew(3)
                            pool_state(3)
                        conv_z(j)
                        b_chunk(j)
                        if j == 1:
                            ws_release_half(S["b"], 0)
                        if extra:
                            hk()
                        y_chunk(j)
                        if j >= 1 and j <= 2:
                            pool_mm(j + 1)
                    ws_release(S["c"])
                    ws_release(S["v"])
                    ws_release(S["b"])
                    while hooks:
                        hk()

                def g_outproj():
                    slot_o = [ws_take(), ws_take()]
                    for c in range(KT):
                        for il in range(ntl):
                            o, tw = lo(il)
                            ti = tl[il]
                            toff = TILES[ti][0]
                            so_ = slot_o[c // 4]
                            lhs = [wap(so_, c % 4, k) for k in range(KT)]
                            rhs = [(mix_t[:, k, o:o + tw], mixB[k][il]) for k in range(KT)]
                            b = mm_group(lhs, rhs, tw)
                            P.op("dve", lambda e, b=b, c=c, toff=toff, tw=tw: e.tensor_tensor(out=h[:, c, toff:toff + tw], in0=h[:, c, toff:toff + tw], in1=psum[b][:, 0:tw], op=ALU.add),
                                 reads=[psB[b]] + hb(c, toff, tw), writes=hb(c, toff, tw))
                        if c == 1:
                            ws_release_half(slot_o[0], 0)
                        if c == 3:
                            ws_release(slot_o[0])
                        if c == 5:
                            ws_release_half(slot_o[1], 0)
                    ws_release(slot_o[1])

                return g_steps, g_apply, g_body, g_outproj

            NPAIR = 9

            def p_geom(idx):
                if idx < 8:
                    return idx * 256, 2
                return SEQ, 1

            def p_dma(idx):
                col0, n = p_geom(idx)
                base = (idx % 2) * 2
                for s_ in range(n):
                    pi = base + s_
                    col = col0 + s_ * 128
                    src = pp[l, col:col + 128, :] if col < SEQ else psm[l]
                    P.op("sp", lambda e, pi=pi, src=src: e.dma_start(out=pstg[:, pi, :], in_=src), writes=[pstgB[pi]], chan=pl_ch[pi])

            def p_pair(idx):
                def f_():
                    col0, n = p_geom(idx)
                    base = (idx % 2) * 2
                    if idx + 1 < NPAIR:
                        p_dma(idx + 1)
                    w_ = n * 128
                    for kk in range(2):
                        b = next_bank()
                        for s_ in range(n):
                            pi = base + s_
                            P.op("pe", lambda e, b=b, s_=s_, pi=pi, kk=kk: e.transpose(out=psum[b][:, s_ * 128:(s_ + 1) * 128], in_=pstg[:, pi, kk * 128:(kk + 1) * 128], identity=ident[:]),
                                 reads=[pstgB[pi], identB], writes=[psB[b]], signal=(s_ == n - 1))
                        P.op("act", lambda e, b=b, kk=kk: e.activation(out=pT_t[:, kk, col0:col0 + w_], in_=psum[b][:, 0:w_], func=AF.Copy),
                             reads=[psB[b]], writes=ptb(kk, col0, w_))
                return f_

            mctx = [dict(), dict()]
            ms0 = norm_steps(0, mctx[0])
            ms1 = norm_steps(1, mctx[1])
            c_hooks = [ms0[0], p_pair(5), ms0[1], p_pair(6), ms0[2], p_pair(7), ms1[0], p_pair(8), ms1[1], ms1[2]]

            gs = [do_group(l, grp) for grp in GROUPS]
            if l == 0:
                for f_ in gs[0][0]():
                    f_()
                a_apply = gs[0][1]
            else:
                a_apply = pre_apply
            a_apply()
            if l == 0:
                gs[0][2](late_hooks + gs[1][0](), extra=True)
            else:
                gs[0][2](gs[1][0]())
            gs[1][1]()
            gs[0][3]()
            cs_ = gs[2][0]()
            p_dma(0)
            gs[1][2]([cs_[0], p_pair(0), cs_[1], p_pair(1), cs_[2], p_pair(2), p_pair(3), p_pair(4)])
            gs[2][1]()
            gs[1][3]()
            gs[2][2](c_hooks)
            gs[2][3]()

            P.stage_barrier()
            NTL = len(MT)
            if l + 1 < DEPTH:
                load_sample_state(l + 1)

            def mlp_norm(ti):
                toff, tw = MT[ti]
                norm_tile(ti, lambda k, toff=toff, tw=tw: m_t[:, k, toff:toff + tw], [mB[k][ti] for k in range(KT)], V_NMLP + l * 8, T=MT)

            def up_unit(su, c, ti):
                toff, tw = MT[ti]
                s_ = su[c // 4]
                lhs = [wap(s_, c % 4, k) for k in range(KT)]
                rhs = [(m_t[:, k, toff:toff + tw], mB[k][ti]) for k in range(KT)]
                b = mm_group(lhs, rhs, tw)
                ri = st["r"] % 3
                st["r"] += 1
                P.op("act", lambda e, b=b, ri=ri, tw=tw: e.activation(out=af(R_OFF[ri], tw), in_=psum[b][:, 0:tw], func=AF.Relu),
                     reads=[psB[b]], writes=[rB[ri]])
                P.op("dve", lambda e, ri=ri, c=c, toff=toff, tw=tw: e.tensor_tensor(out=f_t[:, c, toff:toff + tw], in0=af(R_OFF[ri], tw), in1=af(R_OFF[ri], tw), op=ALU.mult),
                     reads=[rB[ri]], writes=[fB[c][ti]])

            def dn_unit(sd, c, ti):
                toff, tw = MT[ti]
                s_ = sd[c // 4]
                lhs = [wap(s_, c % 4, k) for k in range(KT)]
                rhs = [(f_t[:, k, toff:toff + tw], fB[k][ti]) for k in range(KT)]
                b = mm_group(lhs, rhs, tw)
                P.op("dve", lambda e, b=b, c=c, toff=toff, tw=tw: e.tensor_tensor(out=h[:, c, toff:toff + tw], in0=h[:, c, toff:toff + tw], in1=psum[b][:, 0:tw], op=ALU.add),
                     reads=[psB[b]] + hb(c, toff, tw), writes=hb(c, toff, tw))

            def ple_norm(ti):
                toff, tw = MT[ti]
                norm_tile(ti, lambda k, toff=toff, tw=tw: n_t[:, k, toff:toff + tw], [nB[k][ti] for k in range(KT)], V_NPLE + l * 8, T=MT)

            for q in range(4):
                su = [ws_take(), ws_take()]
                if q == 0:
                    for t01 in range(2):
                        toff_, tw_ = MT[t01]
                        norm_apply(t01, mctx[t01]["n"], lambda k, toff_=toff_, tw_=tw_: m_t[:, k, toff_:toff_ + tw_], [mB[k][t01] for k in range(KT)], V_NMLP + l * 8, T=MT)
                    for ti in range(NTL):
                        nx = ti + 2
                        cx = {}
                        stp = norm_steps(nx, cx, T=MT) if nx < NTL else []
                        for c in range(KT):
                            up_unit(su, c, ti)
                            if stp and c in (0, 2, 4):
                                stp.pop(0)()
                                if c == 4:
                                    toff_, tw_ = MT[nx]
                                    norm_apply(nx, cx["n"], lambda k, toff_=toff_, tw_=tw_: m_t[:, k, toff_:toff_ + tw_], [mB[k][nx] for k in range(KT)], V_NMLP + l * 8, T=MT)
                    ws_release(su[0])
                    ws_release(su[1])
                else:
                    for c in range(KT):
                        for ti in range(NTL):
                            up_unit(su, c, ti)
                        if c == 1:
                            ws_release_half(su[0], 0)
                        if c == 3:
                            ws_release(su[0])
                        if c == 5:
                            ws_release_half(su[1], 0)
                    ws_release(su[1])
                sd = [ws_take(), ws_take()]
                if q < 3:
                    for c in range(KT):
                        for ti in range(NTL):
                            dn_unit(sd, c, ti)
                        if c == 1:
                            ws_release_half(sd[0], 0)
                        if c == 3:
                            ws_release(sd[0])
                        if c == 5:
                            ws_release_half(sd[1], 0)
                    ws_release(sd[1])
                else:
                    for ti in range(NTL):
                        for c in range(KT):
                            dn_unit(sd, c, ti)
                        if ti >= 1:
                            ple_norm(ti - 1)
                    ws_release(sd[0])
                    ws_release(sd[1])

            P.stage_barrier()
            sg = [ws_take(), ws_take()]

            def ple_unit(c, ti):
                toff, tw = MT[ti]
                s_ = sg[c // 4]
                lhs = [wap(s_, c % 4, k) for k in range(KT)]
                rhs = [(n_t[:, k, toff:toff + tw], nB[k][ti]) for k in range(KT)]
                b1 = mm_group(lhs, rhs, tw)
                lhs2 = [(wple[:, kk, c * 128:(c + 1) * 128], wpleB) for kk in range(2)]
                rhs2 = [(pT_t[:, kk, toff:toff + tw], ptb(kk, toff, tw)) for kk in range(2)]
                b2 = mm_group(lhs2, rhs2, tw)
                gi_ = st["g"] % 2
                st["g"] += 1
                P.op("act", lambda e, b1=b1, gi_=gi_, tw=tw: e.activation(out=af(G_OFF[gi_], tw), in_=psum[b1][:, 0:tw], func=AF.Sigmoid),
                     reads=[psB[b1]], writes=[gB[gi_]])
                P.op("dve", lambda e, b2=b2, gi_=gi_, tw=tw: e.tensor_tensor(out=af(TMP_OFF[gi_], tw), in0=af(G_OFF[gi_], tw), in1=psum[b2][:, 0:tw], op=ALU.mult),
                     reads=[gB[gi_], psB[b2]], writes=[tmpB[gi_]])
                P.op("dve", lambda e, gi_=gi_, c=c, toff=toff, tw=tw: e.tensor_tensor(out=h[:, c, toff:toff + tw], in0=h[:, c, toff:toff + tw], in1=af(TMP_OFF[gi_], tw), op=ALU.add),
                     reads=[tmpB[gi_]] + hb(c, toff, tw), writes=hb(c, toff, tw))

            def final_norm(ti):
                toff, tw = MT[ti]
                norm_tile(ti, lambda k, tw=tw: af(YFM_OFF + k * 512, tw), yfmB, V_NF, T=MT)

            def final_tile(ti):
                toff, tw = MT[ti]
                for s_ in range(tw // 128):
                    ti_ = st["y"] % 2
                    st["y"] += 1
                    for hf in range(2):
                        b = next_bank()
                        for kk in range(4):
                            k = hf * 4 + kk
                            P.op("pe", lambda e, b=b, kk=kk, k=k, s_=s_: e.transpose(out=psum[b][:, kk * 128:(kk + 1) * 128],
                                                                                    in_=af(YFM_OFF + k * 512 + s_ * 128, 128), identity=ident[:]),
                                 reads=[yfmB[k], identB], writes=[psB[b]], signal=(kk == 3))
                        if hf == 0:
                            P.op("act", lambda e, b=b, ti_=ti_: e.activation(out=af(YTM_OFF[ti_], 512), in_=psum[b][:], func=AF.Copy),
                                 reads=[psB[b]], writes=[ytmB[ti_]])
                        else:
                            P.op("dve", lambda e, b=b, ti_=ti_: e.tensor_copy(out=af(YTM_OFF[ti_] + 512, 512), in_=psum[b][:]),
                                 reads=[psB[b], ytmB[ti_]], writes=[ytmB[ti_]])
                    col_ = toff + s_ * 128
                    dst = y_p[col_:col_ + 128, :] if col_ < SEQ else y_s
                    P.op("sp", lambda e, ti_=ti_, dst=dst: e.dma_start(out=dst, in_=af(YTM_OFF[ti_], 1024)), reads=[ytmB[ti_]], chan=st_ch[ti_])

            last = (l == DEPTH - 1)
            nhooks = []
            if not last:
                nA = do_group(l + 1, GROUPS[0])
                nhooks = nA[0]()
                pre_apply = nA[1]
            lctx = {}
            lsteps = norm_steps(NTL - 1, lctx, T=MT)
            for ti in range(NTL):
                fsteps, fctx = [], {}
                if last and ti >= 1:
                    fsteps = norm_steps(ti - 1, fctx, T=MT)
                for c in range(KT):
                    ple_unit(c, ti)
                    if ti == 0 and c in (0, 2, 4):
                        lsteps.pop(0)()
                        if c == 4:
                            toff_, tw_ = MT[NTL - 1]
                            norm_apply(NTL - 1, lctx["n"], lambda k, toff_=toff_, tw_=tw_: n_t[:, k, toff_:toff_ + tw_], [nB[k][NTL - 1] for k in range(KT)], V_NPLE + l * 8, T=MT)
                    if nhooks and ti >= 2 and c in (1, 3, 5):
                        nhooks.pop(0)()
                    if fsteps and c in (0, 2, 4):
                        fsteps.pop(0)()
                        if c == 4:
                            tw_ = MT[ti - 1][1]
                            norm_apply(ti - 1, fctx["n"], lambda k, tw_=tw_: af(YFM_OFF + k * 512, tw_), yfmB, V_NF, T=MT)
                if last and ti >= 1:
                    final_tile(ti - 1)
            ws_release(sg[0])
            ws_release(sg[1])
            while nhooks:
                nhooks.pop(0)()
            if last:
                final_norm(NTL - 1)
                final_tile(NTL - 1)

        assert ws["taken"] == len(wq) and ws["issued"] == len(wq), (ws["taken"], ws["issued"], len(wq))

        final_waits = [(c, c.count) for c in st_ch + so_ch + [d2d_ch] if c.count > 0]

        all_ch = list(P.engs.values()) + P.chans
        for c in all_ch:
            c.sem = es.enter_context(nc.semaphore(c.name))
        block = es.enter_context(nc.Block())

        def replay(name, e, tail=()):
            for waits, fn, sig in P.progs[name]:
                for c, v in waits:
                    e.wait_ge(c.sem, v)
                ins = fn(e)
                if sig is not None:
                    ins.then_inc(sig[0].sem, sig[1])
            for c, v in tail:
                e.wait_ge(c.sem, v)

        @block.tensor
        def _(e):
            replay("pe", e)

        @block.scalar
        def _(e):
            replay("act", e)

        @block.vector
        def _(e):
            replay("dve", e)

        @block.gpsimd
        def _(e):
            replay("pool", e)

        @block.sync
        def _(e):
            replay("sp", e, tail=final_waits)

    return nc


_NC = None


def kernel(x_prompt, x_sample, state_pool, state_conv, p_prompt, p_sample,
           norm_mix, w_in, pool_w, pool_scale, conv_w, conv_b, w_out,
           norm_mlp, w_up, w_down, norm_ple, w_ple_gate, w_ple_proj, norm_f):
    global _NC
    f = lambda a: np.ascontiguousarray(np.asarray(a, dtype=np.float32))
    if _NC is None:
        _NC = build()
    nc = _NC
    vecs = np.concatenate([
        f(norm_mix).reshape(16, 128), f(norm_mlp).reshape(16, 128), f(norm_ple).reshape(16, 128),
        f(norm_f).reshape(8, 128), f(pool_scale).reshape(8, 128), f(conv_w).reshape(24, 128),
        f(conv_b).reshape(8, 128)], axis=0)
    shared = dict(vecs=f(vecs), w_in=f(w_in), pool_w=f(pool_w), w_out=f(w_out), w_up=f(w_up),
                  w_down=f(w_down), w_gate=f(w_ple_gate), w_ple=f(w_ple_proj))
    x_prompt, x_sample = f(x_prompt), f(x_sample)
    state_pool, state_conv = f(state_pool), f(state_conv)
    p_prompt, p_sample = f(p_prompt), f(p_sample)
    in_maps = []
    for c in range(NCORES):
        ss = slice(c * NSEQ, (c + 1) * NSEQ)
        m = dict(shared)
        m["x_p"] = x_prompt[c]
        m["x_s"] = f(x_sample[ss].reshape(NS, D))
        m["sp_in"] = f(state_pool[:, ss].reshape(DEPTH, NSEQ * POOL_BUF, 512))
        m["sc_in"] = f(state_conv[:, ss].reshape(DEPTH, NSEQ * CONV_BUF, 512))
        m["pp"] = f(p_prompt[:, c])
        m["psm"] = f(p_sample[:, ss].reshape(DEPTH, NS, PLE))
        in_maps.append(m)
    res = run_bass_kernel_spmd(nc, in_maps, core_ids=list(range(NCORES)))
    R = res.results
    y_prompt = np.stack([R[c]["y_p"] for c in range(NCORES)], axis=0)
    y_sample = np.concatenate([R[c]["y_s"].reshape(NSEQ, DSEQ, D) for c in range(NCORES)], axis=0)
    new_pool_prompt = np.stack([R[c]["npp"] for c in range(NCORES)], axis=1)
    new_conv_prompt = np.stack([R[c]["ncp"] for c in range(NCORES)], axis=1)
    new_pool_sample = np.concatenate([R[c]["nps"] for c in range(NCORES)], axis=1)
    new_conv_sample = np.concatenate([R[c]["ncs"].reshape(DEPTH, NSEQ, CONV_BUF, 512) for c in range(NCORES)], axis=1)
    return (y_prompt.astype(np.float32), y_sample.astype(np.float32), new_pool_prompt.astype(np.float32),
            new_conv_prompt.astype(np.float32), new_pool_sample.astype(np.float32), new_conv_sample.astype(np.float32))
```

```python
import numpy as np
from contextlib import ExitStack
import concourse.bass as bass
import concourse.mybir as mybir
from concourse.bass_utils import run_bass_kernel_spmd

F32 = mybir.dt.float32
BF16 = mybir.dt.bfloat16
AF = mybir.ActivationFunctionType
ALU = mybir.AluOpType

NCORES = 8
D = 1024
KT = 8
SEQ = 2048
NS = 128
NSEQ = 16
DSEQ = 8
NTOK = SEQ + NS
DEPTH = 2
DFF = 4096
PLE = 256
EPS = 1e-6
POOL_BUF = 15
CONV_BUF = 2
ER = POOL_BUF + DSEQ
CR = CONV_BUF + DSEQ

TILES = [(0, 512), (512, 512), (1024, 512), (1536, 512), (2048, 128)]
MT = [(0, 512), (512, 512), (1024, 512), (1536, 384), (1920, 256)]
NSUB = NTOK // 128
GROUPS = [
    dict(tiles=[0, 1], kind="p", first=True, last=False),
    dict(tiles=[2, 3], kind="p", first=False, last=True),
    dict(tiles=[4], kind="s", first=False, last=False),
]

V_NMIX, V_NMLP, V_NPLE, V_NF, V_PSC, V_CW, V_CB = 0, 16, 32, 48, 56, 64, 88
NVEC = 96


class Ch:
    def __init__(self, name):
        self.name = name
        self.count = 0
        self.sem = None


class Buf:
    __slots__ = ("w", "r", "arena")

    def __init__(self, arena=False):
        self.w = {}
        self.r = {}
        self.arena = arena


def _merge(dst, src):
    for c, v in src.items():
        if dst.get(c, 0) < v:
            dst[c] = v


class Prog:
    ENGS = ["pe", "act", "dve", "pool", "sp"]

    def __init__(self):
        self.engs = {n: Ch(n) for n in self.ENGS}
        self.progs = {n: [] for n in self.ENGS}
        self.waited = {n: {} for n in self.ENGS}
        self.chans = []
        self.barrier = {}
        self.arena_bufs = []

    def chan(self, name):
        c = Ch(name)
        self.chans.append(c)
        return c

    def abuf(self):
        b = Buf(arena=True)
        self.arena_bufs.append(b)
        return b

    def stage_barrier(self):
        for b in self.arena_bufs:
            _merge(self.barrier, b.w)
            _merge(self.barrier, b.r)

    def op(self, eng, fn, reads=(), writes=(), signal=True, chan=None):
        need = {}
        for b in reads:
            _merge(need, b.w)
        for b in writes:
            _merge(need, b.w)
            _merge(need, b.r)
            if b.arena:
                _merge(need, self.barrier)
        waits = []
        wd = self.waited[eng]
        pe_ch = self.engs["pe"]
        for c, v in need.items():
            if eng == "pe" and c is pe_ch:
                continue
            if wd.get(c, 0) < v:
                waits.append((c, v))
                wd[c] = v
        if chan is not None:
            chan.count += 16
            tok = {chan: chan.count}
            sig = (chan, 16)
        else:
            ch = self.engs[eng]
            tok = {ch: ch.count + 1}
            if signal:
                ch.count += 1
                sig = (ch, 1)
            else:
                sig = None
        self.progs[eng].append((waits, fn, sig))
        for b in reads:
            _merge(b.r, tok)
        for b in writes:
            b.w = dict(tok)
            b.r = {}
        return tok


def build():
    nc = bass.Bass("TRN2", target_bir_lowering=False)
    dt_in = lambda name, shape: nc.dram_tensor(name, shape, F32, kind="ExternalInput").ap()
    dt_out = lambda name, shape: nc.dram_tensor(name, shape, F32, kind="ExternalOutput").ap()
    x_p = dt_in("x_p", [SEQ, D])
    x_s = dt_in("x_s", [NS, D])
    sp_in = dt_in("sp_in", [DEPTH, NSEQ * POOL_BUF, 512])
    sc_in = dt_in("sc_in", [DEPTH, NSEQ * CONV_BUF, 512])
    pp = dt_in("pp", [DEPTH, SEQ, PLE])
    psm = dt_in("psm", [DEPTH, NS, PLE])
    vecs = dt_in("vecs", [NVEC, 128])
    w_in = dt_in("w_in", [DEPTH, D, 2048])
    pool_w = dt_in("pool_w", [DEPTH, 4, 128, 128])
    w_out = dt_in("w_out", [DEPTH, D, D])
    w_up = dt_in("w_up", [DEPTH, D, DFF])
    w_down = dt_in("w_down", [DEPTH, DFF, D])
    w_gate = dt_in("w_gate", [DEPTH, D, D])
    w_ple = dt_in("w_ple", [DEPTH, PLE, D])
    y_p = dt_out("y_p", [SEQ, D])
    y_s = dt_out("y_s", [NS, D])
    npp = dt_out("npp", [DEPTH, POOL_BUF, 512])
    ncp = dt_out("ncp", [DEPTH, CONV_BUF, 512])
    nps = dt_out("nps", [DEPTH, NSEQ, POOL_BUF, 512])
    ncs = dt_out("ncs", [DEPTH, NSEQ * CONV_BUF, 512])

    P = Prog()
    AW = 19904
    es = ExitStack()
    with es:
        sb = lambda name, shape, dt: es.enter_context(nc.sbuf_tensor(name, shape, dt))
        h = sb("h", [128, KT, NTOK], F32)
        ring = sb("ring", [128, 6, 8, 256], BF16)
        pT_t = sb("pT", [128, 2, NTOK], BF16)
        wple = sb("wple", [128, 2, D], BF16)
        poolw = sb("poolw", [128, 4, 128], BF16)
        sq = sb("sq", [128, 4, 512], BF16)
        rstd = sb("rstd", [128, 2, 512], F32)
        EH = sb("EH", [128, 4, NSEQ * POOL_BUF], F32)
        CVH = sb("CVH", [128, DEPTH, 4, NSEQ * CONV_BUF], F32)
        carryE = sb("carryE", [128, 4, 16], F32)
        carryCV = sb("carryCV", [128, 4, 2], F32)
        vec = sb("vec", [128, NVEC], F32)
        ident = sb("ident", [128, 128], F32)
        ones = sb("ones", [128, 128], BF16)
        invcnt = sb("invcnt", [128, 16], F32)
        so_stg = sb("so_stg", [128, 2, 512], F32)
        pstg = sb("pstg", [128, 4, PLE], F32)
        mixc = sb("mixc", [128, KT, NS], BF16)
        arena = sb("arena", [128, AW], F32)
        psum = [es.enter_context(nc.psum_tensor(f"ps{i}", [128, 512], F32)) for i in range(8)]

        def af(off, n):
            return arena[:, off:off + n]

        def ab(off, nwords):
            return arena[:, off:off + nwords].bitcast(BF16)

        hB = [[Buf() for _ in range(NSUB)] for _ in range(KT)]

        def hb(k, toff, tw):
            return hB[k][toff // 128:(toff + tw) // 128]

        psB = [Buf() for _ in range(8)]
        ringB = [Buf() for _ in range(6)]
        wpleB, poolwB = Buf(), Buf()
        sqB = [Buf() for _ in range(4)]
        rstdB = [Buf(), Buf()]
        EHB, CVHB, vecB, identB, onesB, invB = Buf(), Buf(), Buf(), Buf(), Buf(), Buf()
        carryEB = [Buf() for _ in range(4)]
        carryCVB = [Buf() for _ in range(4)]
        soB = [Buf(), Buf()]
        pstgB = [Buf() for _ in range(4)]
        pTB = [[Buf() for _ in range(NSUB)] for _ in range(2)]

        def ptb(kk, toff, tw):
            return pTB[kk][toff // 128:(toff + tw) // 128]


        st = dict(bank=0, sq=0, norm=0, r=0, g=0, pst=0, y=0)

        reserved = set()

        def next_bank():
            while True:
                b = st["bank"]
                st["bank"] = (b + 1) % 8
                if b not in reserved:
                    return b

        ring_ch = [P.chan(f"ring{i}") for i in range(6)]
        poolw_ch = P.chan("poolw")
        wple_ch = P.chan("wple")
        setup_ch = [P.chan(f"su{i}") for i in range(3)]
        sui = iter(setup_ch)
        ld_ch = [P.chan(f"ld{i}") for i in range(8)]
        pl_ch = [P.chan(f"pl{i}") for i in range(4)]
        st_ch = [P.chan(f"st{i}") for i in range(2)]
        so_ch = [P.chan(f"so{i}") for i in range(2)]
        d2d_ch = P.chan("d2d")

        def vcol(c):
            return vec[:, c:c + 1]

        def wview(w_l, c0):
            v = w_l.rearrange("(k p) n -> p k n", p=128)
            return [v[:, :, c0:c0 + 256], v[:, :, c0 + 256:c0 + 512]]

        wq = []
        for l in range(DEPTH):
            for _g in GROUPS:
                wq += wview(w_in[l], 0) + wview(w_in[l], 1024) + wview(w_in[l], 1536) + wview(w_in[l], 512) \
                    + wview(w_out[l], 0) + wview(w_out[l], 512)
            for q in range(4):
                wq += wview(w_up[l], q * 1024) + wview(w_up[l], q * 1024 + 512) \
                    + wview(w_down[l][q * 1024:(q + 1) * 1024, :], 0) + wview(w_down[l][q * 1024:(q + 1) * 1024, :], 512)
            wq += wview(w_gate[l], 0) + wview(w_gate[l], 512)
        ws = dict(issued=0, taken=0, free=[True] * 6, gate=(), slot={})

        def ws_pump():
            while ws["issued"] < len(wq) and any(ws["free"]):
                i = ws["issued"]
                s_ = ws["free"].index(True)
                ws["slot"][i] = s_
                P.op("pool", lambda e, s_=s_, src=wq[i]: e.dma_start(out=ring[:, s_], in_=src), reads=(ws["gate"] if i in (2, 3, 4, 5) else ()),
                     writes=[ringB[s_]], chan=ring_ch[s_])
                ws["free"][s_] = False
                ws["issued"] += 1

        def ws_take():
            i = ws["taken"]
            ws["taken"] += 2
            assert i + 1 < ws["issued"], "weight block not yet issued"
            return dict(s=[ws["slot"][i], ws["slot"][i + 1]], rel=[False, False])

        def ws_release_half(blk, hh):
            if not blk["rel"][hh]:
                blk["rel"][hh] = True
                ws["free"][blk["s"][hh]] = True
                ws_pump()

        def ws_release(blk):
            ws_release_half(blk, 0)
            ws_release_half(blk, 1)

        def wap(blk, c4, k):
            s_ = blk["s"][c4 // 2]
            return (ring[:, s_, k, (c4 % 2) * 128:(c4 % 2 + 1) * 128], ringB[s_])

        P.op("pool", lambda e: e.memset(ident[:], 0.0), writes=[identB])
        P.op("pool", lambda e: e.affine_select(out=ident[:], in_=ident[:], pattern=[[-1, 128]],
                                               compare_op=ALU.not_equal, fill=1.0, base=0,
                                               channel_multiplier=1), writes=[identB])
        P.op("pool", lambda e: e.memset(ones[:], 1.0), writes=[onesB])
        for t in range(16):
            P.op("pool", lambda e, t=t: e.memset(invcnt[:, t:t + 1], 1.0 / (t + 1)), writes=[invB])

        XSTG = 8192
        SPSTG = 16384
        vstg = pstg[:, 0, 0:128]
        vstgB = pstgB[0]
        P.op("sp", lambda e: e.dma_start(out=vstg[0:NVEC, :], in_=vecs), writes=[vstgB], chan=next(sui))
        b = next_bank()
        P.op("pe", lambda e, b=b: e.transpose(out=psum[b][:, 0:NVEC], in_=vstg[0:NVEC, :], identity=ident[0:NVEC, 0:NVEC]),
             reads=[vstgB, identB], writes=[psB[b]])
        P.op("act", lambda e, b=b: e.activation(out=vec[:], in_=psum[b][:, 0:NVEC], func=AF.Copy),
             reads=[psB[b]], writes=[vecB])

        pstg2 = pstg[:].rearrange("p a c -> p (a c)").rearrange("p (h c) -> p h c", h=2)

        def load_sample_state(l):
            for hf in range(2):
                stg = pstg2[:, hf, :]
                sBs = [pstgB[2 * hf], pstgB[2 * hf + 1]]
                P.op("sp", lambda e, stg=stg, hf=hf: e.dma_start(out=stg[0:120, :], in_=sp_in[l, hf * 120:(hf + 1) * 120, :]),
                     writes=sBs, chan=pl_ch[2 * hf])
                b = next_bank()
                for g in range(4):
                    P.op("pe", lambda e, b=b, g=g, stg=stg: e.transpose(out=psum[b][:, g * 120:(g + 1) * 120], in_=stg[0:120, g * 128:(g + 1) * 128],
                                                                        identity=ident[0:120, 0:120]),
                         reads=sBs + [identB], writes=[psB[b]], signal=(g == 3))
                P.op("act", lambda e, b=b, hf=hf: e.activation(out=EH[:, :, hf * 120:(hf + 1) * 120],
                                                               in_=psum[b][:, 0:480].rearrange("p (g c) -> p g c", g=4), func=AF.Copy),
                     reads=[psB[b]], writes=[EHB])

        load_sample_state(0)
        for l in range(DEPTH):
            stg = so_stg[:, l, :]
            sB_ = soB[l]
            P.op("sp", lambda e, stg=stg, l=l: e.dma_start(out=stg[0:32, :], in_=sc_in[l]), writes=[sB_], chan=next(sui))
            b = next_bank()
            for j in range(4):
                P.op("pe", lambda e, b=b, j=j, stg=stg: e.transpose(out=psum[b][:, j * 32:(j + 1) * 32], in_=stg[0:32, j * 128:(j + 1) * 128],
                                                                    identity=ident[0:32, 0:32]),
                     reads=[sB_, identB], writes=[psB[b]], signal=(j == 3))
            P.op("act", lambda e, b=b, l=l: e.activation(out=CVH[:, l], in_=psum[b][:, 0:128].rearrange("p (g c) -> p g c", g=4), func=AF.Copy),
                 reads=[psB[b]], writes=[CVHB])
            P.op("sp", lambda e, l=l: e.dma_start(out=nps[l, :, 0:POOL_BUF - DSEQ, :],
                                                  in_=sp_in[l].rearrange("(s r) c -> s r c", r=POOL_BUF)[:, DSEQ:POOL_BUF, :]),
                 chan=d2d_ch)

        xstgB = [P.abuf() for _ in range(8)]
        xstg_all = [af(XSTG + i * 1024, 1024) for i in range(8)]
        for ti, (toff, tw) in enumerate(TILES[0:2]):
            nsub = tw // 128
            xo = (ti % 2) * 4
            xstg = xstg_all[xo:xo + 4]
            xsB = xstgB[xo:xo + 4]
            for s_ in range(nsub):
                src = x_p[toff + s_ * 128: toff + (s_ + 1) * 128, :] if toff < SEQ else x_s
                P.op("sp", lambda e, s_=s_, src=src, xstg=xstg: e.dma_start(out=xstg[s_], in_=src), writes=[xsB[s_]], chan=ld_ch[xo + s_])
            if nsub == 4:
                for k in range(KT):
                    b = next_bank()
                    for s_ in range(4):
                        P.op("pe", lambda e, b=b, s_=s_, k=k, xstg=xstg: e.transpose(out=psum[b][:, s_ * 128:(s_ + 1) * 128], in_=xstg[s_][:, k * 128:(k + 1) * 128], identity=ident[:]),
                             reads=[xsB[s_], identB], writes=[psB[b]], signal=(s_ == 3))
                    if ti == 1 and k == 0:
                        gateB = Buf()
                        gateB.w = dict(psB[b].w)
                        ws["gate"] = (gateB,)
                    if k % 2 == 0:
                        P.op("act", lambda e, b=b, k=k, toff=toff: e.activation(out=h[:, k, toff:toff + 512], in_=psum[b][:], func=AF.Copy),
                             reads=[psB[b]], writes=hb(k, toff, 512))
                    else:
                        P.op("dve", lambda e, b=b, k=k, toff=toff: e.tensor_copy(out=h[:, k, toff:toff + 512], in_=psum[b][:]),
                             reads=[psB[b]], writes=hb(k, toff, 512))
            else:
                for kh in range(2):
                    b = next_bank()
                    for kk in range(4):
                        k = kh * 4 + kk
                        P.op("pe", lambda e, b=b, kk=kk, k=k, xstg=xstg: e.transpose(out=psum[b][:, kk * 128:(kk + 1) * 128], in_=xstg[0][:, k * 128:(k + 1) * 128], identity=ident[:]),
                             reads=[xsB[0], identB], writes=[psB[b]], signal=(kk == 3))
                    P.op("act", lambda e, b=b, kh=kh, toff=toff: e.activation(out=h[:, kh * 4:(kh + 1) * 4, toff:toff + 128],
                                                                              in_=psum[b][:].rearrange("p (k c) -> p k c", k=4), func=AF.Copy),
                         reads=[psB[b]], writes=[hb(kh * 4 + kk, toff, 128)[0] for kk in range(4)])

        ws_pump()

        xl = [pT_t[:, j, :].bitcast(F32)[:, 0:1024] for j in range(2)]
        xl_ch = [P.chan(f"xl{j}") for j in range(2)]
        late_list = []
        for ti in range(2, len(TILES)):
            toff, tw = TILES[ti]
            for s_ in range(tw // 128):
                src = x_p[toff + s_ * 128: toff + (s_ + 1) * 128, :] if toff < SEQ else x_s
                late_list.append((ti, toff + s_ * 128, src))

        def late_dma(n):
            ti, col0, src = late_list[n]
            j = n % 2
            P.op("sp", lambda e, j=j, src=src: e.dma_start(out=xl[j], in_=src), writes=list(pTB[j]), chan=xl_ch[j])

        def late_step(n):
            def f_():
                ti, col0, src = late_list[n]
                j = n % 2
                if n + 1 < len(late_list):
                    late_dma(n + 1)
                for hf in range(2):
                    b = next_bank()
                    for kk in range(4):
                        k = hf * 4 + kk
                        P.op("pe", lambda e, b=b, kk=kk, k=k: e.transpose(out=psum[b][:, kk * 128:(kk + 1) * 128], in_=xl[j][:, k * 128:(k + 1) * 128], identity=ident[:]),
                             reads=list(pTB[j]) + [identB], writes=[psB[b]], signal=(kk == 3))
                    P.op("act", lambda e, b=b, hf=hf: e.activation(out=h[:, hf * 4:(hf + 1) * 4, col0:col0 + 128],
                                                                   in_=psum[b][:].rearrange("p (k c) -> p k c", k=4), func=AF.Copy),
                         reads=[psB[b]], writes=[hb(hf * 4 + kk, col0, 128)[0] for kk in range(4)])
            return f_

        late_dma(0)
        late_hooks = [late_step(n) for n in range(len(late_list))]

        def norm_tile(ti, dst_ap_fn, dstB, gbase, T=TILES):
            toff, tw = T[ti]
            b = next_bank()
            for k in range(KT):
                q = st["sq"]
                st["sq"] = (q + 1) % 4
                P.op("act", lambda e, q=q, k=k: e.activation(out=sq[:, q, 0:tw], in_=h[:, k, toff:toff + tw], func=AF.Square),
                     reads=hb(k, toff, tw), writes=[sqB[q]])
                P.op("pe", lambda e, q=q, k=k, b=b: e.matmul(psum[b][:, 0:tw], lhsT=ones[:], rhs=sq[:, q, 0:tw], start=(k == 0), stop=(k == KT - 1)),
                     reads=[sqB[q], onesB], writes=[psB[b]], signal=True)
            n_ = st["norm"]
            st["norm"] = 1 - n_
            P.op("act", lambda e, b=b, n_=n_: e.activation(out=rstd[:, n_, 0:tw], in_=psum[b][:, 0:tw], func=AF.Ln, scale=1.0 / D, bias=EPS),
                 reads=[psB[b]], writes=[rstdB[n_]])
            P.op("act", lambda e, n_=n_: e.activation(out=rstd[:, n_, 0:tw], in_=rstd[:, n_, 0:tw], func=AF.Exp, scale=-0.5),
                 reads=[rstdB[n_]], writes=[rstdB[n_]])
            for k in range(KT):
                P.op("dve", lambda e, k=k, n_=n_: e.scalar_tensor_tensor(out=dst_ap_fn(k), in0=h[:, k, toff:toff + tw], scalar=vcol(gbase + k),
                                                                         in1=rstd[:, n_, 0:tw], op0=ALU.mult, op1=ALU.mult),
                     reads=hb(k, toff, tw) + [rstdB[n_], vecB], writes=[dstB[k]])

        def norm_steps(ti, ctx, T=TILES):
            toff, tw = T[ti]

            def squares(hf):
                for j in range(4):
                    k = hf * 4 + j
                    P.op("act", lambda e, j=j, k=k: e.activation(out=sq[:, j, 0:tw], in_=h[:, k, toff:toff + tw], func=AF.Square),
                         reads=hb(k, toff, tw), writes=[sqB[j]])

            def stats(hf):
                b = ctx["b"]
                for j in range(4):
                    k = hf * 4 + j
                    P.op("pe", lambda e, j=j, k=k, b=b: e.matmul(psum[b][:, 0:tw], lhsT=ones[:], rhs=sq[:, j, 0:tw], start=(k == 0), stop=(k == KT - 1)),
                         reads=[sqB[j], onesB], writes=[psB[b]], signal=True)

            def s1():
                ctx["b"] = next_bank()
                reserved.add(ctx["b"])
                squares(0)

            def s2():
                stats(0)
                squares(1)

            def s3():
                stats(1)
                b = ctx["b"]
                n_ = st["norm"]
                st["norm"] = 1 - n_
                ctx["n"] = n_
                P.op("act", lambda e: e.activation(out=rstd[:, n_, 0:tw], in_=psum[b][:, 0:tw], func=AF.Ln, scale=1.0 / D, bias=EPS),
                     reads=[psB[b]], writes=[rstdB[n_]])
                P.op("act", lambda e: e.activation(out=rstd[:, n_, 0:tw], in_=rstd[:, n_, 0:tw], func=AF.Exp, scale=-0.5),
                     reads=[rstdB[n_]], writes=[rstdB[n_]])
                reserved.discard(b)

            return [s1, s2, s3]

        def norm_apply(ti, n_, dst_ap_fn, dstB, gbase, T=TILES):
            toff, tw = T[ti]
            for k in range(KT):
                P.op("dve", lambda e, k=k: e.scalar_tensor_tensor(out=dst_ap_fn(k), in0=h[:, k, toff:toff + tw], scalar=vcol(gbase + k),
                                                                  in1=rstd[:, n_, 0:tw], op0=ALU.mult, op1=ALU.mult),
                     reads=hb(k, toff, tw) + [rstdB[n_], vecB], writes=[dstB[k]])

        def mm_group(lhs_list, rhs_list, width):
            b = next_bank()
            n = len(lhs_list)
            for i in range(n):
                (lap, lB), (rap, rB) = lhs_list[i], rhs_list[i]
                P.op("pe", lambda e, b=b, lap=lap, rap=rap, i=i: e.matmul(psum[b][:, 0:width], lhsT=lap, rhs=rap, start=(i == 0), stop=(i == n - 1)),
                     reads=(lB if isinstance(lB, list) else [lB]) + (rB if isinstance(rB, list) else [rB]), writes=[psB[b]], signal=(i == n - 1))
            return b

        A_OFF, MIX_OFF = 0, 4096
        E_OFF = [8192, 9232]
        T_OFF = [10272, 11312]
        D_OFF = [12352, 12864]
        CVE_OFF = [13376, 14404]
        Z_OFF = 15432
        C_OFF = [16460, 16972]
        FIX_OFF = 17484
        BS_OFF = [17800, 18824]
        UD_OFF = [17500, 17628]
        CD_OFF = 17756
        a_t = ab(A_OFF, 4096).rearrange("p (k n) -> p k n", k=KT)
        mix_t = ab(MIX_OFF, 4096).rearrange("p (k n) -> p k n", k=KT)
        aB = [[P.abuf() for _ in range(2)] for _ in range(KT)]
        mixB = [[P.abuf() for _ in range(2)] for _ in range(KT)]
        mixcB = [[Buf()] for _ in range(KT)]
        EtB = [[P.abuf() for _ in range(2)] for _ in range(2)]
        EhB = [P.abuf() for _ in range(2)]
        TB = [P.abuf() for _ in range(2)]
        DB = [[P.abuf() for _ in range(2)] for _ in range(2)]
        CtB = [[P.abuf() for _ in range(2)] for _ in range(2)]
        ChB = [P.abuf() for _ in range(2)]
        ZB = P.abuf()
        CsB = [P.abuf() for _ in range(2)]
        fixB = P.abuf()
        BsB = [[P.abuf() for _ in range(2)] for _ in range(2)]
        UdB = [P.abuf(), P.abuf()]
        CdB = P.abuf()
        M_OFF, F_OFF = 0, 8704
        R_OFF = [17408, 17920, 18432]
        m_t = ab(M_OFF, 8704).rearrange("p (k n) -> p k n", k=KT)
        f_t = ab(F_OFF, 8704).rearrange("p (k n) -> p k n", k=KT)
        mB = [[P.abuf() for _ in TILES] for _ in range(KT)]
        fB = [[P.abuf() for _ in TILES] for _ in range(KT)]
        rB = [P.abuf() for _ in range(3)]
        G_OFF = [8704, 9216]
        TMP_OFF = [9728, 10240]
        nB = mB
        n_t = m_t
        gB = [P.abuf() for _ in range(2)]
        tmpB = [P.abuf() for _ in range(2)]
        YFM_OFF = 10752
        YTM_OFF = [14848, 15872]
        yfmB = [P.abuf() for _ in range(KT)]
        ytmB = [P.abuf() for _ in range(2)]

        for l in range(DEPTH):
            P.stage_barrier()
            P.op("pool", lambda e, l=l: e.dma_start(out=poolw[:], in_=pool_w[l].rearrange("g c d -> c g d")), writes=[poolwB], chan=poolw_ch)
            P.op("pool", lambda e, l=l: e.dma_start(out=wple[:], in_=w_ple[l].rearrange("(k p) n -> p k n", p=128)), writes=[wpleB], chan=wple_ch)
            def do_group(l, grp):
                kind = grp["kind"]
                tl = grp["tiles"]
                ntl = len(tl)
                g0 = TILES[tl[0]][0]
                gw = sum(TILES[t][1] for t in tl)
                if kind == "p":
                    EW, CW = POOL_BUF + gw, CONV_BUF + gw
                else:
                    EW, CW = NSEQ * ER, NSEQ * CR

                def lo(il):
                    toff, tw = TILES[tl[il]]
                    return toff - g0, tw

                def etok(off, il):
                    o, tw = lo(il)
                    if kind == "p":
                        return af(off + POOL_BUF + o, tw)
                    return af(off, NSEQ * ER).rearrange("p (s r) -> p s r", r=ER)[:, :, POOL_BUF:ER]

                def ctok(off, il, hist):
                    o, tw = lo(il)
                    if kind == "p":
                        return af(off + hist + o, tw)
                    return af(off, NSEQ * CR).rearrange("p (s r) -> p s r", r=CR)[:, :, hist:hist + DSEQ]

                def pview(b, il):
                    o, tw = lo(il)
                    if kind == "p":
                        return psum[b][:, 0:tw]
                    return psum[b][:, 0:tw].rearrange("p (s t) -> p s t", t=DSEQ)

                def dense(ap2, il):
                    o, tw = lo(il)
                    v = ap2[:, o:o + tw]
                    if kind == "s":
                        return v.rearrange("p (s t) -> p s t", t=DSEQ)
                    return v

                S = {}
                mix_g, mixB_g = (mixc, mixcB) if kind == "s" else (mix_t, mixB)

                nctx = [dict() for _ in tl]

                def g_steps():
                    out = []
                    for il, ti in enumerate(tl):
                        out += norm_steps(ti, nctx[il])
                    return out

                def g_apply():
                    for il, ti in enumerate(tl):
                        o, tw = lo(il)
                        norm_apply(ti, nctx[il]["n"], lambda k, o=o, tw=tw: a_t[:, k, o:o + tw], [aB[k][il] for k in range(KT)], V_NMIX + l * 8)

                def inproj(slot, c, il):
                    o, tw = lo(il)
                    lhs = [wap(slot, c, k) for k in range(KT)]
                    rhs = [(a_t[:, k, o:o + tw], aB[k][il]) for k in range(KT)]
                    return mm_group(lhs, rhs, tw)

                def u_chunk(g):
                    es_ = g % 2
                    eoff = E_OFF[es_]
                    if kind == "p":
                        if grp["first"]:
                            P.op("dve", lambda e: e.memset(af(eoff, POOL_BUF), 0.0), writes=[EhB[es_]])
                        else:
                            P.op("dve", lambda e: e.tensor_copy(out=af(eoff, POOL_BUF), in_=carryE[:, g, 0:POOL_BUF]),
                                 reads=[carryEB[g]], writes=[EhB[es_]])
                    else:
                        P.op("dve", lambda e: e.tensor_copy(
                            out=af(eoff, NSEQ * ER).rearrange("p (s r) -> p s r", r=ER)[:, :, 0:POOL_BUF],
                            in_=EH[:, g, :].rearrange("p (s r) -> p s r", r=POOL_BUF)),
                            reads=[EHB], writes=[EhB[es_]])
                    for il in range(ntl):
                        b = inproj(S["u"], g, il)
                        P.op("act", lambda e, b=b, il=il: e.activation(out=etok(eoff, il), in_=pview(b, il), func=AF.Copy),
                             reads=[psB[b]], writes=[EtB[es_][il]])
                        if kind == "s":
                            P.op("act", lambda e, b=b: e.activation(out=af(UD_OFF[es_], 128), in_=psum[b][:, 0:128], func=AF.Copy),
                                 reads=[psB[b]], writes=[UdB[es_]])

                def pool_ew(g):
                    es_ = g % 2
                    eoff = E_OFF[es_]
                    nlev = g + 1
                    Ereads = [EhB[es_]] + [EtB[es_][il] for il in range(ntl)]
                    src_off, srcB = eoff, Ereads
                    for L in range(1, nlev + 1):
                        sh = 1 << (L - 1)
                        v = (1 << L) - 1
                        ts_ = (L - 1) % 2
                        doff = T_OFF[ts_]
                        P.op("dve", lambda e, doff=doff, src_off=src_off, v=v, sh=sh: e.tensor_tensor(
                            out=af(doff + v, EW - v), in0=af(src_off + v, EW - v), in1=af(src_off + v - sh, EW - v), op=ALU.add),
                            reads=srcB, writes=[TB[ts_]])
                        src_off, srcB = doff, [TB[ts_]]
                    w_ = 1 << nlev
                    doff_ = D_OFF[es_]
                    dten = ab(doff_, 512)
                    for il in range(ntl):
                        P.op("dve", lambda e, il=il, src_off=src_off: e.scalar_tensor_tensor(
                            out=dense(dten, il), in0=etok(src_off, il), scalar=1.0 / w_, in1=etok(eoff, il), op0=ALU.mult, op1=ALU.subtract),
                            reads=srcB + [EtB[es_][il]], writes=[DB[es_][il]])
                    if kind == "p" and grp["first"]:
                        nfix = w_ - 1
                        P.op("dve", lambda e, src_off=src_off: e.tensor_tensor(
                            out=af(FIX_OFF, nfix), in0=af(src_off + POOL_BUF, nfix), in1=invcnt[:, 0:nfix], op=ALU.mult),
                            reads=srcB + [invB], writes=[fixB])
                        P.op("dve", lambda e: e.tensor_tensor(
                            out=dten[:, 0:nfix], in0=af(FIX_OFF, nfix), in1=af(eoff + POOL_BUF, nfix), op=ALU.subtract),
                            reads=[fixB, EtB[es_][0]], writes=[DB[es_][0]])
                    if kind == "p" and not grp["last"]:
                        P.op("dve", lambda e: e.tensor_copy(out=carryE[:, g, 0:POOL_BUF], in_=af(eoff + gw, POOL_BUF)),
                             reads=[EtB[es_][ntl - 1]], writes=[carryEB[g]])

                def pool_state(g):
                    es_ = g % 2
                    eoff = E_OFF[es_]
                    if kind == "p" and grp["last"]:
                        b = next_bank()
                        P.op("pe", lambda e, b=b: e.transpose(out=psum[b][0:POOL_BUF, 0:128], in_=af(eoff + gw, POOL_BUF), identity=ident[:]),
                             reads=[EtB[es_][ntl - 1], identB], writes=[psB[b]])
                        P.op("act", lambda e, b=b: e.activation(out=so_stg[0:POOL_BUF, 0, g * 128:(g + 1) * 128], in_=psum[b][0:POOL_BUF, 0:128], func=AF.Copy),
                             reads=[psB[b]], writes=[soB[0]])
                        if g == 3:
                            P.op("sp", lambda e: e.dma_start(out=npp[l], in_=so_stg[0:POOL_BUF, 0, :]), reads=[soB[0]], chan=so_ch[0])
                    elif kind == "s":
                        b = next_bank()
                        P.op("pe", lambda e, b=b: e.transpose(out=psum[b][:, 0:128], in_=af(UD_OFF[es_], 128), identity=ident[:]),
                             reads=[UdB[es_], identB], writes=[psB[b]])
                        P.op("act", lambda e, b=b: e.activation(out=so_stg[:, 0, g * 128:(g + 1) * 128], in_=psum[b][:, 0:128], func=AF.Copy),
                             reads=[psB[b]], writes=[soB[0]])
                        if g == 3:
                            for s_ in range(NSEQ):
                                P.op("sp", lambda e, s_=s_: e.dma_start(out=nps[l, s_, POOL_BUF - DSEQ:POOL_BUF, :], in_=so_stg[s_ * DSEQ:(s_ + 1) * DSEQ, 0, :]),
                                     reads=[soB[0]], chan=so_ch[0])

                def pool_mm(g):
                    es_ = g % 2
                    dten = ab(D_OFF[es_], 512)
                    for il in range(ntl):
                        o, tw = lo(il)
                        b = mm_group([(poolw[:, g, :], poolwB)], [(dten[:, o:o + tw], DB[es_][il])], tw)
                        P.op("act", lambda e, b=b, o=o, tw=tw: e.activation(out=mix_g[:, g, o:o + tw], in_=psum[b][:, 0:tw], func=AF.Identity,
                                                                            scale=vcol(V_PSC + l * 4 + g)),
                             reads=[psB[b], vecB], writes=[mixB_g[g][il]])

                def cv_chunk(j):
                    cs_ = j % 2
                    coff = CVE_OFF[cs_]
                    if kind == "p":
                        if grp["first"]:
                            P.op("dve", lambda e: e.memset(af(coff, CONV_BUF), 0.0), writes=[ChB[cs_]])
                        else:
                            P.op("dve", lambda e: e.tensor_copy(out=af(coff, CONV_BUF), in_=carryCV[:, j, :]),
                                 reads=[carryCVB[j]], writes=[ChB[cs_]])
                    else:
                        P.op("dve", lambda e: e.tensor_copy(
                            out=af(coff, NSEQ * CR).rearrange("p (s r) -> p s r", r=CR)[:, :, 0:CONV_BUF],
                            in_=CVH[:, l, j, :].rearrange("p (s r) -> p s r", r=CONV_BUF)),
                            reads=[CVHB], writes=[ChB[cs_]])
                    for il in range(ntl):
                        o, tw = lo(il)
                        bc = inproj(S["c"], j, il)
                        ci = st["r"] % 2
                        st["r"] += 1
                        P.op("act", lambda e, bc=bc, ci=ci, tw=tw: e.activation(out=af(C_OFF[ci], tw), in_=psum[bc][:, 0:tw], func=AF.Copy),
                             reads=[psB[bc]], writes=[CsB[ci]])
                        bv = inproj(S["v"], j, il)
                        P.op("dve", lambda e, bv=bv, ci=ci, il=il: e.tensor_tensor(out=ctok(coff, il, CONV_BUF), in0=dense(af(C_OFF[ci], 512), il) if kind == "s" else af(C_OFF[ci], lo(il)[1]),
                                                                                   in1=pview(bv, il), op=ALU.mult),
                             reads=[CsB[ci], psB[bv]], writes=[CtB[cs_][il]])
                    if kind == "p" and not grp["last"]:
                        P.op("dve", lambda e: e.tensor_copy(out=carryCV[:, j, :], in_=af(coff + gw, CONV_BUF)),
                             reads=[CtB[cs_][ntl - 1]], writes=[carryCVB[j]])

                def conv_state(j):
                    cs_ = j % 2
                    coff = CVE_OFF[cs_]
                    if kind == "p" and grp["last"]:
                        b = next_bank()
                        P.op("pe", lambda e, b=b: e.transpose(out=psum[b][0:CONV_BUF, 0:128], in_=af(coff + gw, CONV_BUF), identity=ident[:]),
                             reads=[CtB[cs_][ntl - 1], identB], writes=[psB[b]])
                        P.op("act", lambda e, b=b: e.activation(out=so_stg[0:CONV_BUF, 1, j * 128:(j + 1) * 128], in_=psum[b][0:CONV_BUF, 0:128], func=AF.Copy),
                             reads=[psB[b]], writes=[soB[1]])
                        if j == 3:
                            P.op("sp", lambda e: e.dma_start(out=ncp[l], in_=so_stg[0:CONV_BUF, 1, :]), reads=[soB[1]], chan=so_ch[1])
                    elif kind == "s":
                        b = next_bank()
                        P.op("dve", lambda e: e.tensor_copy(out=af(CD_OFF, NSEQ * CONV_BUF).rearrange("p (s r) -> p s r", r=CONV_BUF),
                                                            in_=af(coff, NSEQ * CR).rearrange("p (s r) -> p s r", r=CR)[:, :, DSEQ:CR]),
                             reads=[CtB[cs_][0]], writes=[CdB])
                        P.op("pe", lambda e, b=b: e.transpose(out=psum[b][0:NSEQ * CONV_BUF, 0:128],
                                                              in_=af(CD_OFF, NSEQ * CONV_BUF), identity=ident[:]),
                             reads=[CdB, identB], writes=[psB[b]])
                        P.op("act", lambda e, b=b: e.activation(out=so_stg[0:NSEQ * CONV_BUF, 1, j * 128:(j + 1) * 128], in_=psum[b][0:NSEQ * CONV_BUF, 0:128], func=AF.Copy),
                             reads=[psB[b]], writes=[soB[1]])
                        if j == 3:
                            P.op("sp", lambda e: e.dma_start(out=ncs[l], in_=so_stg[0:NSEQ * CONV_BUF, 1, :]), reads=[soB[1]], chan=so_ch[1])

                def conv_z(j):
                    cs_ = j % 2
                    coff = CVE_OFF[cs_]
                    Creads = [ChB[cs_]] + [CtB[cs_][il] for il in range(ntl)]
                    n_ = CW - 2
                    cwb = V_CW + l * 12 + j
                    P.op("act", lambda e: e.activation(out=af(Z_OFF, n_), in_=af(coff + 2, n_), func=AF.Identity, scale=vcol(cwb + 8), bias=vcol(V_CB + l * 4 + j)),
                         reads=Creads + [vecB], writes=[ZB])
                    P.op("dve", lambda e: e.scalar_tensor_tensor(out=af(Z_OFF, n_), in0=af(coff + 1, n_), scalar=vcol(cwb + 4), in1=af(Z_OFF, n_),
                                                                 op0=ALU.mult, op1=ALU.add),
                         reads=Creads + [vecB, ZB], writes=[ZB])
                    P.op("dve", lambda e: e.scalar_tensor_tensor(out=af(Z_OFF, n_), in0=af(coff, n_), scalar=vcol(cwb), in1=af(Z_OFF, n_),
                                                                 op0=ALU.mult, op1=ALU.add),
                         reads=Creads + [vecB, ZB], writes=[ZB])

                def b_chunk(j):
                    bs_ = j % 2
                    for il in range(ntl):
                        o, tw = lo(il)
                        bb = inproj(S["b"], j, il)
                        P.op("act", lambda e, bb=bb, o=o, tw=tw: e.activation(out=af(BS_OFF[bs_] + o, tw), in_=psum[bb][:, 0:tw], func=AF.Copy),
                             reads=[psB[bb]], writes=[BsB[bs_][il]])

                def y_chunk(j):
                    bs_ = j % 2
                    for il in range(ntl):
                        P.op("dve", lambda e, il=il: e.tensor_tensor(out=dense(mix_g[:, 4 + j, :], il), in0=ctok(Z_OFF, il, 0), in1=dense(af(BS_OFF[bs_], 1024), il), op=ALU.mult),
                             reads=[ZB, BsB[bs_][il]], writes=[mixB_g[4 + j][il]])

                def g_body(hooks=(), extra=False, lazy=False):
                    hooks = list(hooks)

                    def hk():
                        if hooks:
                            hooks.pop(0)()
                    S["u"] = ws_take()
                    if not lazy:
                        S["c"] = ws_take()
                        S["v"] = ws_take()
                    u_chunk(0)
                    hk()
                    u_chunk(1)
                    ws_release_half(S["u"], 0)
                    hk()
                    pool_ew(0)
                    pool_state(0)
                    u_chunk(2)
                    hk()
                    pool_ew(1)
                    pool_state(1)
                    u_chunk(3)
                    hk()
                    ws_release(S["u"])
                    if lazy:
                        S["c"] = ws_take()
                        S["v"] = ws_take()
                    else:
                        S["b"] = ws_take()
                    pool_mm(0)
                    pool_ew(2)
                    pool_state(2)
                    if lazy:
                        cv_chunk(0); hk(); conv_state(0); pool_mm(1); pool_ew(3); pool_state(3); conv_z(0)
                        cv_chunk(1)
                        ws_release_half(S["c"], 0)
                        ws_release_half(S["v"], 0)
                        hk(); conv_state(1)
                        S["b"] = ws_take()
                        b_chunk(0); y_chunk(0); conv_z(1); b_chunk(1)
                        ws_release_half(S["b"], 0)
                        y_chunk(1); pool_mm(2)
                        cv_chunk(2); hk(); conv_state(2); conv_z(2); b_chunk(2); y_chunk(2); pool_mm(3)
                        cv_chunk(3); hk(); conv_state(3); conv_z(3); b_chunk(3); y_chunk(3)
                    for j in (range(4) if not lazy else ()):
                        cv_chunk(j)
                        if j == 1:
                            ws_release_half(S["c"], 0)
                            ws_release_half(S["v"], 0)
                        hk()
                        conv_state(j)
                        if j == 0:
                            pool_mm(1)
                            pool_ew(3)
                            pool_state(3)
                        conv_z(j)
                        b_chunk(j)
                        if j == 1:
                            ws_release_half(S["b"], 0)
                        if extra:
                            hk()
                        y_chunk(j)
                        if j >= 1 and j <= 2:
                            pool_mm(j + 1)
                    ws_release(S["c"])
                    ws_release(S["v"])
                    ws_release(S["b"])
                    while hooks:
                        hk()

                def g_outproj_gen():
                    slot_o = [ws_take(), ws_take()]
                    yield
                    for c in range(KT):
                        for il in range(ntl):
                            o, tw = lo(il)
                            ti = tl[il]
                            toff = TILES[ti][0]
                            so_ = slot_o[c // 4]
                            lhs = [wap(so_, c % 4, k) for k in range(KT)]
                            rhs = [(mix_g[:, k, o:o + tw], mixB_g[k][il]) for k in range(KT)]
                            b = mm_group(lhs, rhs, tw)
                            P.op("dve", lambda e, b=b, c=c, toff=toff, tw=tw: e.tensor_tensor(out=h[:, c, toff:toff + tw], in0=h[:, c, toff:toff + tw], in1=psum[b][:, 0:tw], op=ALU.add),
                                 reads=[psB[b]] + hb(c, toff, tw), writes=hb(c, toff, tw))
                            yield
                        if c == 1:
                            ws_release_half(slot_o[0], 0)
                        if c == 3:
                            ws_release(slot_o[0])
                        if c == 5:
                            ws_release_half(slot_o[1], 0)
                    ws_release(slot_o[1])

                def g_outproj():
                    for _ in g_outproj_gen():
                        pass

                return g_steps, g_apply, g_body, g_outproj, g_outproj_gen

            NPAIR = 9

            def p_geom(idx):
                if idx < 8:
                    return idx * 256, 2
                return SEQ, 1

            def p_dma(idx):
                col0, n = p_geom(idx)
                base = (idx % 2) * 2
                for s_ in range(n):
                    pi = base + s_
                    col = col0 + s_ * 128
                    src = pp[l, col:col + 128, :] if col < SEQ else psm[l]
                    P.op("sp", lambda e, pi=pi, src=src: e.dma_start(out=pstg[:, pi, :], in_=src), writes=[pstgB[pi]], chan=pl_ch[pi])

            def p_pair(idx):
                def f_():
                    col0, n = p_geom(idx)
                    base = (idx % 2) * 2
                    if idx + 1 < NPAIR:
                        p_dma(idx + 1)
                    w_ = n * 128
                    for kk in range(2):
                        b = next_bank()
                        for s_ in range(n):
                            pi = base + s_
                            P.op("pe", lambda e, b=b, s_=s_, pi=pi, kk=kk: e.transpose(out=psum[b][:, s_ * 128:(s_ + 1) * 128], in_=pstg[:, pi, kk * 128:(kk + 1) * 128], identity=ident[:]),
                                 reads=[pstgB[pi], identB], writes=[psB[b]], signal=(s_ == n - 1))
                        P.op("act", lambda e, b=b, kk=kk: e.activation(out=pT_t[:, kk, col0:col0 + w_], in_=psum[b][:, 0:w_], func=AF.Copy),
                             reads=[psB[b]], writes=ptb(kk, col0, w_))
                return f_

            mctx = [dict(), dict()]
            ms0 = norm_steps(0, mctx[0])
            ms1 = norm_steps(1, mctx[1])
            c_hooks = [ms0[0], p_pair(5), ms0[1], p_pair(6), ms0[2], p_pair(7), ms1[0], p_pair(8), ms1[1], ms1[2]]

            gs = [do_group(l, grp) for grp in GROUPS]
            if l == 0:
                for f_ in gs[0][0]():
                    f_()
                a_apply = gs[0][1]
            else:
                a_apply = pre_apply
            a_apply()
            if l == 0:
                gs[0][2](late_hooks + gs[1][0](), extra=True)
            else:
                gs[0][2](gs[1][0]())
            gs[1][1]()
            gs[0][3]()
            cs_ = gs[2][0]()
            p_dma(0)
            gs[1][2]([cs_[0], p_pair(0), cs_[1], p_pair(1), cs_[2], p_pair(2), p_pair(3), p_pair(4)])
            gs[2][1]()
            og = gs[1][4]()
            next(og)

            def adv(n, filler):
                def f_():
                    for _ in range(n):
                        next(og, None)
                    filler()
                return f_
            sched = [3, 3, 3, 3, 1, 1, 1, 1]
            hooksC = [adv(sched[i], c_hooks[i]) for i in range(8)] + c_hooks[8:]
            gs[2][2](hooksC, lazy=True)
            for _ in og:
                pass
            gs[2][3]()

            P.stage_barrier()
            NTL = len(MT)
            if l + 1 < DEPTH:
                load_sample_state(l + 1)

            def mlp_norm(ti):
                toff, tw = MT[ti]
                norm_tile(ti, lambda k, toff=toff, tw=tw: m_t[:, k, toff:toff + tw], [mB[k][ti] for k in range(KT)], V_NMLP + l * 8, T=MT)

            def up_unit(su, c, ti):
                toff, tw = MT[ti]
                s_ = su[c // 4]
                lhs = [wap(s_, c % 4, k) for k in range(KT)]
                rhs = [(m_t[:, k, toff:toff + tw], mB[k][ti]) for k in range(KT)]
                b = mm_group(lhs, rhs, tw)
                ri = st["r"] % 3
                st["r"] += 1
                P.op("act", lambda e, b=b, ri=ri, tw=tw: e.activation(out=af(R_OFF[ri], tw), in_=psum[b][:, 0:tw], func=AF.Relu),
                     reads=[psB[b]], writes=[rB[ri]])
                P.op("dve", lambda e, ri=ri, c=c, toff=toff, tw=tw: e.tensor_tensor(out=f_t[:, c, toff:toff + tw], in0=af(R_OFF[ri], tw), in1=af(R_OFF[ri], tw), op=ALU.mult),
                     reads=[rB[ri]], writes=[fB[c][ti]])

            def dn_unit(sd, c, ti):
                toff, tw = MT[ti]
                s_ = sd[c // 4]
                lhs = [wap(s_, c % 4, k) for k in range(KT)]
                rhs = [(f_t[:, k, toff:toff + tw], fB[k][ti]) for k in range(KT)]
                b = mm_group(lhs, rhs, tw)
                P.op("dve", lambda e, b=b, c=c, toff=toff, tw=tw: e.tensor_tensor(out=h[:, c, toff:toff + tw], in0=h[:, c, toff:toff + tw], in1=psum[b][:, 0:tw], op=ALU.add),
                     reads=[psB[b]] + hb(c, toff, tw), writes=hb(c, toff, tw))

            def ple_norm(ti):
                toff, tw = MT[ti]
                norm_tile(ti, lambda k, toff=toff, tw=tw: n_t[:, k, toff:toff + tw], [nB[k][ti] for k in range(KT)], V_NPLE + l * 8, T=MT)

            for q in range(4):
                su = [ws_take(), ws_take()]
                if q == 0:
                    for t01 in range(2):
                        toff_, tw_ = MT[t01]
                        norm_apply(t01, mctx[t01]["n"], lambda k, toff_=toff_, tw_=tw_: m_t[:, k, toff_:toff_ + tw_], [mB[k][t01] for k in range(KT)], V_NMLP + l * 8, T=MT)
                    for ti in range(NTL):
                        nx = ti + 2
                        cx = {}
                        stp = norm_steps(nx, cx, T=MT) if nx < NTL else []
                        for c in range(KT):
                            up_unit(su, c, ti)
                            if stp and c in (0, 2, 4):
                                stp.pop(0)()
                                if c == 4:
                                    toff_, tw_ = MT[nx]
                                    norm_apply(nx, cx["n"], lambda k, toff_=toff_, tw_=tw_: m_t[:, k, toff_:toff_ + tw_], [mB[k][nx] for k in range(KT)], V_NMLP + l * 8, T=MT)
                    ws_release(su[0])
                    ws_release(su[1])
                else:
                    for c in range(KT):
                        for ti in range(NTL):
                            up_unit(su, c, ti)
                        if c == 1:
                            ws_release_half(su[0], 0)
                        if c == 3:
                            ws_release(su[0])
                        if c == 5:
                            ws_release_half(su[1], 0)
                    ws_release(su[1])
                sd = [ws_take(), ws_take()]
                if q < 3:
                    for c in range(KT):
                        for ti in range(NTL):
                            dn_unit(sd, c, ti)
                        if c == 1:
                            ws_release_half(sd[0], 0)
                        if c == 3:
                            ws_release(sd[0])
                        if c == 5:
                            ws_release_half(sd[1], 0)
                    ws_release(sd[1])
                else:
                    for ti in range(NTL):
                        for c in range(KT):
                            dn_unit(sd, c, ti)
                        if ti >= 1:
                            ple_norm(ti - 1)
                    ws_release(sd[0])
                    ws_release(sd[1])

            P.stage_barrier()
            sg = [ws_take(), ws_take()]

            def ple_unit(c, ti):
                toff, tw = MT[ti]
                s_ = sg[c // 4]
                lhs = [wap(s_, c % 4, k) for k in range(KT)]
                rhs = [(n_t[:, k, toff:toff + tw], nB[k][ti]) for k in range(KT)]
                b1 = mm_group(lhs, rhs, tw)
                lhs2 = [(wple[:, kk, c * 128:(c + 1) * 128], wpleB) for kk in range(2)]
                rhs2 = [(pT_t[:, kk, toff:toff + tw], ptb(kk, toff, tw)) for kk in range(2)]
                b2 = mm_group(lhs2, rhs2, tw)
                gi_ = st["g"] % 2
                st["g"] += 1
                P.op("act", lambda e, b1=b1, gi_=gi_, tw=tw: e.activation(out=af(G_OFF[gi_], tw), in_=psum[b1][:, 0:tw], func=AF.Sigmoid),
                     reads=[psB[b1]], writes=[gB[gi_]])
                P.op("dve", lambda e, b2=b2, gi_=gi_, tw=tw: e.tensor_tensor(out=af(TMP_OFF[gi_], tw), in0=af(G_OFF[gi_], tw), in1=psum[b2][:, 0:tw], op=ALU.mult),
                     reads=[gB[gi_], psB[b2]], writes=[tmpB[gi_]])
                P.op("dve", lambda e, gi_=gi_, c=c, toff=toff, tw=tw: e.tensor_tensor(out=h[:, c, toff:toff + tw], in0=h[:, c, toff:toff + tw], in1=af(TMP_OFF[gi_], tw), op=ALU.add),
                     reads=[tmpB[gi_]] + hb(c, toff, tw), writes=hb(c, toff, tw))

            def final_norm(ti):
                toff, tw = MT[ti]
                norm_tile(ti, lambda k, tw=tw: af(YFM_OFF + k * 512, tw), yfmB, V_NF, T=MT)

            def final_tile(ti):
                toff, tw = MT[ti]
                for s_ in range(tw // 128):
                    ti_ = st["y"] % 2
                    st["y"] += 1
                    for hf in range(2):
                        b = next_bank()
                        for kk in range(4):
                            k = hf * 4 + kk
                            P.op("pe", lambda e, b=b, kk=kk, k=k, s_=s_: e.transpose(out=psum[b][:, kk * 128:(kk + 1) * 128],
                                                                                    in_=af(YFM_OFF + k * 512 + s_ * 128, 128), identity=ident[:]),
                                 reads=[yfmB[k], identB], writes=[psB[b]], signal=(kk == 3))
                        if hf == 0:
                            P.op("act", lambda e, b=b, ti_=ti_: e.activation(out=af(YTM_OFF[ti_], 512), in_=psum[b][:], func=AF.Copy),
                                 reads=[psB[b]], writes=[ytmB[ti_]])
                        else:
                            P.op("dve", lambda e, b=b, ti_=ti_: e.tensor_copy(out=af(YTM_OFF[ti_] + 512, 512), in_=psum[b][:]),
                                 reads=[psB[b], ytmB[ti_]], writes=[ytmB[ti_]])
                    col_ = toff + s_ * 128
                    dst = y_p[col_:col_ + 128, :] if col_ < SEQ else y_s
                    P.op("sp", lambda e, ti_=ti_, dst=dst: e.dma_start(out=dst, in_=af(YTM_OFF[ti_], 1024)), reads=[ytmB[ti_]], chan=st_ch[ti_])

            last = (l == DEPTH - 1)
            nhooks = []
            if not last:
                nA = do_group(l + 1, GROUPS[0])
                nhooks = nA[0]()
                pre_apply = nA[1]
            lctx = {}
            lsteps = norm_steps(NTL - 1, lctx, T=MT)
            for ti in range(NTL):
                fsteps, fctx = [], {}
                if last and ti >= 1:
                    fsteps = norm_steps(ti - 1, fctx, T=MT)
                for c in range(KT):
                    ple_unit(c, ti)
                    if ti == 0 and c in (0, 2, 4):
                        lsteps.pop(0)()
                        if c == 4:
                            toff_, tw_ = MT[NTL - 1]
                            norm_apply(NTL - 1, lctx["n"], lambda k, toff_=toff_, tw_=tw_: n_t[:, k, toff_:toff_ + tw_], [nB[k][NTL - 1] for k in range(KT)], V_NPLE + l * 8, T=MT)
                    if nhooks and ti >= 2 and c in (1, 3, 5):
                        nhooks.pop(0)()
                    if fsteps and c in (0, 2, 4):
                        fsteps.pop(0)()
                        if c == 4:
                            tw_ = MT[ti - 1][1]
                            norm_apply(ti - 1, fctx["n"], lambda k, tw_=tw_: af(YFM_OFF + k * 512, tw_), yfmB, V_NF, T=MT)
                if last and ti >= 1:
                    final_tile(ti - 1)
            ws_release(sg[0])
            ws_release(sg[1])
            while nhooks:
                nhooks.pop(0)()
            if last:
                final_norm(NTL - 1)
                final_tile(NTL - 1)

        assert ws["taken"] == len(wq) and ws["issued"] == len(wq), (ws["taken"], ws["issued"], len(wq))

        final_waits = [(c, c.count) for c in st_ch + so_ch + [d2d_ch] if c.count > 0]

        all_ch = list(P.engs.values()) + P.chans
        for c in all_ch:
            c.sem = es.enter_context(nc.semaphore(c.name))
        block = es.enter_context(nc.Block())

        def replay(name, e, tail=()):
            for waits, fn, sig in P.progs[name]:
                for c, v in waits:
                    e.wait_ge(c.sem, v)
                ins = fn(e)
                if sig is not None:
                    ins.then_inc(sig[0].sem, sig[1])
            for c, v in tail:
                e.wait_ge(c.sem, v)

        @block.tensor
        def _(e):
            replay("pe", e)

        @block.scalar
        def _(e):
            replay("act", e)

        @block.vector
        def _(e):
            replay("dve", e)

        @block.gpsimd
        def _(e):
            replay("pool", e)

        @block.sync
        def _(e):
            replay("sp", e, tail=final_waits)

    return nc


_NC = None


def kernel(x_prompt, x_sample, state_pool, state_conv, p_prompt, p_sample,
           norm_mix, w_in, pool_w, pool_scale, conv_w, conv_b, w_out,
           norm_mlp, w_up, w_down, norm_ple, w_ple_gate, w_ple_proj, norm_f):
    global _NC
    f = lambda a: np.ascontiguousarray(np.asarray(a, dtype=np.float32))
    if _NC is None:
        _NC = build()
    nc = _NC
    vecs = np.concatenate([
        f(norm_mix).reshape(16, 128), f(norm_mlp).reshape(16, 128), f(norm_ple).reshape(16, 128),
        f(norm_f).reshape(8, 128), f(pool_scale).reshape(8, 128), f(conv_w).reshape(24, 128),
        f(conv_b).reshape(8, 128)], axis=0)
    shared = dict(vecs=f(vecs), w_in=f(w_in), pool_w=f(pool_w), w_out=f(w_out), w_up=f(w_up),
                  w_down=f(w_down), w_gate=f(w_ple_gate), w_ple=f(w_ple_proj))
    x_prompt, x_sample = f(x_prompt), f(x_sample)
    state_pool, state_conv = f(state_pool), f(state_conv)
    p_prompt, p_sample = f(p_prompt), f(p_sample)
    in_maps = []
    for c in range(NCORES):
        ss = slice(c * NSEQ, (c + 1) * NSEQ)
        m = dict(shared)
        m["x_p"] = x_prompt[c]
        m["x_s"] = f(x_sample[ss].reshape(NS, D))
        m["sp_in"] = f(state_pool[:, ss].reshape(DEPTH, NSEQ * POOL_BUF, 512))
        m["sc_in"] = f(state_conv[:, ss].reshape(DEPTH, NSEQ * CONV_BUF, 512))
        m["pp"] = f(p_prompt[:, c])
        m["psm"] = f(p_sample[:, ss].reshape(DEPTH, NS, PLE))
        in_maps.append(m)
    res = run_bass_kernel_spmd(nc, in_maps, core_ids=list(range(NCORES)))
    R = res.results
    y_prompt = np.stack([R[c]["y_p"] for c in range(NCORES)], axis=0)
    y_sample = np.concatenate([R[c]["y_s"].reshape(NSEQ, DSEQ, D) for c in range(NCORES)], axis=0)
    new_pool_prompt = np.stack([R[c]["npp"] for c in range(NCORES)], axis=1)
    new_conv_prompt = np.stack([R[c]["ncp"] for c in range(NCORES)], axis=1)
    new_pool_sample = np.concatenate([R[c]["nps"] for c in range(NCORES)], axis=1)
    new_conv_sample = np.concatenate([R[c]["ncs"].reshape(DEPTH, NSEQ, CONV_BUF, 512) for c in range(NCORES)], axis=1)
    return (y_prompt.astype(np.float32), y_sample.astype(np.float32), new_pool_prompt.astype(np.float32),
            new_conv_prompt.astype(np.float32), new_pool_sample.astype(np.float32), new_conv_sample.astype(np.float32))
```

```python
import numpy as np
from contextlib import ExitStack
import concourse.bass as bass
import concourse.mybir as mybir
from concourse.bass_utils import run_bass_kernel_spmd

F32 = mybir.dt.float32
BF16 = mybir.dt.bfloat16
AF = mybir.ActivationFunctionType
ALU = mybir.AluOpType

NCORES = 8
D = 1024
KT = 8
SEQ = 2048
NS = 128
NSEQ = 16
DSEQ = 8
NTOK = SEQ + NS
DEPTH = 2
DFF = 4096
PLE = 256
EPS = 1e-6
POOL_BUF = 15
CONV_BUF = 2
ER = POOL_BUF + DSEQ
CR = CONV_BUF + DSEQ

TILES = [(0, 512), (512, 512), (1024, 512), (1536, 512), (2048, 128)]
MT = [(0, 512), (512, 512), (1024, 512), (1536, 384), (1920, 256)]
NSUB = NTOK // 128
GROUPS = [
    dict(tiles=[0, 1], kind="p", first=True, last=False),
    dict(tiles=[2, 3], kind="p", first=False, last=True),
    dict(tiles=[4], kind="s", first=False, last=False),
]

V_NMIX, V_NMLP, V_NPLE, V_NF, V_PSC, V_CW, V_CB = 0, 16, 32, 48, 56, 64, 88
NVEC = 96


class Ch:
    def __init__(self, name):
        self.name = name
        self.count = 0
        self.sem = None


class Buf:
    __slots__ = ("w", "r", "arena")

    def __init__(self, arena=False):
        self.w = {}
        self.r = {}
        self.arena = arena


def _merge(dst, src):
    for c, v in src.items():
        if dst.get(c, 0) < v:
            dst[c] = v


class Prog:
    ENGS = ["pe", "act", "dve", "pool", "sp"]

    def __init__(self):
        self.engs = {n: Ch(n) for n in self.ENGS}
        self.progs = {n: [] for n in self.ENGS}
        self.waited = {n: {} for n in self.ENGS}
        self.chans = []
        self.barrier = {}
        self.arena_bufs = []

    def chan(self, name):
        c = Ch(name)
        self.chans.append(c)
        return c

    def abuf(self):
        b = Buf(arena=True)
        self.arena_bufs.append(b)
        return b

    def stage_barrier(self):
        for b in self.arena_bufs:
            _merge(self.barrier, b.w)
            _merge(self.barrier, b.r)

    def op(self, eng, fn, reads=(), writes=(), signal=True, chan=None):
        need = {}
        for b in reads:
            _merge(need, b.w)
        for b in writes:
            _merge(need, b.w)
            _merge(need, b.r)
            if b.arena:
                _merge(need, self.barrier)
        waits = []
        wd = self.waited[eng]
        pe_ch = self.engs["pe"]
        for c, v in need.items():
            if eng == "pe" and c is pe_ch:
                continue
            if wd.get(c, 0) < v:
                waits.append((c, v))
                wd[c] = v
        if chan is not None:
            chan.count += 16
            tok = {chan: chan.count}
            sig = (chan, 16)
        else:
            ch = self.engs[eng]
            tok = {ch: ch.count + 1}
            if signal:
                ch.count += 1
                sig = (ch, 1)
            else:
                sig = None
        self.progs[eng].append((waits, fn, sig))
        for b in reads:
            _merge(b.r, tok)
        for b in writes:
            b.w = dict(tok)
            b.r = {}
        return tok


def build():
    nc = bass.Bass("TRN2", target_bir_lowering=False)
    dt_in = lambda name, shape: nc.dram_tensor(name, shape, F32, kind="ExternalInput").ap()
    dt_out = lambda name, shape: nc.dram_tensor(name, shape, F32, kind="ExternalOutput").ap()
    x_p = dt_in("x_p", [SEQ, D])
    x_s = dt_in("x_s", [NS, D])
    sp_in = dt_in("sp_in", [DEPTH, NSEQ * POOL_BUF, 512])
    sc_in = dt_in("sc_in", [DEPTH, NSEQ * CONV_BUF, 512])
    pp = dt_in("pp", [DEPTH, SEQ, PLE])
    psm = dt_in("psm", [DEPTH, NS, PLE])
    vecs = dt_in("vecs", [NVEC, 128])
    w_in = dt_in("w_in", [DEPTH, D, 2048])
    pool_w = dt_in("pool_w", [DEPTH, 4, 128, 128])
    w_out = dt_in("w_out", [DEPTH, D, D])
    w_up = dt_in("w_up", [DEPTH, D, DFF])
    w_down = dt_in("w_down", [DEPTH, DFF, D])
    w_gate = dt_in("w_gate", [DEPTH, D, D])
    w_ple = dt_in("w_ple", [DEPTH, PLE, D])
    y_p = dt_out("y_p", [SEQ, D])
    y_s = dt_out("y_s", [NS, D])
    npp = dt_out("npp", [DEPTH, POOL_BUF, 512])
    ncp = dt_out("ncp", [DEPTH, CONV_BUF, 512])
    nps = dt_out("nps", [DEPTH, NSEQ, POOL_BUF, 512])
    ncs = dt_out("ncs", [DEPTH, NSEQ * CONV_BUF, 512])

    P = Prog()
    AW = 19904
    es = ExitStack()
    with es:
        sb = lambda name, shape, dt: es.enter_context(nc.sbuf_tensor(name, shape, dt))
        h = sb("h", [128, KT, NTOK], F32)
        ring = sb("ring", [128, 6, 8, 256], BF16)
        pT_t = sb("pT", [128, 2, NTOK], BF16)
        wple = sb("wple", [128, 2, D], BF16)
        poolw = sb("poolw", [128, 4, 128], BF16)
        sq = sb("sq", [128, 4, 512], BF16)
        rstd = sb("rstd", [128, 2, 512], F32)
        EH = sb("EH", [128, 4, NSEQ * POOL_BUF], F32)
        CVH = sb("CVH", [128, DEPTH, 4, NSEQ * CONV_BUF], F32)
        carryE = sb("carryE", [128, 4, 16], F32)
        carryCV = sb("carryCV", [128, 4, 2], F32)
        vec = sb("vec", [128, NVEC], F32)
        ident = sb("ident", [128, 128], F32)
        ones = sb("ones", [128, 128], BF16)
        invcnt = sb("invcnt", [128, 16], F32)
        so_stg = sb("so_stg", [128, 2, 512], F32)
        pstg = sb("pstg", [128, 4, PLE], F32)
        mixc = sb("mixc", [128, KT, NS], BF16)
        arena = sb("arena", [128, AW], F32)
        psum = [es.enter_context(nc.psum_tensor(f"ps{i}", [128, 512], F32)) for i in range(8)]

        def af(off, n):
            return arena[:, off:off + n]

        def ab(off, nwords):
            return arena[:, off:off + nwords].bitcast(BF16)

        hB = [[Buf() for _ in range(NSUB)] for _ in range(KT)]

        def hb(k, toff, tw):
            return hB[k][toff // 128:(toff + tw) // 128]

        psB = [Buf() for _ in range(8)]
        ringB = [Buf() for _ in range(6)]
        wpleB, poolwB = Buf(), Buf()
        sqB = [Buf() for _ in range(4)]
        rstdB = [Buf(), Buf()]
        EHB, CVHB, vecB, identB, onesB, invB = Buf(), Buf(), Buf(), Buf(), Buf(), Buf()
        carryEB = [Buf() for _ in range(4)]
        carryCVB = [Buf() for _ in range(4)]
        soB = [Buf(), Buf()]
        pstgB = [Buf() for _ in range(4)]
        pTB = [[Buf() for _ in range(NSUB)] for _ in range(2)]

        def ptb(kk, toff, tw):
            return pTB[kk][toff // 128:(toff + tw) // 128]


        st = dict(bank=0, sq=0, norm=0, r=0, g=0, pst=0, y=0)

        reserved = set()

        def next_bank():
            while True:
                b = st["bank"]
                st["bank"] = (b + 1) % 8
                if b not in reserved:
                    return b

        ring_ch = [P.chan(f"ring{i}") for i in range(6)]
        poolw_ch = P.chan("poolw")
        wple_ch = P.chan("wple")
        setup_ch = [P.chan(f"su{i}") for i in range(3)]
        sui = iter(setup_ch)
        ld_ch = [P.chan(f"ld{i}") for i in range(8)]
        pl_ch = [P.chan(f"pl{i}") for i in range(4)]
        st_ch = [P.chan(f"st{i}") for i in range(2)]
        so_ch = [P.chan(f"so{i}") for i in range(2)]
        d2d_ch = P.chan("d2d")

        def vcol(c):
            return vec[:, c:c + 1]

        def wview(w_l, c0):
            v = w_l.rearrange("(k p) n -> p k n", p=128)
            return [v[:, :, c0:c0 + 256], v[:, :, c0 + 256:c0 + 512]]

        wq = []
        for l in range(DEPTH):
            for _g in GROUPS:
                wq += wview(w_in[l], 0) + wview(w_in[l], 1024) + wview(w_in[l], 1536) + wview(w_in[l], 512) \
                    + wview(w_out[l], 0) + wview(w_out[l], 512)
            for q in range(4):
                wq += wview(w_up[l], q * 1024) + wview(w_up[l], q * 1024 + 512) \
                    + wview(w_down[l][q * 1024:(q + 1) * 1024, :], 0) + wview(w_down[l][q * 1024:(q + 1) * 1024, :], 512)
            wq += wview(w_gate[l], 0) + wview(w_gate[l], 512)
        ws = dict(issued=0, taken=0, free=[True] * 6, gate=(), slot={})

        def ws_pump():
            while ws["issued"] < len(wq) and any(ws["free"]):
                i = ws["issued"]
                s_ = ws["free"].index(True)
                ws["slot"][i] = s_
                P.op("pool", lambda e, s_=s_, src=wq[i]: e.dma_start(out=ring[:, s_], in_=src), reads=(ws["gate"] if i in (2, 3, 4, 5) else ()),
                     writes=[ringB[s_]], chan=ring_ch[s_])
                ws["free"][s_] = False
                ws["issued"] += 1

        def ws_take():
            i = ws["taken"]
            ws["taken"] += 2
            assert i + 1 < ws["issued"], "weight block not yet issued"
            return dict(s=[ws["slot"][i], ws["slot"][i + 1]], rel=[False, False])

        def ws_release_half(blk, hh):
            if not blk["rel"][hh]:
                blk["rel"][hh] = True
                ws["free"][blk["s"][hh]] = True
                ws_pump()

        def ws_release(blk):
            ws_release_half(blk, 0)
            ws_release_half(blk, 1)

        def wap(blk, c4, k):
            s_ = blk["s"][c4 // 2]
            return (ring[:, s_, k, (c4 % 2) * 128:(c4 % 2 + 1) * 128], ringB[s_])

        P.op("pool", lambda e: e.memset(ident[:], 0.0), writes=[identB])
        P.op("pool", lambda e: e.affine_select(out=ident[:], in_=ident[:], pattern=[[-1, 128]],
                                               compare_op=ALU.not_equal, fill=1.0, base=0,
                                               channel_multiplier=1), writes=[identB])
        P.op("pool", lambda e: e.memset(ones[:], 1.0), writes=[onesB])
        for t in range(16):
            P.op("pool", lambda e, t=t: e.memset(invcnt[:, t:t + 1], 1.0 / (t + 1)), writes=[invB])

        XSTG = 8192
        SPSTG = 16384
        vstg = pstg[:, 0, 0:128]
        vstgB = pstgB[0]
        P.op("sp", lambda e: e.dma_start(out=vstg[0:NVEC, :], in_=vecs), writes=[vstgB], chan=next(sui))
        b = next_bank()
        P.op("pe", lambda e, b=b: e.transpose(out=psum[b][:, 0:NVEC], in_=vstg[0:NVEC, :], identity=ident[0:NVEC, 0:NVEC]),
             reads=[vstgB, identB], writes=[psB[b]])
        P.op("act", lambda e, b=b: e.activation(out=vec[:], in_=psum[b][:, 0:NVEC], func=AF.Copy),
             reads=[psB[b]], writes=[vecB])

        pstg2 = pstg[:].rearrange("p a c -> p (a c)").rearrange("p (h c) -> p h c", h=2)

        def load_sample_state(l):
            for hf in range(2):
                stg = pstg2[:, hf, :]
                sBs = [pstgB[2 * hf], pstgB[2 * hf + 1]]
                P.op("sp", lambda e, stg=stg, hf=hf: e.dma_start(out=stg[0:120, :], in_=sp_in[l, hf * 120:(hf + 1) * 120, :]),
                     writes=sBs, chan=pl_ch[2 * hf])
                b = next_bank()
                for g in range(4):
                    P.op("pe", lambda e, b=b, g=g, stg=stg: e.transpose(out=psum[b][:, g * 120:(g + 1) * 120], in_=stg[0:120, g * 128:(g + 1) * 128],
                                                                        identity=ident[0:120, 0:120]),
                         reads=sBs + [identB], writes=[psB[b]], signal=(g == 3))
                P.op("act", lambda e, b=b, hf=hf: e.activation(out=EH[:, :, hf * 120:(hf + 1) * 120],
                                                               in_=psum[b][:, 0:480].rearrange("p (g c) -> p g c", g=4), func=AF.Copy),
                     reads=[psB[b]], writes=[EHB])

        load_sample_state(0)
        for l in range(DEPTH):
            stg = so_stg[:, l, :]
            sB_ = soB[l]
            P.op("sp", lambda e, stg=stg, l=l: e.dma_start(out=stg[0:32, :], in_=sc_in[l]), writes=[sB_], chan=next(sui))
            b = next_bank()
            for j in range(4):
                P.op("pe", lambda e, b=b, j=j, stg=stg: e.transpose(out=psum[b][:, j * 32:(j + 1) * 32], in_=stg[0:32, j * 128:(j + 1) * 128],
                                                                    identity=ident[0:32, 0:32]),
                     reads=[sB_, identB], writes=[psB[b]], signal=(j == 3))
            P.op("act", lambda e, b=b, l=l: e.activation(out=CVH[:, l], in_=psum[b][:, 0:128].rearrange("p (g c) -> p g c", g=4), func=AF.Copy),
                 reads=[psB[b]], writes=[CVHB])
            P.op("sp", lambda e, l=l: e.dma_start(out=nps[l, :, 0:POOL_BUF - DSEQ, :],
                                                  in_=sp_in[l].rearrange("(s r) c -> s r c", r=POOL_BUF)[:, DSEQ:POOL_BUF, :]),
                 chan=d2d_ch)

        xstgB = [P.abuf() for _ in range(8)]
        xstg_all = [af(XSTG + i * 1024, 1024) for i in range(8)]
        for ti, (toff, tw) in enumerate(TILES[0:2]):
            nsub = tw // 128
            xo = (ti % 2) * 4
            xstg = xstg_all[xo:xo + 4]
            xsB = xstgB[xo:xo + 4]
            for s_ in range(nsub):
                src = x_p[toff + s_ * 128: toff + (s_ + 1) * 128, :] if toff < SEQ else x_s
                P.op("sp", lambda e, s_=s_, src=src, xstg=xstg: e.dma_start(out=xstg[s_], in_=src), writes=[xsB[s_]], chan=ld_ch[xo + s_])
            if nsub == 4:
                for k in range(KT):
                    b = next_bank()
                    for s_ in range(4):
                        P.op("pe", lambda e, b=b, s_=s_, k=k, xstg=xstg: e.transpose(out=psum[b][:, s_ * 128:(s_ + 1) * 128], in_=xstg[s_][:, k * 128:(k + 1) * 128], identity=ident[:]),
                             reads=[xsB[s_], identB], writes=[psB[b]], signal=(s_ == 3))
                    if ti == 1 and k == 0:
                        gateB = Buf()
                        gateB.w = dict(psB[b].w)
                        ws["gate"] = (gateB,)
                    if k % 2 == 0:
                        P.op("act", lambda e, b=b, k=k, toff=toff: e.activation(out=h[:, k, toff:toff + 512], in_=psum[b][:], func=AF.Copy),
                             reads=[psB[b]], writes=hb(k, toff, 512))
                    else:
                        P.op("dve", lambda e, b=b, k=k, toff=toff: e.tensor_copy(out=h[:, k, toff:toff + 512], in_=psum[b][:]),
                             reads=[psB[b]], writes=hb(k, toff, 512))
            else:
                for kh in range(2):
                    b = next_bank()
                    for kk in range(4):
                        k = kh * 4 + kk
                        P.op("pe", lambda e, b=b, kk=kk, k=k, xstg=xstg: e.transpose(out=psum[b][:, kk * 128:(kk + 1) * 128], in_=xstg[0][:, k * 128:(k + 1) * 128], identity=ident[:]),
                             reads=[xsB[0], identB], writes=[psB[b]], signal=(kk == 3))
                    P.op("act", lambda e, b=b, kh=kh, toff=toff: e.activation(out=h[:, kh * 4:(kh + 1) * 4, toff:toff + 128],
                                                                              in_=psum[b][:].rearrange("p (k c) -> p k c", k=4), func=AF.Copy),
                         reads=[psB[b]], writes=[hb(kh * 4 + kk, toff, 128)[0] for kk in range(4)])

        ws_pump()

        xl = [pT_t[:, j, :].bitcast(F32)[:, 0:1024] for j in range(2)]
        xl_ch = [P.chan(f"xl{j}") for j in range(2)]
        late_list = []
        for ti in range(2, len(TILES)):
            toff, tw = TILES[ti]
            for s_ in range(tw // 128):
                src = x_p[toff + s_ * 128: toff + (s_ + 1) * 128, :] if toff < SEQ else x_s
                late_list.append((ti, toff + s_ * 128, src))

        def late_dma(n):
            ti, col0, src = late_list[n]
            j = n % 2
            P.op("sp", lambda e, j=j, src=src: e.dma_start(out=xl[j], in_=src), writes=list(pTB[j]), chan=xl_ch[j])

        def late_step(n):
            def f_():
                ti, col0, src = late_list[n]
                j = n % 2
                if n + 1 < len(late_list):
                    late_dma(n + 1)
                for hf in range(2):
                    b = next_bank()
                    for kk in range(4):
                        k = hf * 4 + kk
                        P.op("pe", lambda e, b=b, kk=kk, k=k: e.transpose(out=psum[b][:, kk * 128:(kk + 1) * 128], in_=xl[j][:, k * 128:(k + 1) * 128], identity=ident[:]),
                             reads=list(pTB[j]) + [identB], writes=[psB[b]], signal=(kk == 3))
                    P.op("act", lambda e, b=b, hf=hf: e.activation(out=h[:, hf * 4:(hf + 1) * 4, col0:col0 + 128],
                                                                   in_=psum[b][:].rearrange("p (k c) -> p k c", k=4), func=AF.Copy),
                         reads=[psB[b]], writes=[hb(hf * 4 + kk, col0, 128)[0] for kk in range(4)])
            return f_

        late_dma(0)
        late_hooks = [late_step(n) for n in range(len(late_list))]

        def norm_tile(ti, dst_ap_fn, dstB, gbase, T=TILES):
            toff, tw = T[ti]
            b = next_bank()
            for k in range(KT):
                q = st["sq"]
                st["sq"] = (q + 1) % 4
                P.op("act", lambda e, q=q, k=k: e.activation(out=sq[:, q, 0:tw], in_=h[:, k, toff:toff + tw], func=AF.Square),
                     reads=hb(k, toff, tw), writes=[sqB[q]])
                P.op("pe", lambda e, q=q, k=k, b=b: e.matmul(psum[b][:, 0:tw], lhsT=ones[:], rhs=sq[:, q, 0:tw], start=(k == 0), stop=(k == KT - 1)),
                     reads=[sqB[q], onesB], writes=[psB[b]], signal=True)
            n_ = st["norm"]
            st["norm"] = 1 - n_
            P.op("act", lambda e, b=b, n_=n_: e.activation(out=rstd[:, n_, 0:tw], in_=psum[b][:, 0:tw], func=AF.Ln, scale=1.0 / D, bias=EPS),
                 reads=[psB[b]], writes=[rstdB[n_]])
            P.op("act", lambda e, n_=n_: e.activation(out=rstd[:, n_, 0:tw], in_=rstd[:, n_, 0:tw], func=AF.Exp, scale=-0.5),
                 reads=[rstdB[n_]], writes=[rstdB[n_]])
            for k in range(KT):
                P.op("dve", lambda e, k=k, n_=n_: e.scalar_tensor_tensor(out=dst_ap_fn(k), in0=h[:, k, toff:toff + tw], scalar=vcol(gbase + k),
                                                                         in1=rstd[:, n_, 0:tw], op0=ALU.mult, op1=ALU.mult),
                     reads=hb(k, toff, tw) + [rstdB[n_], vecB], writes=[dstB[k]])

        def norm_steps(ti, ctx, T=TILES):
            toff, tw = T[ti]

            def squares(hf):
                for j in range(4):
                    k = hf * 4 + j
                    P.op("act", lambda e, j=j, k=k: e.activation(out=sq[:, j, 0:tw], in_=h[:, k, toff:toff + tw], func=AF.Square),
                         reads=hb(k, toff, tw), writes=[sqB[j]])

            def stats(hf):
                b = ctx["b"]
                for j in range(4):
                    k = hf * 4 + j
                    P.op("pe", lambda e, j=j, k=k, b=b: e.matmul(psum[b][:, 0:tw], lhsT=ones[:], rhs=sq[:, j, 0:tw], start=(k == 0), stop=(k == KT - 1)),
                         reads=[sqB[j], onesB], writes=[psB[b]], signal=True)

            def s1():
                ctx["b"] = next_bank()
                reserved.add(ctx["b"])
                squares(0)

            def s2():
                stats(0)
                squares(1)

            def s3():
                stats(1)
                b = ctx["b"]
                n_ = st["norm"]
                st["norm"] = 1 - n_
                ctx["n"] = n_
                P.op("act", lambda e: e.activation(out=rstd[:, n_, 0:tw], in_=psum[b][:, 0:tw], func=AF.Ln, scale=1.0 / D, bias=EPS),
                     reads=[psB[b]], writes=[rstdB[n_]])
                P.op("act", lambda e: e.activation(out=rstd[:, n_, 0:tw], in_=rstd[:, n_, 0:tw], func=AF.Exp, scale=-0.5),
                     reads=[rstdB[n_]], writes=[rstdB[n_]])
                reserved.discard(b)

            return [s1, s2, s3]

        def norm_apply(ti, n_, dst_ap_fn, dstB, gbase, T=TILES):
            toff, tw = T[ti]
            for k in range(KT):
                P.op("dve", lambda e, k=k: e.scalar_tensor_tensor(out=dst_ap_fn(k), in0=h[:, k, toff:toff + tw], scalar=vcol(gbase + k),
                                                                  in1=rstd[:, n_, 0:tw], op0=ALU.mult, op1=ALU.mult),
                     reads=hb(k, toff, tw) + [rstdB[n_], vecB], writes=[dstB[k]])

        def mm_group(lhs_list, rhs_list, width):
            b = next_bank()
            n = len(lhs_list)
            for i in range(n):
                (lap, lB), (rap, rB) = lhs_list[i], rhs_list[i]
                P.op("pe", lambda e, b=b, lap=lap, rap=rap, i=i: e.matmul(psum[b][:, 0:width], lhsT=lap, rhs=rap, start=(i == 0), stop=(i == n - 1)),
                     reads=(lB if isinstance(lB, list) else [lB]) + (rB if isinstance(rB, list) else [rB]), writes=[psB[b]], signal=(i == n - 1))
            return b

        A_OFF, MIX_OFF = 0, 4096
        E_OFF = [8192, 9232]
        T_OFF = [10272, 11312]
        D_OFF = [12352, 12864]
        CVE_OFF = [13376, 14404]
        Z_OFF = 15432
        C_OFF = [16460, 16972]
        FIX_OFF = 17484
        BS_OFF = [17800, 18824]
        UD_OFF = [17500, 17628]
        CD_OFF = 17756
        a_t = ab(A_OFF, 4096).rearrange("p (k n) -> p k n", k=KT)
        mix_t = ab(MIX_OFF, 4096).rearrange("p (k n) -> p k n", k=KT)
        aB = [[P.abuf() for _ in range(2)] for _ in range(KT)]
        mixB = [[P.abuf() for _ in range(2)] for _ in range(KT)]
        mixcB = [[Buf()] for _ in range(KT)]
        EtB = [[P.abuf() for _ in range(2)] for _ in range(2)]
        EhB = [P.abuf() for _ in range(2)]
        TB = [P.abuf() for _ in range(2)]
        DB = [[P.abuf() for _ in range(2)] for _ in range(2)]
        CtB = [[P.abuf() for _ in range(2)] for _ in range(2)]
        ChB = [P.abuf() for _ in range(2)]
        ZB = P.abuf()
        CsB = [P.abuf() for _ in range(2)]
        fixB = P.abuf()
        BsB = [[P.abuf() for _ in range(2)] for _ in range(2)]
        UdB = [P.abuf(), P.abuf()]
        CdB = P.abuf()
        M_OFF, F_OFF = 0, 8704
        R_OFF = [17408, 17920, 18432]
        m_t = ab(M_OFF, 8704).rearrange("p (k n) -> p k n", k=KT)
        f_t = ab(F_OFF, 8704).rearrange("p (k n) -> p k n", k=KT)
        mB = [[P.abuf() for _ in TILES] for _ in range(KT)]
        fB = [[P.abuf() for _ in TILES] for _ in range(KT)]
        rB = [P.abuf() for _ in range(3)]
        G_OFF = [8704, 9216]
        TMP_OFF = [9728, 10240]
        nB = mB
        n_t = m_t
        gB = [P.abuf() for _ in range(2)]
        tmpB = [P.abuf() for _ in range(2)]
        YFM_OFF = 10752
        YTM_OFF = [14848, 15872]
        yfmB = [P.abuf() for _ in range(KT)]
        ytmB = [P.abuf() for _ in range(2)]

        for l in range(DEPTH):
            P.stage_barrier()
            P.op("pool", lambda e, l=l: e.dma_start(out=poolw[:], in_=pool_w[l].rearrange("g c d -> c g d")), writes=[poolwB], chan=poolw_ch)
            P.op("pool", lambda e, l=l: e.dma_start(out=wple[:], in_=w_ple[l].rearrange("(k p) n -> p k n", p=128)), writes=[wpleB], chan=wple_ch)
            def do_group(l, grp):
                kind = grp["kind"]
                tl = grp["tiles"]
                ntl = len(tl)
                g0 = TILES[tl[0]][0]
                gw = sum(TILES[t][1] for t in tl)
                if kind == "p":
                    EW, CW = POOL_BUF + gw, CONV_BUF + gw
                else:
                    EW, CW = NSEQ * ER, NSEQ * CR

                def lo(il):
                    toff, tw = TILES[tl[il]]
                    return toff - g0, tw

                def etok(off, il):
                    o, tw = lo(il)
                    if kind == "p":
                        return af(off + POOL_BUF + o, tw)
                    return af(off, NSEQ * ER).rearrange("p (s r) -> p s r", r=ER)[:, :, POOL_BUF:ER]

                def ctok(off, il, hist):
                    o, tw = lo(il)
                    if kind == "p":
                        return af(off + hist + o, tw)
                    return af(off, NSEQ * CR).rearrange("p (s r) -> p s r", r=CR)[:, :, hist:hist + DSEQ]

                def pview(b, il):
                    o, tw = lo(il)
                    if kind == "p":
                        return psum[b][:, 0:tw]
                    return psum[b][:, 0:tw].rearrange("p (s t) -> p s t", t=DSEQ)

                def dense(ap2, il):
                    o, tw = lo(il)
                    v = ap2[:, o:o + tw]
                    if kind == "s":
                        return v.rearrange("p (s t) -> p s t", t=DSEQ)
                    return v

                S = {}
                mix_g, mixB_g = (mixc, mixcB) if kind == "s" else (mix_t, mixB)

                nctx = [dict() for _ in tl]

                def g_steps():
                    out = []
                    for il, ti in enumerate(tl):
                        out += norm_steps(ti, nctx[il])
                    return out

                def g_apply():
                    for il, ti in enumerate(tl):
                        o, tw = lo(il)
                        norm_apply(ti, nctx[il]["n"], lambda k, o=o, tw=tw: a_t[:, k, o:o + tw], [aB[k][il] for k in range(KT)], V_NMIX + l * 8)

                def inproj(slot, c, il):
                    o, tw = lo(il)
                    lhs = [wap(slot, c, k) for k in range(KT)]
                    rhs = [(a_t[:, k, o:o + tw], aB[k][il]) for k in range(KT)]
                    return mm_group(lhs, rhs, tw)

                def u_chunk(g):
                    es_ = g % 2
                    eoff = E_OFF[es_]
                    if kind == "p":
                        if grp["first"]:
                            P.op("dve", lambda e: e.memset(af(eoff, POOL_BUF), 0.0), writes=[EhB[es_]])
                        else:
                            P.op("dve", lambda e: e.tensor_copy(out=af(eoff, POOL_BUF), in_=carryE[:, g, 0:POOL_BUF]),
                                 reads=[carryEB[g]], writes=[EhB[es_]])
                    else:
                        P.op("dve", lambda e: e.tensor_copy(
                            out=af(eoff, NSEQ * ER).rearrange("p (s r) -> p s r", r=ER)[:, :, 0:POOL_BUF],
                            in_=EH[:, g, :].rearrange("p (s r) -> p s r", r=POOL_BUF)),
                            reads=[EHB], writes=[EhB[es_]])
                    for il in range(ntl):
                        b = inproj(S["u"], g, il)
                        P.op("act", lambda e, b=b, il=il: e.activation(out=etok(eoff, il), in_=pview(b, il), func=AF.Copy),
                             reads=[psB[b]], writes=[EtB[es_][il]])
                        if kind == "s":
                            P.op("act", lambda e, b=b: e.activation(out=af(UD_OFF[es_], 128), in_=psum[b][:, 0:128], func=AF.Copy),
                                 reads=[psB[b]], writes=[UdB[es_]])

                def pool_ew(g):
                    es_ = g % 2
                    eoff = E_OFF[es_]
                    nlev = g + 1
                    Ereads = [EhB[es_]] + [EtB[es_][il] for il in range(ntl)]
                    src_off, srcB = eoff, Ereads
                    for L in range(1, nlev + 1):
                        sh = 1 << (L - 1)
                        v = (1 << L) - 1
                        ts_ = (L - 1) % 2
                        doff = T_OFF[ts_]
                        P.op("dve", lambda e, doff=doff, src_off=src_off, v=v, sh=sh: e.tensor_tensor(
                            out=af(doff + v, EW - v), in0=af(src_off + v, EW - v), in1=af(src_off + v - sh, EW - v), op=ALU.add),
                            reads=srcB, writes=[TB[ts_]])
                        src_off, srcB = doff, [TB[ts_]]
                    w_ = 1 << nlev
                    doff_ = D_OFF[es_]
                    dten = ab(doff_, 512)
                    for il in range(ntl):
                        P.op("dve", lambda e, il=il, src_off=src_off: e.scalar_tensor_tensor(
                            out=dense(dten, il), in0=etok(src_off, il), scalar=1.0 / w_, in1=etok(eoff, il), op0=ALU.mult, op1=ALU.subtract),
                            reads=srcB + [EtB[es_][il]], writes=[DB[es_][il]])
                    if kind == "p" and grp["first"]:
                        nfix = w_ - 1
                        P.op("dve", lambda e, src_off=src_off: e.tensor_tensor(
                            out=af(FIX_OFF, nfix), in0=af(src_off + POOL_BUF, nfix), in1=invcnt[:, 0:nfix], op=ALU.mult),
                            reads=srcB + [invB], writes=[fixB])
                        P.op("dve", lambda e: e.tensor_tensor(
                            out=dten[:, 0:nfix], in0=af(FIX_OFF, nfix), in1=af(eoff + POOL_BUF, nfix), op=ALU.subtract),
                            reads=[fixB, EtB[es_][0]], writes=[DB[es_][0]])
                    if kind == "p" and not grp["last"]:
                        P.op("dve", lambda e: e.tensor_copy(out=carryE[:, g, 0:POOL_BUF], in_=af(eoff + gw, POOL_BUF)),
                             reads=[EtB[es_][ntl - 1]], writes=[carryEB[g]])

                def pool_state(g):
                    es_ = g % 2
                    eoff = E_OFF[es_]
                    if kind == "p" and grp["last"]:
                        b = next_bank()
                        P.op("pe", lambda e, b=b: e.transpose(out=psum[b][0:POOL_BUF, 0:128], in_=af(eoff + gw, POOL_BUF), identity=ident[:]),
                             reads=[EtB[es_][ntl - 1], identB], writes=[psB[b]])
                        P.op("act", lambda e, b=b: e.activation(out=so_stg[0:POOL_BUF, 0, g * 128:(g + 1) * 128], in_=psum[b][0:POOL_BUF, 0:128], func=AF.Copy),
                             reads=[psB[b]], writes=[soB[0]])
                        if g == 3:
                            P.op("sp", lambda e: e.dma_start(out=npp[l], in_=so_stg[0:POOL_BUF, 0, :]), reads=[soB[0]], chan=so_ch[0])
                    elif kind == "s":
                        b = next_bank()
                        P.op("pe", lambda e, b=b: e.transpose(out=psum[b][:, 0:128], in_=af(UD_OFF[es_], 128), identity=ident[:]),
                             reads=[UdB[es_], identB], writes=[psB[b]])
                        P.op("act", lambda e, b=b: e.activation(out=so_stg[:, 0, g * 128:(g + 1) * 128], in_=psum[b][:, 0:128], func=AF.Copy),
                             reads=[psB[b]], writes=[soB[0]])
                        if g == 3:
                            for s_ in range(NSEQ):
                                P.op("sp", lambda e, s_=s_: e.dma_start(out=nps[l, s_, POOL_BUF - DSEQ:POOL_BUF, :], in_=so_stg[s_ * DSEQ:(s_ + 1) * DSEQ, 0, :]),
                                     reads=[soB[0]], chan=so_ch[0])

                def pool_mm(g):
                    es_ = g % 2
                    dten = ab(D_OFF[es_], 512)
                    for il in range(ntl):
                        o, tw = lo(il)
                        b = mm_group([(poolw[:, g, :], poolwB)], [(dten[:, o:o + tw], DB[es_][il])], tw)
                        P.op("act", lambda e, b=b, o=o, tw=tw: e.activation(out=mix_g[:, g, o:o + tw], in_=psum[b][:, 0:tw], func=AF.Identity,
                                                                            scale=vcol(V_PSC + l * 4 + g)),
                             reads=[psB[b], vecB], writes=[mixB_g[g][il]])

                def cv_chunk(j):
                    cs_ = j % 2
                    coff = CVE_OFF[cs_]
                    if kind == "p":
                        if grp["first"]:
                            P.op("dve", lambda e: e.memset(af(coff, CONV_BUF), 0.0), writes=[ChB[cs_]])
                        else:
                            P.op("dve", lambda e: e.tensor_copy(out=af(coff, CONV_BUF), in_=carryCV[:, j, :]),
                                 reads=[carryCVB[j]], writes=[ChB[cs_]])
                    else:
                        P.op("dve", lambda e: e.tensor_copy(
                            out=af(coff, NSEQ * CR).rearrange("p (s r) -> p s r", r=CR)[:, :, 0:CONV_BUF],
                            in_=CVH[:, l, j, :].rearrange("p (s r) -> p s r", r=CONV_BUF)),
                            reads=[CVHB], writes=[ChB[cs_]])
                    for il in range(ntl):
                        o, tw = lo(il)
                        bc = inproj(S["c"], j, il)
                        ci = st["r"] % 2
                        st["r"] += 1
                        P.op("act", lambda e, bc=bc, ci=ci, tw=tw: e.activation(out=af(C_OFF[ci], tw), in_=psum[bc][:, 0:tw], func=AF.Copy),
                             reads=[psB[bc]], writes=[CsB[ci]])
                        bv = inproj(S["v"], j, il)
                        P.op("dve", lambda e, bv=bv, ci=ci, il=il: e.tensor_tensor(out=ctok(coff, il, CONV_BUF), in0=dense(af(C_OFF[ci], 512), il) if kind == "s" else af(C_OFF[ci], lo(il)[1]),
                                                                                   in1=pview(bv, il), op=ALU.mult),
                             reads=[CsB[ci], psB[bv]], writes=[CtB[cs_][il]])
                    if kind == "p" and not grp["last"]:
                        P.op("dve", lambda e: e.tensor_copy(out=carryCV[:, j, :], in_=af(coff + gw, CONV_BUF)),
                             reads=[CtB[cs_][ntl - 1]], writes=[carryCVB[j]])

                def conv_state(j):
                    cs_ = j % 2
                    coff = CVE_OFF[cs_]
                    if kind == "p" and grp["last"]:
                        b = next_bank()
                        P.op("pe", lambda e, b=b: e.transpose(out=psum[b][0:CONV_BUF, 0:128], in_=af(coff + gw, CONV_BUF), identity=ident[:]),
                             reads=[CtB[cs_][ntl - 1], identB], writes=[psB[b]])
                        P.op("act", lambda e, b=b: e.activation(out=so_stg[0:CONV_BUF, 1, j * 128:(j + 1) * 128], in_=psum[b][0:CONV_BUF, 0:128], func=AF.Copy),
                             reads=[psB[b]], writes=[soB[1]])
                        if j == 3:
                            P.op("sp", lambda e: e.dma_start(out=ncp[l], in_=so_stg[0:CONV_BUF, 1, :]), reads=[soB[1]], chan=so_ch[1])
                    elif kind == "s":
                        b = next_bank()
                        P.op("dve", lambda e: e.tensor_copy(out=af(CD_OFF, NSEQ * CONV_BUF).rearrange("p (s r) -> p s r", r=CONV_BUF),
                                                            in_=af(coff, NSEQ * CR).rearrange("p (s r) -> p s r", r=CR)[:, :, DSEQ:CR]),
                             reads=[CtB[cs_][0]], writes=[CdB])
                        P.op("pe", lambda e, b=b: e.transpose(out=psum[b][0:NSEQ * CONV_BUF, 0:128],
                                                              in_=af(CD_OFF, NSEQ * CONV_BUF), identity=ident[:]),
                             reads=[CdB, identB], writes=[psB[b]])
                        P.op("act", lambda e, b=b: e.activation(out=so_stg[0:NSEQ * CONV_BUF, 1, j * 128:(j + 1) * 128], in_=psum[b][0:NSEQ * CONV_BUF, 0:128], func=AF.Copy),
                             reads=[psB[b]], writes=[soB[1]])
                        if j == 3:
                            P.op("sp", lambda e: e.dma_start(out=ncs[l], in_=so_stg[0:NSEQ * CONV_BUF, 1, :]), reads=[soB[1]], chan=so_ch[1])

                def conv_z(j):
                    cs_ = j % 2
                    coff = CVE_OFF[cs_]
                    Creads = [ChB[cs_]] + [CtB[cs_][il] for il in range(ntl)]
                    n_ = CW - 2
                    cwb = V_CW + l * 12 + j
                    P.op("act", lambda e: e.activation(out=af(Z_OFF, n_), in_=af(coff + 2, n_), func=AF.Identity, scale=vcol(cwb + 8), bias=vcol(V_CB + l * 4 + j)),
                         reads=Creads + [vecB], writes=[ZB])
                    P.op("dve", lambda e: e.scalar_tensor_tensor(out=af(Z_OFF, n_), in0=af(coff + 1, n_), scalar=vcol(cwb + 4), in1=af(Z_OFF, n_),
                                                                 op0=ALU.mult, op1=ALU.add),
                         reads=Creads + [vecB, ZB], writes=[ZB])
                    P.op("dve", lambda e: e.scalar_tensor_tensor(out=af(Z_OFF, n_), in0=af(coff, n_), scalar=vcol(cwb), in1=af(Z_OFF, n_),
                                                                 op0=ALU.mult, op1=ALU.add),
                         reads=Creads + [vecB, ZB], writes=[ZB])

                def b_chunk(j):
                    bs_ = j % 2
                    for il in range(ntl):
                        o, tw = lo(il)
                        bb = inproj(S["b"], j, il)
                        P.op("act", lambda e, bb=bb, o=o, tw=tw: e.activation(out=af(BS_OFF[bs_] + o, tw), in_=psum[bb][:, 0:tw], func=AF.Copy),
                             reads=[psB[bb]], writes=[BsB[bs_][il]])

                def y_chunk(j):
                    bs_ = j % 2
                    for il in range(ntl):
                        P.op("dve", lambda e, il=il: e.tensor_tensor(out=dense(mix_g[:, 4 + j, :], il), in0=ctok(Z_OFF, il, 0), in1=dense(af(BS_OFF[bs_], 1024), il), op=ALU.mult),
                             reads=[ZB, BsB[bs_][il]], writes=[mixB_g[4 + j][il]])

                def g_body(hooks=(), extra=False, lazy=False):
                    hooks = list(hooks)

                    def hk():
                        if hooks:
                            hooks.pop(0)()
                    S["u"] = ws_take()
                    if not lazy:
                        S["c"] = ws_take()
                        S["v"] = ws_take()
                    else:
                        hk()
                    u_chunk(0)
                    hk()
                    u_chunk(1)
                    ws_release_half(S["u"], 0)
                    hk()
                    pool_ew(0)
                    pool_state(0)
                    u_chunk(2)
                    hk()
                    pool_ew(1)
                    pool_state(1)
                    u_chunk(3)
                    hk()
                    ws_release(S["u"])
                    if lazy:
                        S["c"] = ws_take()
                        S["v"] = ws_take()
                    else:
                        S["b"] = ws_take()
                    pool_mm(0)
                    pool_ew(2)
                    pool_state(2)
                    if lazy:
                        cv_chunk(0); hk(); conv_state(0); pool_mm(1); pool_ew(3); pool_state(3); conv_z(0)
                        cv_chunk(1)
                        ws_release_half(S["c"], 0)
                        ws_release_half(S["v"], 0)
                        hk(); conv_state(1)
                        S["b"] = ws_take()
                        b_chunk(0); y_chunk(0); conv_z(1); b_chunk(1)
                        ws_release_half(S["b"], 0)
                        y_chunk(1); pool_mm(2)
                        cv_chunk(2); hk(); conv_state(2); conv_z(2); b_chunk(2); y_chunk(2); pool_mm(3)
                        cv_chunk(3); hk(); conv_state(3); conv_z(3); b_chunk(3); y_chunk(3)
                    for j in (range(4) if not lazy else ()):
                        cv_chunk(j)
                        if j == 1:
                            ws_release_half(S["c"], 0)
                            ws_release_half(S["v"], 0)
                        hk()
                        conv_state(j)
                        if j == 0:
                            pool_mm(1)
                            pool_ew(3)
                            pool_state(3)
                        conv_z(j)
                        b_chunk(j)
                        if j == 1:
                            ws_release_half(S["b"], 0)
                        if extra:
                            hk()
                        y_chunk(j)
                        if j >= 1 and j <= 2:
                            pool_mm(j + 1)
                    ws_release(S["c"])
                    ws_release(S["v"])
                    ws_release(S["b"])
                    while hooks:
                        hk()

                def g_outproj_gen():
                    slot_o = [ws_take(), ws_take()]
                    yield
                    for c in range(KT):
                        for il in range(ntl):
                            o, tw = lo(il)
                            ti = tl[il]
                            toff = TILES[ti][0]
                            so_ = slot_o[c // 4]
                            lhs = [wap(so_, c % 4, k) for k in range(KT)]
                            rhs = [(mix_g[:, k, o:o + tw], mixB_g[k][il]) for k in range(KT)]
                            b = mm_group(lhs, rhs, tw)
                            P.op("dve", lambda e, b=b, c=c, toff=toff, tw=tw: e.tensor_tensor(out=h[:, c, toff:toff + tw], in0=h[:, c, toff:toff + tw], in1=psum[b][:, 0:tw], op=ALU.add),
                                 reads=[psB[b]] + hb(c, toff, tw), writes=hb(c, toff, tw))
                            yield
                        if c == 1:
                            ws_release_half(slot_o[0], 0)
                        if c == 3:
                            ws_release(slot_o[0])
                        if c == 5:
                            ws_release_half(slot_o[1], 0)
                    ws_release(slot_o[1])

                def g_outproj():
                    for _ in g_outproj_gen():
                        pass

                return g_steps, g_apply, g_body, g_outproj, g_outproj_gen

            NPAIR = 9

            def p_geom(idx):
                if idx < 8:
                    return idx * 256, 2
                return SEQ, 1

            def p_dma(idx):
                col0, n = p_geom(idx)
                base = (idx % 2) * 2
                for s_ in range(n):
                    pi = base + s_
                    col = col0 + s_ * 128
                    src = pp[l, col:col + 128, :] if col < SEQ else psm[l]
                    P.op("sp", lambda e, pi=pi, src=src: e.dma_start(out=pstg[:, pi, :], in_=src), writes=[pstgB[pi]], chan=pl_ch[pi])

            def p_pair(idx):
                def f_():
                    col0, n = p_geom(idx)
                    base = (idx % 2) * 2
                    if idx + 1 < NPAIR:
                        p_dma(idx + 1)
                    w_ = n * 128
                    for kk in range(2):
                        b = next_bank()
                        for s_ in range(n):
                            pi = base + s_
                            P.op("pe", lambda e, b=b, s_=s_, pi=pi, kk=kk: e.transpose(out=psum[b][:, s_ * 128:(s_ + 1) * 128], in_=pstg[:, pi, kk * 128:(kk + 1) * 128], identity=ident[:]),
                                 reads=[pstgB[pi], identB], writes=[psB[b]], signal=(s_ == n - 1))
                        P.op("act", lambda e, b=b, kk=kk: e.activation(out=pT_t[:, kk, col0:col0 + w_], in_=psum[b][:, 0:w_], func=AF.Copy),
                             reads=[psB[b]], writes=ptb(kk, col0, w_))
                return f_

            mctx = [dict(), dict()]
            ms0 = norm_steps(0, mctx[0])
            ms1 = norm_steps(1, mctx[1])
            c_hooks = [ms0[0], p_pair(5), ms0[1], p_pair(6), ms0[2], p_pair(7), ms1[0], p_pair(8), ms1[1], ms1[2]]

            gs = [do_group(l, grp) for grp in GROUPS]
            if l == 0:
                for f_ in gs[0][0]():
                    f_()
                a_apply = gs[0][1]
            else:
                a_apply = pre_apply
            a_apply()
            if l == 0:
                gs[0][2](late_hooks + gs[1][0](), extra=True)
            else:
                gs[0][2](gs[1][0]())
            gs[1][1]()
            gs[0][3]()
            cs_ = gs[2][0]()
            p_dma(0)
            gs[1][2]([cs_[0], p_pair(0), cs_[1], p_pair(1), cs_[2], p_pair(2), p_pair(3), p_pair(4)])
            gs[2][1]()
            og = gs[1][4]()
            next(og)

            def adv(n, filler):
                def f_():
                    for _ in range(n):
                        next(og, None)
                    filler()
                return f_
            sched = [3, 3, 3, 3, 1, 1, 1, 1]
            hooksC = [adv(sched[i], c_hooks[i]) for i in range(8)] + c_hooks[8:]
            gs[2][2](hooksC, lazy=True)
            for _ in og:
                pass
            gs[2][3]()

            P.stage_barrier()
            NTL = len(MT)
            if l + 1 < DEPTH:
                load_sample_state(l + 1)

            def mlp_norm(ti):
                toff, tw = MT[ti]
                norm_tile(ti, lambda k, toff=toff, tw=tw: m_t[:, k, toff:toff + tw], [mB[k][ti] for k in range(KT)], V_NMLP + l * 8, T=MT)

            def up_unit(su, c, ti):
                toff, tw = MT[ti]
                s_ = su[c // 4]
                lhs = [wap(s_, c % 4, k) for k in range(KT)]
                rhs = [(m_t[:, k, toff:toff + tw], mB[k][ti]) for k in range(KT)]
                b = mm_group(lhs, rhs, tw)
                ri = st["r"] % 3
                st["r"] += 1
                P.op("act", lambda e, b=b, ri=ri, tw=tw: e.activation(out=af(R_OFF[ri], tw), in_=psum[b][:, 0:tw], func=AF.Relu),
                     reads=[psB[b]], writes=[rB[ri]])
                P.op("dve", lambda e, ri=ri, c=c, toff=toff, tw=tw: e.tensor_tensor(out=f_t[:, c, toff:toff + tw], in0=af(R_OFF[ri], tw), in1=af(R_OFF[ri], tw), op=ALU.mult),
                     reads=[rB[ri]], writes=[fB[c][ti]])

            def dn_unit(sd, c, ti):
                toff, tw = MT[ti]
                s_ = sd[c // 4]
                lhs = [wap(s_, c % 4, k) for k in range(KT)]
                rhs = [(f_t[:, k, toff:toff + tw], fB[k][ti]) for k in range(KT)]
                b = mm_group(lhs, rhs, tw)
                P.op("dve", lambda e, b=b, c=c, toff=toff, tw=tw: e.tensor_tensor(out=h[:, c, toff:toff + tw], in0=h[:, c, toff:toff + tw], in1=psum[b][:, 0:tw], op=ALU.add),
                     reads=[psB[b]] + hb(c, toff, tw), writes=hb(c, toff, tw))

            def ple_norm(ti):
                toff, tw = MT[ti]
                norm_tile(ti, lambda k, toff=toff, tw=tw: n_t[:, k, toff:toff + tw], [nB[k][ti] for k in range(KT)], V_NPLE + l * 8, T=MT)

            for q in range(4):
                su = [ws_take(), ws_take()]
                if q == 0:
                    for t01 in range(2):
                        toff_, tw_ = MT[t01]
                        norm_apply(t01, mctx[t01]["n"], lambda k, toff_=toff_, tw_=tw_: m_t[:, k, toff_:toff_ + tw_], [mB[k][t01] for k in range(KT)], V_NMLP + l * 8, T=MT)
                    for ti in range(NTL):
                        nx = ti + 2
                        cx = {}
                        stp = norm_steps(nx, cx, T=MT) if nx < NTL else []
                        for c in range(KT):
                            up_unit(su, c, ti)
                            if stp and c in (0, 2, 4):
                                stp.pop(0)()
                                if c == 4:
                                    toff_, tw_ = MT[nx]
                                    norm_apply(nx, cx["n"], lambda k, toff_=toff_, tw_=tw_: m_t[:, k, toff_:toff_ + tw_], [mB[k][nx] for k in range(KT)], V_NMLP + l * 8, T=MT)
                    ws_release(su[0])
                    ws_release(su[1])
                else:
                    for c in range(KT):
                        for ti in range(NTL):
                            up_unit(su, c, ti)
                        if c == 1:
                            ws_release_half(su[0], 0)
                        if c == 3:
                            ws_release(su[0])
                        if c == 5:
                            ws_release_half(su[1], 0)
                    ws_release(su[1])
                sd = [ws_take(), ws_take()]
                if q < 3:
                    for c in range(KT):
                        for ti in range(NTL):
                            dn_unit(sd, c, ti)
                        if c == 1:
                            ws_release_half(sd[0], 0)
                        if c == 3:
                            ws_release(sd[0])
                        if c == 5:
                            ws_release_half(sd[1], 0)
                    ws_release(sd[1])
                else:
                    for ti in range(NTL):
                        for c in range(KT):
                            dn_unit(sd, c, ti)
                        if ti >= 1:
                            ple_norm(ti - 1)
                    ws_release(sd[0])
                    ws_release(sd[1])

            P.stage_barrier()
            sg = [ws_take(), ws_take()]

            def ple_unit(c, ti):
                toff, tw = MT[ti]
                s_ = sg[c // 4]
                lhs = [wap(s_, c % 4, k) for k in range(KT)]
                rhs = [(n_t[:, k, toff:toff + tw], nB[k][ti]) for k in range(KT)]
                b1 = mm_group(lhs, rhs, tw)
                lhs2 = [(wple[:, kk, c * 128:(c + 1) * 128], wpleB) for kk in range(2)]
                rhs2 = [(pT_t[:, kk, toff:toff + tw], ptb(kk, toff, tw)) for kk in range(2)]
                b2 = mm_group(lhs2, rhs2, tw)
                gi_ = st["g"] % 2
                st["g"] += 1
                P.op("act", lambda e, b1=b1, gi_=gi_, tw=tw: e.activation(out=af(G_OFF[gi_], tw), in_=psum[b1][:, 0:tw], func=AF.Sigmoid),
                     reads=[psB[b1]], writes=[gB[gi_]])
                P.op("dve", lambda e, b2=b2, gi_=gi_, tw=tw: e.tensor_tensor(out=af(TMP_OFF[gi_], tw), in0=af(G_OFF[gi_], tw), in1=psum[b2][:, 0:tw], op=ALU.mult),
                     reads=[gB[gi_], psB[b2]], writes=[tmpB[gi_]])
                P.op("dve", lambda e, gi_=gi_, c=c, toff=toff, tw=tw: e.tensor_tensor(out=h[:, c, toff:toff + tw], in0=h[:, c, toff:toff + tw], in1=af(TMP_OFF[gi_], tw), op=ALU.add),
                     reads=[tmpB[gi_]] + hb(c, toff, tw), writes=hb(c, toff, tw))

            def final_norm(ti):
                toff, tw = MT[ti]
                norm_tile(ti, lambda k, tw=tw: af(YFM_OFF + k * 512, tw), yfmB, V_NF, T=MT)

            def final_tile(ti):
                toff, tw = MT[ti]
                for s_ in range(tw // 128):
                    ti_ = st["y"] % 2
                    st["y"] += 1
                    for hf in range(2):
                        b = next_bank()
                        for kk in range(4):
                            k = hf * 4 + kk
                            P.op("pe", lambda e, b=b, kk=kk, k=k, s_=s_: e.transpose(out=psum[b][:, kk * 128:(kk + 1) * 128],
                                                                                    in_=af(YFM_OFF + k * 512 + s_ * 128, 128), identity=ident[:]),
                                 reads=[yfmB[k], identB], writes=[psB[b]], signal=(kk == 3))
                        if hf == 0:
                            P.op("act", lambda e, b=b, ti_=ti_: e.activation(out=af(YTM_OFF[ti_], 512), in_=psum[b][:], func=AF.Copy),
                                 reads=[psB[b]], writes=[ytmB[ti_]])
                        else:
                            P.op("dve", lambda e, b=b, ti_=ti_: e.tensor_copy(out=af(YTM_OFF[ti_] + 512, 512), in_=psum[b][:]),
                                 reads=[psB[b], ytmB[ti_]], writes=[ytmB[ti_]])
                    col_ = toff + s_ * 128
                    dst = y_p[col_:col_ + 128, :] if col_ < SEQ else y_s
                    P.op("sp", lambda e, ti_=ti_, dst=dst: e.dma_start(out=dst, in_=af(YTM_OFF[ti_], 1024)), reads=[ytmB[ti_]], chan=st_ch[ti_])

            last = (l == DEPTH - 1)
            nhooks = []
            if not last:
                nA = do_group(l + 1, GROUPS[0])
                nhooks = nA[0]()
                pre_apply = nA[1]
            lctx = {}
            lsteps = norm_steps(NTL - 1, lctx, T=MT)
            for ti in range(NTL):
                fsteps, fctx = [], {}
                if last and ti >= 1:
                    fsteps = norm_steps(ti - 1, fctx, T=MT)
                for c in range(KT):
                    ple_unit(c, ti)
                    if ti == 0 and c in (0, 2, 4):
                        lsteps.pop(0)()
                        if c == 4:
                            toff_, tw_ = MT[NTL - 1]
                            norm_apply(NTL - 1, lctx["n"], lambda k, toff_=toff_, tw_=tw_: n_t[:, k, toff_:toff_ + tw_], [nB[k][NTL - 1] for k in range(KT)], V_NPLE + l * 8, T=MT)
                    if nhooks and ti >= 2 and c in (1, 3, 5):
                        nhooks.pop(0)()
                    if fsteps and c in (0, 2, 4):
                        fsteps.pop(0)()
                        if c == 4:
                            tw_ = MT[ti - 1][1]
                            norm_apply(ti - 1, fctx["n"], lambda k, tw_=tw_: af(YFM_OFF + k * 512, tw_), yfmB, V_NF, T=MT)
                if last and ti >= 1:
                    final_tile(ti - 1)
            ws_release(sg[0])
            ws_release(sg[1])
            while nhooks:
                nhooks.pop(0)()
            if last:
                final_norm(NTL - 1)
                final_tile(NTL - 1)

        assert ws["taken"] == len(wq) and ws["issued"] == len(wq), (ws["taken"], ws["issued"], len(wq))

        final_waits = [(c, c.count) for c in st_ch + so_ch + [d2d_ch] if c.count > 0]

        all_ch = list(P.engs.values()) + P.chans
        for c in all_ch:
            c.sem = es.enter_context(nc.semaphore(c.name))
        block = es.enter_context(nc.Block())

        def replay(name, e, tail=()):
            for waits, fn, sig in P.progs[name]:
                for c, v in waits:
                    e.wait_ge(c.sem, v)
                ins = fn(e)
                if sig is not None:
                    ins.then_inc(sig[0].sem, sig[1])
            for c, v in tail:
                e.wait_ge(c.sem, v)

        @block.tensor
        def _(e):
            replay("pe", e)

        @block.scalar
        def _(e):
            replay("act", e)

        @block.vector
        def _(e):
            replay("dve", e)

        @block.gpsimd
        def _(e):
            replay("pool", e)

        @block.sync
        def _(e):
            replay("sp", e, tail=final_waits)

    return nc


_NC = None


def kernel(x_prompt, x_sample, state_pool, state_conv, p_prompt, p_sample,
           norm_mix, w_in, pool_w, pool_scale, conv_w, conv_b, w_out,
           norm_mlp, w_up, w_down, norm_ple, w_ple_gate, w_ple_proj, norm_f):
    global _NC
    f = lambda a: np.ascontiguousarray(np.asarray(a, dtype=np.float32))
    if _NC is None:
        _NC = build()
    nc = _NC
    vecs = np.concatenate([
        f(norm_mix).reshape(16, 128), f(norm_mlp).reshape(16, 128), f(norm_ple).reshape(16, 128),
        f(norm_f).reshape(8, 128), f(pool_scale).reshape(8, 128), f(conv_w).reshape(24, 128),
        f(conv_b).reshape(8, 128)], axis=0)
    shared = dict(vecs=f(vecs), w_in=f(w_in), pool_w=f(pool_w), w_out=f(w_out), w_up=f(w_up),
                  w_down=f(w_down), w_gate=f(w_ple_gate), w_ple=f(w_ple_proj))
    x_prompt, x_sample = f(x_prompt), f(x_sample)
    state_pool, state_conv = f(state_pool), f(state_conv)
    p_prompt, p_sample = f(p_prompt), f(p_sample)
    in_maps = []
    for c in range(NCORES):
        ss = slice(c * NSEQ, (c + 1) * NSEQ)
        m = dict(shared)
        m["x_p"] = x_prompt[c]
        m["x_s"] = f(x_sample[ss].reshape(NS, D))
        m["sp_in"] = f(state_pool[:, ss].reshape(DEPTH, NSEQ * POOL_BUF, 512))
        m["sc_in"] = f(state_conv[:, ss].reshape(DEPTH, NSEQ * CONV_BUF, 512))
        m["pp"] = f(p_prompt[:, c])
        m["psm"] = f(p_sample[:, ss].reshape(DEPTH, NS, PLE))
        in_maps.append(m)
    res = run_bass_kernel_spmd(nc, in_maps, core_ids=list(range(NCORES)))
    R = res.results
    y_prompt = np.stack([R[c]["y_p"] for c in range(NCORES)], axis=0)
    y_sample = np.concatenate([R[c]["y_s"].reshape(NSEQ, DSEQ, D) for c in range(NCORES)], axis=0)
    new_pool_prompt = np.stack([R[c]["npp"] for c in range(NCORES)], axis=1)
    new_conv_prompt = np.stack([R[c]["ncp"] for c in range(NCORES)], axis=1)
    new_pool_sample = np.concatenate([R[c]["nps"] for c in range(NCORES)], axis=1)
    new_conv_sample = np.concatenate([R[c]["ncs"].reshape(DEPTH, NSEQ, CONV_BUF, 512) for c in range(NCORES)], axis=1)
    return (y_prompt.astype(np.float32), y_sample.astype(np.float32), new_pool_prompt.astype(np.float32),
            new_conv_prompt.astype(np.float32), new_pool_sample.astype(np.float32), new_conv_sample.astype(np.float32))
```

```python
import numpy as np
from contextlib import ExitStack
import concourse.bass as bass
import concourse.mybir as mybir
from concourse.bass_utils import run_bass_kernel_spmd

F32 = mybir.dt.float32
BF16 = mybir.dt.bfloat16
AF = mybir.ActivationFunctionType
ALU = mybir.AluOpType

NCORES = 8
D = 1024
KT = 8
SEQ = 2048
NS = 128
NSEQ = 16
DSEQ = 8
NTOK = SEQ + NS
DEPTH = 2
DFF = 4096
PLE = 256
EPS = 1e-6
POOL_BUF = 15
CONV_BUF = 2
ER = POOL_BUF + DSEQ
CR = CONV_BUF + DSEQ

TILES = [(0, 512), (512, 512), (1024, 512), (1536, 512), (2048, 128)]
MT = [(0, 512), (512, 512), (1024, 512), (1536, 384), (1920, 256)]
NSUB = NTOK // 128
GROUPS = [
    dict(tiles=[0, 1], kind="p", first=True, last=False),
    dict(tiles=[2, 3], kind="p", first=False, last=True),
    dict(tiles=[4], kind="s", first=False, last=False),
]

V_NMIX, V_NMLP, V_NPLE, V_NF, V_PSC, V_CW, V_CB = 0, 16, 32, 48, 56, 64, 88
NVEC = 96


class Ch:
    def __init__(self, name):
        self.name = name
        self.count = 0
        self.sem = None


class Buf:
    __slots__ = ("w", "r", "arena")

    def __init__(self, arena=False):
        self.w = {}
        self.r = {}
        self.arena = arena


def _merge(dst, src):
    for c, v in src.items():
        if dst.get(c, 0) < v:
            dst[c] = v


class Prog:
    ENGS = ["pe", "act", "dve", "pool", "sp"]

    def __init__(self):
        self.engs = {n: Ch(n) for n in self.ENGS}
        self.progs = {n: [] for n in self.ENGS}
        self.waited = {n: {} for n in self.ENGS}
        self.chans = []
        self.barrier = {}
        self.arena_bufs = []

    def chan(self, name):
        c = Ch(name)
        self.chans.append(c)
        return c

    def abuf(self):
        b = Buf(arena=True)
        self.arena_bufs.append(b)
        return b

    def stage_barrier(self):
        for b in self.arena_bufs:
            _merge(self.barrier, b.w)
            _merge(self.barrier, b.r)

    def op(self, eng, fn, reads=(), writes=(), signal=True, chan=None):
        need = {}
        for b in reads:
            _merge(need, b.w)
        for b in writes:
            _merge(need, b.w)
            _merge(need, b.r)
            if b.arena:
                _merge(need, self.barrier)
        waits = []
        wd = self.waited[eng]
        pe_ch = self.engs["pe"]
        for c, v in need.items():
            if eng == "pe" and c is pe_ch:
                continue
            if wd.get(c, 0) < v:
                waits.append((c, v))
                wd[c] = v
        if chan is not None:
            chan.count += 16
            tok = {chan: chan.count}
            sig = (chan, 16)
        else:
            ch = self.engs[eng]
            tok = {ch: ch.count + 1}
            if signal:
                ch.count += 1
                sig = (ch, 1)
            else:
                sig = None
        self.progs[eng].append((waits, fn, sig))
        for b in reads:
            _merge(b.r, tok)
        for b in writes:
            b.w = dict(tok)
            b.r = {}
        return tok


def build():
    nc = bass.Bass("TRN2", target_bir_lowering=False)
    dt_in = lambda name, shape: nc.dram_tensor(name, shape, F32, kind="ExternalInput").ap()
    dt_out = lambda name, shape: nc.dram_tensor(name, shape, F32, kind="ExternalOutput").ap()
    x_p = dt_in("x_p", [SEQ, D])
    x_s = dt_in("x_s", [NS, D])
    sp_in = dt_in("sp_in", [DEPTH, NSEQ * POOL_BUF, 512])
    sc_in = dt_in("sc_in", [DEPTH, NSEQ * CONV_BUF, 512])
    pp = dt_in("pp", [DEPTH, SEQ, PLE])
    psm = dt_in("psm", [DEPTH, NS, PLE])
    vecs = dt_in("vecs", [NVEC, 128])
    w_in = dt_in("w_in", [DEPTH, D, 2048])
    pool_w = dt_in("pool_w", [DEPTH, 4, 128, 128])
    w_out = dt_in("w_out", [DEPTH, D, D])
    w_up = dt_in("w_up", [DEPTH, D, DFF])
    w_down = dt_in("w_down", [DEPTH, DFF, D])
    w_gate = dt_in("w_gate", [DEPTH, D, D])
    w_ple = dt_in("w_ple", [DEPTH, PLE, D])
    y_p = dt_out("y_p", [SEQ, D])
    y_s = dt_out("y_s", [NS, D])
    npp = dt_out("npp", [DEPTH, POOL_BUF, 512])
    ncp = dt_out("ncp", [DEPTH, CONV_BUF, 512])
    nps = dt_out("nps", [DEPTH, NSEQ, POOL_BUF, 512])
    ncs = dt_out("ncs", [DEPTH, NSEQ * CONV_BUF, 512])

    P = Prog()
    AW = 19904
    es = ExitStack()
    with es:
        sb = lambda name, shape, dt: es.enter_context(nc.sbuf_tensor(name, shape, dt))
        h = sb("h", [128, KT, NTOK], F32)
        ring = sb("ring", [128, 6, 8, 256], BF16)
        pT_t = sb("pT", [128, 2, NTOK], BF16)
        wple = sb("wple", [128, 2, D], BF16)
        poolw = sb("poolw", [128, 4, 128], BF16)
        sq = sb("sq", [128, 4, 512], BF16)
        rstd = sb("rstd", [128, 2, 512], F32)
        EH = sb("EH", [128, 4, NSEQ * POOL_BUF], F32)
        CVH = sb("CVH", [128, DEPTH, 4, NSEQ * CONV_BUF], F32)
        carryE = sb("carryE", [128, 4, 16], F32)
        carryCV = sb("carryCV", [128, 4, 2], F32)
        vec = sb("vec", [128, NVEC], F32)
        ident = sb("ident", [128, 128], F32)
        ones = sb("ones", [128, 128], BF16)
        invcnt = sb("invcnt", [128, 16], F32)
        so_stg = sb("so_stg", [128, 2, 512], F32)
        pstg = sb("pstg", [128, 4, PLE], F32)
        mixc = sb("mixc", [128, KT, NS], BF16)
        arena = sb("arena", [128, AW], F32)
        psum = [es.enter_context(nc.psum_tensor(f"ps{i}", [128, 512], F32)) for i in range(8)]

        def af(off, n):
            return arena[:, off:off + n]

        def ab(off, nwords):
            return arena[:, off:off + nwords].bitcast(BF16)

        hB = [[Buf() for _ in range(NSUB)] for _ in range(KT)]

        def hb(k, toff, tw):
            return hB[k][toff // 128:(toff + tw) // 128]

        psB = [Buf() for _ in range(8)]
        ringB = [Buf() for _ in range(6)]
        wpleB, poolwB = Buf(), Buf()
        sqB = [Buf() for _ in range(4)]
        rstdB = [Buf(), Buf()]
        EHB, CVHB, vecB, identB, onesB, invB = Buf(), Buf(), Buf(), Buf(), Buf(), Buf()
        carryEB = [Buf() for _ in range(4)]
        carryCVB = [Buf() for _ in range(4)]
        soB = [Buf(), Buf()]
        pstgB = [Buf() for _ in range(4)]
        pTB = [[Buf() for _ in range(NSUB)] for _ in range(2)]

        def ptb(kk, toff, tw):
            return pTB[kk][toff // 128:(toff + tw) // 128]


        st = dict(bank=0, sq=0, norm=0, r=0, g=0, pst=0, y=0)

        reserved = set()

        def next_bank():
            while True:
                b = st["bank"]
                st["bank"] = (b + 1) % 8
                if b not in reserved:
                    return b

        ring_ch = [P.chan(f"ring{i}") for i in range(6)]
        poolw_ch = P.chan("poolw")
        wple_ch = P.chan("wple")
        setup_ch = [P.chan(f"su{i}") for i in range(3)]
        sui = iter(setup_ch)
        ld_ch = [P.chan(f"ld{i}") for i in range(8)]
        pl_ch = [P.chan(f"pl{i}") for i in range(4)]
        st_ch = [P.chan(f"st{i}") for i in range(2)]
        so_ch = [P.chan(f"so{i}") for i in range(2)]
        d2d_ch = P.chan("d2d")

        def vcol(c):
            return vec[:, c:c + 1]

        def wview(w_l, c0):
            v = w_l.rearrange("(k p) n -> p k n", p=128)
            return [v[:, :, c0:c0 + 256], v[:, :, c0 + 256:c0 + 512]]

        wq = []
        for l in range(DEPTH):
            for _g in GROUPS:
                wq += wview(w_in[l], 0) + wview(w_in[l], 1024) + wview(w_in[l], 1536) + wview(w_in[l], 512) \
                    + wview(w_out[l], 0) + wview(w_out[l], 512)
            for q in range(4):
                wq += wview(w_up[l], q * 1024) + wview(w_up[l], q * 1024 + 512) \
                    + wview(w_down[l][q * 1024:(q + 1) * 1024, :], 0) + wview(w_down[l][q * 1024:(q + 1) * 1024, :], 512)
            wq += wview(w_gate[l], 0) + wview(w_gate[l], 512)
        ws = dict(issued=0, taken=0, free=[True] * 6, gate=(), slot={})

        def ws_pump():
            while ws["issued"] < len(wq) and any(ws["free"]):
                i = ws["issued"]
                s_ = ws["free"].index(True)
                ws["slot"][i] = s_
                P.op("pool", lambda e, s_=s_, src=wq[i]: e.dma_start(out=ring[:, s_], in_=src), reads=(ws["gate"] if i in (2, 3, 4, 5) else ()),
                     writes=[ringB[s_]], chan=ring_ch[s_])
                ws["free"][s_] = False
                ws["issued"] += 1

        def ws_take():
            i = ws["taken"]
            ws["taken"] += 2
            assert i + 1 < ws["issued"], "weight block not yet issued"
            return dict(s=[ws["slot"][i], ws["slot"][i + 1]], rel=[False, False])

        def ws_release_half(blk, hh):
            if not blk["rel"][hh]:
                blk["rel"][hh] = True
                ws["free"][blk["s"][hh]] = True
                ws_pump()

        def ws_release(blk):
            ws_release_half(blk, 0)
            ws_release_half(blk, 1)

        def wap(blk, c4, k):
            s_ = blk["s"][c4 // 2]
            return (ring[:, s_, k, (c4 % 2) * 128:(c4 % 2 + 1) * 128], ringB[s_])

        P.op("pool", lambda e: e.memset(ident[:], 0.0), writes=[identB])
        P.op("pool", lambda e: e.affine_select(out=ident[:], in_=ident[:], pattern=[[-1, 128]],
                                               compare_op=ALU.not_equal, fill=1.0, base=0,
                                               channel_multiplier=1), writes=[identB])
        P.op("pool", lambda e: e.memset(ones[:], 1.0), writes=[onesB])
        for t in range(16):
            P.op("pool", lambda e, t=t: e.memset(invcnt[:, t:t + 1], 1.0 / (t + 1)), writes=[invB])

        XSTG = 8192
        SPSTG = 16384
        vstg = pstg[:, 0, 0:128]
        vstgB = pstgB[0]
        P.op("sp", lambda e: e.dma_start(out=vstg[0:NVEC, :], in_=vecs), writes=[vstgB], chan=next(sui))
        b = next_bank()
        P.op("pe", lambda e, b=b: e.transpose(out=psum[b][:, 0:NVEC], in_=vstg[0:NVEC, :], identity=ident[0:NVEC, 0:NVEC]),
             reads=[vstgB, identB], writes=[psB[b]])
        P.op("act", lambda e, b=b: e.activation(out=vec[:], in_=psum[b][:, 0:NVEC], func=AF.Copy),
             reads=[psB[b]], writes=[vecB])

        pstg2 = pstg[:].rearrange("p a c -> p (a c)").rearrange("p (h c) -> p h c", h=2)

        def load_sample_state(l):
            for hf in range(2):
                stg = pstg2[:, hf, :]
                sBs = [pstgB[2 * hf], pstgB[2 * hf + 1]]
                P.op("sp", lambda e, stg=stg, hf=hf: e.dma_start(out=stg[0:120, :], in_=sp_in[l, hf * 120:(hf + 1) * 120, :]),
                     writes=sBs, chan=pl_ch[2 * hf])
                b = next_bank()
                for g in range(4):
                    P.op("pe", lambda e, b=b, g=g, stg=stg: e.transpose(out=psum[b][:, g * 120:(g + 1) * 120], in_=stg[0:120, g * 128:(g + 1) * 128],
                                                                        identity=ident[0:120, 0:120]),
                         reads=sBs + [identB], writes=[psB[b]], signal=(g == 3))
                P.op("act", lambda e, b=b, hf=hf: e.activation(out=EH[:, :, hf * 120:(hf + 1) * 120],
                                                               in_=psum[b][:, 0:480].rearrange("p (g c) -> p g c", g=4), func=AF.Copy),
                     reads=[psB[b]], writes=[EHB])

        load_sample_state(0)
        for l in range(DEPTH):
            stg = so_stg[:, l, :]
            sB_ = soB[l]
            P.op("sp", lambda e, stg=stg, l=l: e.dma_start(out=stg[0:32, :], in_=sc_in[l]), writes=[sB_], chan=next(sui))
            b = next_bank()
            for j in range(4):
                P.op("pe", lambda e, b=b, j=j, stg=stg: e.transpose(out=psum[b][:, j * 32:(j + 1) * 32], in_=stg[0:32, j * 128:(j + 1) * 128],
                                                                    identity=ident[0:32, 0:32]),
                     reads=[sB_, identB], writes=[psB[b]], signal=(j == 3))
            P.op("act", lambda e, b=b, l=l: e.activation(out=CVH[:, l], in_=psum[b][:, 0:128].rearrange("p (g c) -> p g c", g=4), func=AF.Copy),
                 reads=[psB[b]], writes=[CVHB])
            P.op("sp", lambda e, l=l: e.dma_start(out=nps[l, :, 0:POOL_BUF - DSEQ, :],
                                                  in_=sp_in[l].rearrange("(s r) c -> s r c", r=POOL_BUF)[:, DSEQ:POOL_BUF, :]),
                 chan=d2d_ch)

        xstgB = [P.abuf() for _ in range(8)]
        xstg_all = [af(XSTG + i * 1024, 1024) for i in range(8)]
        for ti, (toff, tw) in enumerate(TILES[0:2]):
            nsub = tw // 128
            xo = (ti % 2) * 4
            xstg = xstg_all[xo:xo + 4]
            xsB = xstgB[xo:xo + 4]
            for s_ in range(nsub):
                src = x_p[toff + s_ * 128: toff + (s_ + 1) * 128, :] if toff < SEQ else x_s
                P.op("sp", lambda e, s_=s_, src=src, xstg=xstg: e.dma_start(out=xstg[s_], in_=src), writes=[xsB[s_]], chan=ld_ch[xo + s_])
            if nsub == 4:
                for k in range(KT):
                    b = next_bank()
                    for s_ in range(4):
                        P.op("pe", lambda e, b=b, s_=s_, k=k, xstg=xstg: e.transpose(out=psum[b][:, s_ * 128:(s_ + 1) * 128], in_=xstg[s_][:, k * 128:(k + 1) * 128], identity=ident[:]),
                             reads=[xsB[s_], identB], writes=[psB[b]], signal=(s_ == 3))
                    if ti == 1 and k == 0:
                        gateB = Buf()
                        gateB.w = dict(psB[b].w)
                        ws["gate"] = (gateB,)
                    if k % 2 == 0:
                        P.op("act", lambda e, b=b, k=k, toff=toff: e.activation(out=h[:, k, toff:toff + 512], in_=psum[b][:], func=AF.Copy),
                             reads=[psB[b]], writes=hb(k, toff, 512))
                    else:
                        P.op("dve", lambda e, b=b, k=k, toff=toff: e.tensor_copy(out=h[:, k, toff:toff + 512], in_=psum[b][:]),
                             reads=[psB[b]], writes=hb(k, toff, 512))
            else:
                for kh in range(2):
                    b = next_bank()
                    for kk in range(4):
                        k = kh * 4 + kk
                        P.op("pe", lambda e, b=b, kk=kk, k=k, xstg=xstg: e.transpose(out=psum[b][:, kk * 128:(kk + 1) * 128], in_=xstg[0][:, k * 128:(k + 1) * 128], identity=ident[:]),
                             reads=[xsB[0], identB], writes=[psB[b]], signal=(kk == 3))
                    P.op("act", lambda e, b=b, kh=kh, toff=toff: e.activation(out=h[:, kh * 4:(kh + 1) * 4, toff:toff + 128],
                                                                              in_=psum[b][:].rearrange("p (k c) -> p k c", k=4), func=AF.Copy),
                         reads=[psB[b]], writes=[hb(kh * 4 + kk, toff, 128)[0] for kk in range(4)])

        ws_pump()

        xl = [pT_t[:, j, :].bitcast(F32)[:, 0:1024] for j in range(2)]
        xl_ch = [P.chan(f"xl{j}") for j in range(2)]
        late_list = []
        for ti in range(2, len(TILES)):
            toff, tw = TILES[ti]
            for s_ in range(tw // 128):
                src = x_p[toff + s_ * 128: toff + (s_ + 1) * 128, :] if toff < SEQ else x_s
                late_list.append((ti, toff + s_ * 128, src))

        def late_dma(n):
            ti, col0, src = late_list[n]
            j = n % 2
            P.op("sp", lambda e, j=j, src=src: e.dma_start(out=xl[j], in_=src), writes=list(pTB[j]), chan=xl_ch[j])

        def late_step(n):
            def f_():
                ti, col0, src = late_list[n]
                j = n % 2
                if n + 1 < len(late_list):
                    late_dma(n + 1)
                for hf in range(2):
                    b = next_bank()
                    for kk in range(4):
                        k = hf * 4 + kk
                        P.op("pe", lambda e, b=b, kk=kk, k=k: e.transpose(out=psum[b][:, kk * 128:(kk + 1) * 128], in_=xl[j][:, k * 128:(k + 1) * 128], identity=ident[:]),
                             reads=list(pTB[j]) + [identB], writes=[psB[b]], signal=(kk == 3))
                    P.op("act", lambda e, b=b, hf=hf: e.activation(out=h[:, hf * 4:(hf + 1) * 4, col0:col0 + 128],
                                                                   in_=psum[b][:].rearrange("p (k c) -> p k c", k=4), func=AF.Copy),
                         reads=[psB[b]], writes=[hb(hf * 4 + kk, col0, 128)[0] for kk in range(4)])
            return f_

        late_dma(0)
        late_hooks = [late_step(n) for n in range(len(late_list))]

        def norm_tile(ti, dst_ap_fn, dstB, gbase, T=TILES):
            toff, tw = T[ti]
            b = next_bank()
            for k in range(KT):
                q = st["sq"]
                st["sq"] = (q + 1) % 4
                P.op("act", lambda e, q=q, k=k: e.activation(out=sq[:, q, 0:tw], in_=h[:, k, toff:toff + tw], func=AF.Square),
                     reads=hb(k, toff, tw), writes=[sqB[q]])
                P.op("pe", lambda e, q=q, k=k, b=b: e.matmul(psum[b][:, 0:tw], lhsT=ones[:], rhs=sq[:, q, 0:tw], start=(k == 0), stop=(k == KT - 1)),
                     reads=[sqB[q], onesB], writes=[psB[b]], signal=True)
            n_ = st["norm"]
            st["norm"] = 1 - n_
            P.op("act", lambda e, b=b, n_=n_: e.activation(out=rstd[:, n_, 0:tw], in_=psum[b][:, 0:tw], func=AF.Ln, scale=1.0 / D, bias=EPS),
                 reads=[psB[b]], writes=[rstdB[n_]])
            P.op("act", lambda e, n_=n_: e.activation(out=rstd[:, n_, 0:tw], in_=rstd[:, n_, 0:tw], func=AF.Exp, scale=-0.5),
                 reads=[rstdB[n_]], writes=[rstdB[n_]])
            for k in range(KT):
                P.op("dve", lambda e, k=k, n_=n_: e.scalar_tensor_tensor(out=dst_ap_fn(k), in0=h[:, k, toff:toff + tw], scalar=vcol(gbase + k),
                                                                         in1=rstd[:, n_, 0:tw], op0=ALU.mult, op1=ALU.mult),
                     reads=hb(k, toff, tw) + [rstdB[n_], vecB], writes=[dstB[k]])

        def norm_steps(ti, ctx, T=TILES):
            toff, tw = T[ti]

            def squares(hf):
                for j in range(4):
                    k = hf * 4 + j
                    P.op("act", lambda e, j=j, k=k: e.activation(out=sq[:, j, 0:tw], in_=h[:, k, toff:toff + tw], func=AF.Square),
                         reads=hb(k, toff, tw), writes=[sqB[j]])

            def stats(hf):
                b = ctx["b"]
                for j in range(4):
                    k = hf * 4 + j
                    P.op("pe", lambda e, j=j, k=k, b=b: e.matmul(psum[b][:, 0:tw], lhsT=ones[:], rhs=sq[:, j, 0:tw], start=(k == 0), stop=(k == KT - 1)),
                         reads=[sqB[j], onesB], writes=[psB[b]], signal=True)

            def s1():
                ctx["b"] = next_bank()
                reserved.add(ctx["b"])
                squares(0)

            def s2():
                stats(0)
                squares(1)

            def s3():
                stats(1)
                b = ctx["b"]
                n_ = st["norm"]
                st["norm"] = 1 - n_
                ctx["n"] = n_
                P.op("act", lambda e: e.activation(out=rstd[:, n_, 0:tw], in_=psum[b][:, 0:tw], func=AF.Ln, scale=1.0 / D, bias=EPS),
                     reads=[psB[b]], writes=[rstdB[n_]])
                P.op("act", lambda e: e.activation(out=rstd[:, n_, 0:tw], in_=rstd[:, n_, 0:tw], func=AF.Exp, scale=-0.5),
                     reads=[rstdB[n_]], writes=[rstdB[n_]])
                reserved.discard(b)

            return [s1, s2, s3]

        def norm_apply(ti, n_, dst_ap_fn, dstB, gbase, T=TILES):
            toff, tw = T[ti]
            for k in range(KT):
                P.op("dve", lambda e, k=k: e.scalar_tensor_tensor(out=dst_ap_fn(k), in0=h[:, k, toff:toff + tw], scalar=vcol(gbase + k),
                                                                  in1=rstd[:, n_, 0:tw], op0=ALU.mult, op1=ALU.mult),
                     reads=hb(k, toff, tw) + [rstdB[n_], vecB], writes=[dstB[k]])

        def mm_group(lhs_list, rhs_list, width):
            b = next_bank()
            n = len(lhs_list)
            for i in range(n):
                (lap, lB), (rap, rB) = lhs_list[i], rhs_list[i]
                P.op("pe", lambda e, b=b, lap=lap, rap=rap, i=i: e.matmul(psum[b][:, 0:width], lhsT=lap, rhs=rap, start=(i == 0), stop=(i == n - 1)),
                     reads=(lB if isinstance(lB, list) else [lB]) + (rB if isinstance(rB, list) else [rB]), writes=[psB[b]], signal=(i == n - 1))
            return b

        A_OFF, MIX_OFF = 0, 4096
        E_OFF = [8192, 9232]
        T_OFF = [10272, 11312]
        D_OFF = [12352, 12864]
        CVE_OFF = [13376, 14404]
        Z_OFF = 15432
        C_OFF = [16460, 16972]
        FIX_OFF = 17484
        BS_OFF = [17800, 18824]
        UD_OFF = [17500, 17628]
        CD_OFF = 17756
        a_t = ab(A_OFF, 4096).rearrange("p (k n) -> p k n", k=KT)
        mix_t = ab(MIX_OFF, 4096).rearrange("p (k n) -> p k n", k=KT)
        aB = [[P.abuf() for _ in range(2)] for _ in range(KT)]
        mixB = [[P.abuf() for _ in range(2)] for _ in range(KT)]
        mixcB = [[Buf()] for _ in range(KT)]
        EtB = [[P.abuf() for _ in range(2)] for _ in range(2)]
        EhB = [P.abuf() for _ in range(2)]
        TB = [P.abuf() for _ in range(2)]
        DB = [[P.abuf() for _ in range(2)] for _ in range(2)]
        CtB = [[P.abuf() for _ in range(2)] for _ in range(2)]
        ChB = [P.abuf() for _ in range(2)]
        ZB = P.abuf()
        CsB = [P.abuf() for _ in range(2)]
        fixB = P.abuf()
        BsB = [[P.abuf() for _ in range(2)] for _ in range(2)]
        UdB = [P.abuf(), P.abuf()]
        CdB = P.abuf()
        M_OFF, F_OFF = 0, 8704
        R_OFF = [17408, 17920, 18432]
        m_t = ab(M_OFF, 8704).rearrange("p (k n) -> p k n", k=KT)
        f_t = ab(F_OFF, 8704).rearrange("p (k n) -> p k n", k=KT)
        mB = [[P.abuf() for _ in TILES] for _ in range(KT)]
        fB = [[P.abuf() for _ in TILES] for _ in range(KT)]
        rB = [P.abuf() for _ in range(3)]
        G_OFF = [8704, 9216]
        TMP_OFF = [9728, 10240]
        nB = mB
        n_t = m_t
        gB = [P.abuf() for _ in range(2)]
        tmpB = [P.abuf() for _ in range(2)]
        YFM_OFF = 10752
        YTM_OFF = [14848, 15872]
        yfmB = [P.abuf() for _ in range(KT)]
        ytmB = [P.abuf() for _ in range(2)]

        for l in range(DEPTH):
            P.stage_barrier()
            P.op("pool", lambda e, l=l: e.dma_start(out=poolw[:], in_=pool_w[l].rearrange("g c d -> c g d")), writes=[poolwB], chan=poolw_ch)
            P.op("pool", lambda e, l=l: e.dma_start(out=wple[:], in_=w_ple[l].rearrange("(k p) n -> p k n", p=128)), writes=[wpleB], chan=wple_ch)
            def do_group(l, grp):
                kind = grp["kind"]
                tl = grp["tiles"]
                ntl = len(tl)
                g0 = TILES[tl[0]][0]
                gw = sum(TILES[t][1] for t in tl)
                if kind == "p":
                    EW, CW = POOL_BUF + gw, CONV_BUF + gw
                else:
                    EW, CW = NSEQ * ER, NSEQ * CR

                def lo(il):
                    toff, tw = TILES[tl[il]]
                    return toff - g0, tw

                def etok(off, il):
                    o, tw = lo(il)
                    if kind == "p":
                        return af(off + POOL_BUF + o, tw)
                    return af(off, NSEQ * ER).rearrange("p (s r) -> p s r", r=ER)[:, :, POOL_BUF:ER]

                def ctok(off, il, hist):
                    o, tw = lo(il)
                    if kind == "p":
                        return af(off + hist + o, tw)
                    return af(off, NSEQ * CR).rearrange("p (s r) -> p s r", r=CR)[:, :, hist:hist + DSEQ]

                def pview(b, il):
                    o, tw = lo(il)
                    if kind == "p":
                        return psum[b][:, 0:tw]
                    return psum[b][:, 0:tw].rearrange("p (s t) -> p s t", t=DSEQ)

                def dense(ap2, il):
                    o, tw = lo(il)
                    v = ap2[:, o:o + tw]
                    if kind == "s":
                        return v.rearrange("p (s t) -> p s t", t=DSEQ)
                    return v

                S = {}
                mix_g, mixB_g = (mixc, mixcB) if kind == "s" else (mix_t, mixB)

                nctx = [dict() for _ in tl]

                def g_steps():
                    out = []
                    for il, ti in enumerate(tl):
                        out += norm_steps(ti, nctx[il])
                    return out

                def g_apply():
                    for il, ti in enumerate(tl):
                        o, tw = lo(il)
                        norm_apply(ti, nctx[il]["n"], lambda k, o=o, tw=tw: a_t[:, k, o:o + tw], [aB[k][il] for k in range(KT)], V_NMIX + l * 8)

                def inproj(slot, c, il):
                    o, tw = lo(il)
                    lhs = [wap(slot, c, k) for k in range(KT)]
                    rhs = [(a_t[:, k, o:o + tw], aB[k][il]) for k in range(KT)]
                    return mm_group(lhs, rhs, tw)

                def u_chunk(g):
                    es_ = g % 2
                    eoff = E_OFF[es_]
                    if kind == "p":
                        if grp["first"]:
                            P.op("dve", lambda e: e.memset(af(eoff, POOL_BUF), 0.0), writes=[EhB[es_]])
                        else:
                            P.op("dve", lambda e: e.tensor_copy(out=af(eoff, POOL_BUF), in_=carryE[:, g, 0:POOL_BUF]),
                                 reads=[carryEB[g]], writes=[EhB[es_]])
                    else:
                        P.op("dve", lambda e: e.tensor_copy(
                            out=af(eoff, NSEQ * ER).rearrange("p (s r) -> p s r", r=ER)[:, :, 0:POOL_BUF],
                            in_=EH[:, g, :].rearrange("p (s r) -> p s r", r=POOL_BUF)),
                            reads=[EHB], writes=[EhB[es_]])
                    for il in range(ntl):
                        b = inproj(S["u"], g, il)
                        P.op("act", lambda e, b=b, il=il: e.activation(out=etok(eoff, il), in_=pview(b, il), func=AF.Copy),
                             reads=[psB[b]], writes=[EtB[es_][il]])
                        if kind == "s":
                            P.op("act", lambda e, b=b: e.activation(out=af(UD_OFF[es_], 128), in_=psum[b][:, 0:128], func=AF.Copy),
                                 reads=[psB[b]], writes=[UdB[es_]])

                def pool_ew(g):
                    es_ = g % 2
                    eoff = E_OFF[es_]
                    nlev = g + 1
                    Ereads = [EhB[es_]] + [EtB[es_][il] for il in range(ntl)]
                    src_off, srcB = eoff, Ereads
                    for L in range(1, nlev + 1):
                        sh = 1 << (L - 1)
                        v = (1 << L) - 1
                        ts_ = (L - 1) % 2
                        doff = T_OFF[ts_]
                        P.op("dve", lambda e, doff=doff, src_off=src_off, v=v, sh=sh: e.tensor_tensor(
                            out=af(doff + v, EW - v), in0=af(src_off + v, EW - v), in1=af(src_off + v - sh, EW - v), op=ALU.add),
                            reads=srcB, writes=[TB[ts_]])
                        src_off, srcB = doff, [TB[ts_]]
                    w_ = 1 << nlev
                    doff_ = D_OFF[es_]
                    dten = ab(doff_, 512)
                    for il in range(ntl):
                        P.op("dve", lambda e, il=il, src_off=src_off: e.scalar_tensor_tensor(
                            out=dense(dten, il), in0=etok(src_off, il), scalar=1.0 / w_, in1=etok(eoff, il), op0=ALU.mult, op1=ALU.subtract),
                            reads=srcB + [EtB[es_][il]], writes=[DB[es_][il]])
                    if kind == "p" and grp["first"]:
                        nfix = w_ - 1
                        P.op("dve", lambda e, src_off=src_off: e.tensor_tensor(
                            out=af(FIX_OFF, nfix), in0=af(src_off + POOL_BUF, nfix), in1=invcnt[:, 0:nfix], op=ALU.mult),
                            reads=srcB + [invB], writes=[fixB])
                        P.op("dve", lambda e: e.tensor_tensor(
                            out=dten[:, 0:nfix], in0=af(FIX_OFF, nfix), in1=af(eoff + POOL_BUF, nfix), op=ALU.subtract),
                            reads=[fixB, EtB[es_][0]], writes=[DB[es_][0]])
                    if kind == "p" and not grp["last"]:
                        P.op("dve", lambda e: e.tensor_copy(out=carryE[:, g, 0:POOL_BUF], in_=af(eoff + gw, POOL_BUF)),
                             reads=[EtB[es_][ntl - 1]], writes=[carryEB[g]])

                def pool_state(g):
                    es_ = g % 2
                    eoff = E_OFF[es_]
                    if kind == "p" and grp["last"]:
                        b = next_bank()
                        P.op("pe", lambda e, b=b: e.transpose(out=psum[b][0:POOL_BUF, 0:128], in_=af(eoff + gw, POOL_BUF), identity=ident[:]),
                             reads=[EtB[es_][ntl - 1], identB], writes=[psB[b]])
                        P.op("act", lambda e, b=b: e.activation(out=so_stg[0:POOL_BUF, 0, g * 128:(g + 1) * 128], in_=psum[b][0:POOL_BUF, 0:128], func=AF.Copy),
                             reads=[psB[b]], writes=[soB[0]])
                        if g == 3:
                            P.op("sp", lambda e: e.dma_start(out=npp[l], in_=so_stg[0:POOL_BUF, 0, :]), reads=[soB[0]], chan=so_ch[0])
                    elif kind == "s":
                        b = next_bank()
                        P.op("pe", lambda e, b=b: e.transpose(out=psum[b][:, 0:128], in_=af(UD_OFF[es_], 128), identity=ident[:]),
                             reads=[UdB[es_], identB], writes=[psB[b]])
                        P.op("act", lambda e, b=b: e.activation(out=so_stg[:, 0, g * 128:(g + 1) * 128], in_=psum[b][:, 0:128], func=AF.Copy),
                             reads=[psB[b]], writes=[soB[0]])
                        if g == 3:
                            for s_ in range(NSEQ):
                                P.op("sp", lambda e, s_=s_: e.dma_start(out=nps[l, s_, POOL_BUF - DSEQ:POOL_BUF, :], in_=so_stg[s_ * DSEQ:(s_ + 1) * DSEQ, 0, :]),
                                     reads=[soB[0]], chan=so_ch[0])

                def pool_mm(g):
                    es_ = g % 2
                    dten = ab(D_OFF[es_], 512)
                    for il in range(ntl):
                        o, tw = lo(il)
                        b = mm_group([(poolw[:, g, :], poolwB)], [(dten[:, o:o + tw], DB[es_][il])], tw)
                        P.op("act", lambda e, b=b, o=o, tw=tw: e.activation(out=mix_g[:, g, o:o + tw], in_=psum[b][:, 0:tw], func=AF.Identity,
                                                                            scale=vcol(V_PSC + l * 4 + g)),
                             reads=[psB[b], vecB], writes=[mixB_g[g][il]])

                def cv_chunk(j):
                    cs_ = j % 2
                    coff = CVE_OFF[cs_]
                    if kind == "p":
                        if grp["first"]:
                            P.op("dve", lambda e: e.memset(af(coff, CONV_BUF), 0.0), writes=[ChB[cs_]])
                        else:
                            P.op("dve", lambda e: e.tensor_copy(out=af(coff, CONV_BUF), in_=carryCV[:, j, :]),
                                 reads=[carryCVB[j]], writes=[ChB[cs_]])
                    else:
                        P.op("dve", lambda e: e.tensor_copy(
                            out=af(coff, NSEQ * CR).rearrange("p (s r) -> p s r", r=CR)[:, :, 0:CONV_BUF],
                            in_=CVH[:, l, j, :].rearrange("p (s r) -> p s r", r=CONV_BUF)),
                            reads=[CVHB], writes=[ChB[cs_]])
                    for il in range(ntl):
                        o, tw = lo(il)
                        bc = inproj(S["c"], j, il)
                        ci = st["r"] % 2
                        st["r"] += 1
                        P.op("act", lambda e, bc=bc, ci=ci, tw=tw: e.activation(out=af(C_OFF[ci], tw), in_=psum[bc][:, 0:tw], func=AF.Copy),
                             reads=[psB[bc]], writes=[CsB[ci]])
                        bv = inproj(S["v"], j, il)
                        P.op("dve", lambda e, bv=bv, ci=ci, il=il: e.tensor_tensor(out=ctok(coff, il, CONV_BUF), in0=dense(af(C_OFF[ci], 512), il) if kind == "s" else af(C_OFF[ci], lo(il)[1]),
                                                                                   in1=pview(bv, il), op=ALU.mult),
                             reads=[CsB[ci], psB[bv]], writes=[CtB[cs_][il]])
                    if kind == "p" and not grp["last"]:
                        P.op("dve", lambda e: e.tensor_copy(out=carryCV[:, j, :], in_=af(coff + gw, CONV_BUF)),
                             reads=[CtB[cs_][ntl - 1]], writes=[carryCVB[j]])

                def conv_state(j):
                    cs_ = j % 2
                    coff = CVE_OFF[cs_]
                    if kind == "p" and grp["last"]:
                        b = next_bank()
                        P.op("pe", lambda e, b=b: e.transpose(out=psum[b][0:CONV_BUF, 0:128], in_=af(coff + gw, CONV_BUF), identity=ident[:]),
                             reads=[CtB[cs_][ntl - 1], identB], writes=[psB[b]])
                        P.op("act", lambda e, b=b: e.activation(out=so_stg[0:CONV_BUF, 1, j * 128:(j + 1) * 128], in_=psum[b][0:CONV_BUF, 0:128], func=AF.Copy),
                             reads=[psB[b]], writes=[soB[1]])
                        if j == 3:
                            P.op("sp", lambda e: e.dma_start(out=ncp[l], in_=so_stg[0:CONV_BUF, 1, :]), reads=[soB[1]], chan=so_ch[1])
                    elif kind == "s":
                        b = next_bank()
                        P.op("dve", lambda e: e.tensor_copy(out=af(CD_OFF, NSEQ * CONV_BUF).rearrange("p (s r) -> p s r", r=CONV_BUF),
                                                            in_=af(coff, NSEQ * CR).rearrange("p (s r) -> p s r", r=CR)[:, :, DSEQ:CR]),
                             reads=[CtB[cs_][0]], writes=[CdB])
                        P.op("pe", lambda e, b=b: e.transpose(out=psum[b][0:NSEQ * CONV_BUF, 0:128],
                                                              in_=af(CD_OFF, NSEQ * CONV_BUF), identity=ident[:]),
                             reads=[CdB, identB], writes=[psB[b]])
                        P.op("act", lambda e, b=b: e.activation(out=so_stg[0:NSEQ * CONV_BUF, 1, j * 128:(j + 1) * 128], in_=psum[b][0:NSEQ * CONV_BUF, 0:128], func=AF.Copy),
                             reads=[psB[b]], writes=[soB[1]])
                        if j == 3:
                            P.op("sp", lambda e: e.dma_start(out=ncs[l], in_=so_stg[0:NSEQ * CONV_BUF, 1, :]), reads=[soB[1]], chan=so_ch[1])

                def conv_z(j):
                    cs_ = j % 2
                    coff = CVE_OFF[cs_]
                    Creads = [ChB[cs_]] + [CtB[cs_][il] for il in range(ntl)]
                    n_ = CW - 2
                    cwb = V_CW + l * 12 + j
                    P.op("act", lambda e: e.activation(out=af(Z_OFF, n_), in_=af(coff + 2, n_), func=AF.Identity, scale=vcol(cwb + 8), bias=vcol(V_CB + l * 4 + j)),
                         reads=Creads + [vecB], writes=[ZB])
                    P.op("dve", lambda e: e.scalar_tensor_tensor(out=af(Z_OFF, n_), in0=af(coff + 1, n_), scalar=vcol(cwb + 4), in1=af(Z_OFF, n_),
                                                                 op0=ALU.mult, op1=ALU.add),
                         reads=Creads + [vecB, ZB], writes=[ZB])
                    P.op("dve", lambda e: e.scalar_tensor_tensor(out=af(Z_OFF, n_), in0=af(coff, n_), scalar=vcol(cwb), in1=af(Z_OFF, n_),
                                                                 op0=ALU.mult, op1=ALU.add),
                         reads=Creads + [vecB, ZB], writes=[ZB])

                def b_chunk(j):
                    bs_ = j % 2
                    for il in range(ntl):
                        o, tw = lo(il)
                        bb = inproj(S["b"], j, il)
                        P.op("act", lambda e, bb=bb, o=o, tw=tw: e.activation(out=af(BS_OFF[bs_] + o, tw), in_=psum[bb][:, 0:tw], func=AF.Copy),
                             reads=[psB[bb]], writes=[BsB[bs_][il]])

                def y_chunk(j):
                    bs_ = j % 2
                    for il in range(ntl):
                        P.op("dve", lambda e, il=il: e.tensor_tensor(out=dense(mix_g[:, 4 + j, :], il), in0=ctok(Z_OFF, il, 0), in1=dense(af(BS_OFF[bs_], 1024), il), op=ALU.mult),
                             reads=[ZB, BsB[bs_][il]], writes=[mixB_g[4 + j][il]])

                def g_body(hooks=(), extra=False, lazy=False):
                    hooks = list(hooks)

                    def hk():
                        if hooks:
                            hooks.pop(0)()
                    S["u"] = ws_take()
                    if not lazy:
                        S["c"] = ws_take()
                        S["v"] = ws_take()
                    else:
                        hk()
                    u_chunk(0)
                    hk()
                    u_chunk(1)
                    ws_release_half(S["u"], 0)
                    hk()
                    pool_ew(0)
                    pool_state(0)
                    u_chunk(2)
                    hk()
                    pool_ew(1)
                    pool_state(1)
                    u_chunk(3)
                    hk()
                    ws_release(S["u"])
                    if lazy:
                        S["c"] = ws_take()
                        S["v"] = ws_take()
                    else:
                        S["b"] = ws_take()
                    pool_mm(0)
                    pool_ew(2)
                    pool_state(2)
                    if lazy:
                        cv_chunk(0); hk(); conv_state(0); pool_mm(1); pool_ew(3); pool_state(3); conv_z(0)
                        cv_chunk(1)
                        ws_release_half(S["c"], 0)
                        ws_release_half(S["v"], 0)
                        hk(); conv_state(1)
                        S["b"] = ws_take()
                        b_chunk(0); y_chunk(0); conv_z(1); b_chunk(1)
                        ws_release_half(S["b"], 0)
                        y_chunk(1); pool_mm(2)
                        cv_chunk(2); hk(); conv_state(2); conv_z(2); b_chunk(2); y_chunk(2); pool_mm(3)
                        cv_chunk(3); hk(); conv_state(3); conv_z(3); b_chunk(3); y_chunk(3)
                    for j in (range(4) if not lazy else ()):
                        cv_chunk(j)
                        if j == 1:
                            ws_release_half(S["c"], 0)
                            ws_release_half(S["v"], 0)
                        hk()
                        conv_state(j)
                        if j == 0:
                            pool_mm(1)
                            pool_ew(3)
                            pool_state(3)
                        conv_z(j)
                        b_chunk(j)
                        if j == 1:
                            ws_release_half(S["b"], 0)
                        if extra:
                            hk()
                        y_chunk(j)
                        if j >= 1 and j <= 2:
                            pool_mm(j + 1)
                    ws_release(S["c"])
                    ws_release(S["v"])
                    ws_release(S["b"])
                    while hooks:
                        hk()

                def g_outproj_gen():
                    slot_o = [ws_take(), ws_take()]
                    yield
                    for c in range(KT):
                        for il in range(ntl):
                            o, tw = lo(il)
                            ti = tl[il]
                            toff = TILES[ti][0]
                            so_ = slot_o[c // 4]
                            lhs = [wap(so_, c % 4, k) for k in range(KT)]
                            rhs = [(mix_g[:, k, o:o + tw], mixB_g[k][il]) for k in range(KT)]
                            b = mm_group(lhs, rhs, tw)
                            P.op("dve", lambda e, b=b, c=c, toff=toff, tw=tw: e.tensor_tensor(out=h[:, c, toff:toff + tw], in0=h[:, c, toff:toff + tw], in1=psum[b][:, 0:tw], op=ALU.add),
                                 reads=[psB[b]] + hb(c, toff, tw), writes=hb(c, toff, tw))
                            yield
                        if c == 1:
                            ws_release_half(slot_o[0], 0)
                        if c == 3:
                            ws_release(slot_o[0])
                        if c == 5:
                            ws_release_half(slot_o[1], 0)
                    ws_release(slot_o[1])

                def g_outproj():
                    for _ in g_outproj_gen():
                        pass

                return g_steps, g_apply, g_body, g_outproj, g_outproj_gen

            NPAIR = 9

            def p_geom(idx):
                if idx < 8:
                    return idx * 256, 2
                return SEQ, 1

            def p_dma(idx):
                col0, n = p_geom(idx)
                base = (idx % 2) * 2
                for s_ in range(n):
                    pi = base + s_
                    col = col0 + s_ * 128
                    src = pp[l, col:col + 128, :] if col < SEQ else psm[l]
                    P.op("sp", lambda e, pi=pi, src=src: e.dma_start(out=pstg[:, pi, :], in_=src), writes=[pstgB[pi]], chan=pl_ch[pi])

            def p_pair(idx):
                def f_():
                    col0, n = p_geom(idx)
                    base = (idx % 2) * 2
                    if idx + 1 < NPAIR:
                        p_dma(idx + 1)
                    w_ = n * 128
                    for kk in range(2):
                        b = next_bank()
                        for s_ in range(n):
                            pi = base + s_
                            P.op("pe", lambda e, b=b, s_=s_, pi=pi, kk=kk: e.transpose(out=psum[b][:, s_ * 128:(s_ + 1) * 128], in_=pstg[:, pi, kk * 128:(kk + 1) * 128], identity=ident[:]),
                                 reads=[pstgB[pi], identB], writes=[psB[b]], signal=(s_ == n - 1))
                        P.op("act", lambda e, b=b, kk=kk: e.activation(out=pT_t[:, kk, col0:col0 + w_], in_=psum[b][:, 0:w_], func=AF.Copy),
                             reads=[psB[b]], writes=ptb(kk, col0, w_))
                return f_

            mctx = [dict(), dict()]
            ms0 = norm_steps(0, mctx[0])
            ms1 = norm_steps(1, mctx[1])
            c_hooks = [ms0[0], p_pair(5), ms0[1], p_pair(6), ms0[2], p_pair(7), ms1[0], p_pair(8), ms1[1], ms1[2]]

            gs = [do_group(l, grp) for grp in GROUPS]
            if l == 0:
                for f_ in gs[0][0]():
                    f_()
                a_apply = gs[0][1]
            else:
                a_apply = pre_apply
            a_apply()
            if l == 0:
                gs[0][2](late_hooks + gs[1][0](), extra=True)
            else:
                gs[0][2](gs[1][0]())
            gs[1][1]()
            gs[0][3]()
            cs_ = gs[2][0]()
            p_dma(0)
            gs[1][2]([cs_[0], p_pair(0), cs_[1], p_pair(1), cs_[2], p_pair(2), p_pair(3), p_pair(4)])
            gs[2][1]()
            og = gs[1][4]()
            next(og)

            def adv(n, filler):
                def f_():
                    for _ in range(n):
                        next(og, None)
                    filler()
                return f_
            sched = [3, 3, 3, 3, 1, 1, 1, 1, 1]
            hooksC = [adv(sched[i], c_hooks[i]) for i in range(9)] + c_hooks[9:]
            gs[2][2](hooksC, lazy=True)
            for _ in og:
                pass
            gs[2][3]()

            P.stage_barrier()
            NTL = len(MT)
            if l + 1 < DEPTH:
                load_sample_state(l + 1)

            def mlp_norm(ti):
                toff, tw = MT[ti]
                norm_tile(ti, lambda k, toff=toff, tw=tw: m_t[:, k, toff:toff + tw], [mB[k][ti] for k in range(KT)], V_NMLP + l * 8, T=MT)

            def up_unit(su, c, ti):
                toff, tw = MT[ti]
                s_ = su[c // 4]
                lhs = [wap(s_, c % 4, k) for k in range(KT)]
                rhs = [(m_t[:, k, toff:toff + tw], mB[k][ti]) for k in range(KT)]
                b = mm_group(lhs, rhs, tw)
                ri = st["r"] % 3
                st["r"] += 1
                P.op("act", lambda e, b=b, ri=ri, tw=tw: e.activation(out=af(R_OFF[ri], tw), in_=psum[b][:, 0:tw], func=AF.Relu),
                     reads=[psB[b]], writes=[rB[ri]])
                P.op("dve", lambda e, ri=ri, c=c, toff=toff, tw=tw: e.tensor_tensor(out=f_t[:, c, toff:toff + tw], in0=af(R_OFF[ri], tw), in1=af(R_OFF[ri], tw), op=ALU.mult),
                     reads=[rB[ri]], writes=[fB[c][ti]])

            def dn_unit(sd, c, ti):
                toff, tw = MT[ti]
                s_ = sd[c // 4]
                lhs = [wap(s_, c % 4, k) for k in range(KT)]
                rhs = [(f_t[:, k, toff:toff + tw], fB[k][ti]) for k in range(KT)]
                b = mm_group(lhs, rhs, tw)
                P.op("dve", lambda e, b=b, c=c, toff=toff, tw=tw: e.tensor_tensor(out=h[:, c, toff:toff + tw], in0=h[:, c, toff:toff + tw], in1=psum[b][:, 0:tw], op=ALU.add),
                     reads=[psB[b]] + hb(c, toff, tw), writes=hb(c, toff, tw))

            def ple_norm(ti):
                toff, tw = MT[ti]
                norm_tile(ti, lambda k, toff=toff, tw=tw: n_t[:, k, toff:toff + tw], [nB[k][ti] for k in range(KT)], V_NPLE + l * 8, T=MT)

            for q in range(4):
                su = [ws_take(), ws_take()]
                if q == 0:
                    for t01 in range(2):
                        toff_, tw_ = MT[t01]
                        norm_apply(t01, mctx[t01]["n"], lambda k, toff_=toff_, tw_=tw_: m_t[:, k, toff_:toff_ + tw_], [mB[k][t01] for k in range(KT)], V_NMLP + l * 8, T=MT)
                    for ti in range(NTL):
                        nx = ti + 2
                        cx = {}
                        stp = norm_steps(nx, cx, T=MT) if nx < NTL else []
                        for c in range(KT):
                            up_unit(su, c, ti)
                            if stp and c in (0, 2, 4):
                                stp.pop(0)()
                                if c == 4:
                                    toff_, tw_ = MT[nx]
                                    norm_apply(nx, cx["n"], lambda k, toff_=toff_, tw_=tw_: m_t[:, k, toff_:toff_ + tw_], [mB[k][nx] for k in range(KT)], V_NMLP + l * 8, T=MT)
                    ws_release(su[0])
                    ws_release(su[1])
                else:
                    for c in range(KT):
                        for ti in range(NTL):
                            up_unit(su, c, ti)
                        if c == 1:
                            ws_release_half(su[0], 0)
                        if c == 3:
                            ws_release(su[0])
                        if c == 5:
                            ws_release_half(su[1], 0)
                    ws_release(su[1])
                sd = [ws_take(), ws_take()]
                if q < 3:
                    for c in range(KT):
                        for ti in range(NTL):
                            dn_unit(sd, c, ti)
                        if c == 1:
                            ws_release_half(sd[0], 0)
                        if c == 3:
                            ws_release(sd[0])
                        if c == 5:
                            ws_release_half(sd[1], 0)
                    ws_release(sd[1])
                else:
                    for ti in range(NTL):
                        for c in range(KT):
                            dn_unit(sd, c, ti)
                        if ti >= 1:
                            ple_norm(ti - 1)
                    ws_release(sd[0])
                    ws_release(sd[1])

            P.stage_barrier()
            sg = [ws_take(), ws_take()]

            def ple_unit(c, ti):
                toff, tw = MT[ti]
                s_ = sg[c // 4]
                lhs = [wap(s_, c % 4, k) for k in range(KT)]
                rhs = [(n_t[:, k, toff:toff + tw], nB[k][ti]) for k in range(KT)]
                b1 = mm_group(lhs, rhs, tw)
                lhs2 = [(wple[:, kk, c * 128:(c + 1) * 128], wpleB) for kk in range(2)]
                rhs2 = [(pT_t[:, kk, toff:toff + tw], ptb(kk, toff, tw)) for kk in range(2)]
                b2 = mm_group(lhs2, rhs2, tw)
                gi_ = st["g"] % 2
                st["g"] += 1
                P.op("act", lambda e, b1=b1, gi_=gi_, tw=tw: e.activation(out=af(G_OFF[gi_], tw), in_=psum[b1][:, 0:tw], func=AF.Sigmoid),
                     reads=[psB[b1]], writes=[gB[gi_]])
                P.op("dve", lambda e, b2=b2, gi_=gi_, tw=tw: e.tensor_tensor(out=af(TMP_OFF[gi_], tw), in0=af(G_OFF[gi_], tw), in1=psum[b2][:, 0:tw], op=ALU.mult),
                     reads=[gB[gi_], psB[b2]], writes=[tmpB[gi_]])
                P.op("dve", lambda e, gi_=gi_, c=c, toff=toff, tw=tw: e.tensor_tensor(out=h[:, c, toff:toff + tw], in0=h[:, c, toff:toff + tw], in1=af(TMP_OFF[gi_], tw), op=ALU.add),
                     reads=[tmpB[gi_]] + hb(c, toff, tw), writes=hb(c, toff, tw))

            def final_norm(ti):
                toff, tw = MT[ti]
                norm_tile(ti, lambda k, tw=tw: af(YFM_OFF + k * 512, tw), yfmB, V_NF, T=MT)

            def final_tile(ti):
                toff, tw = MT[ti]
                for s_ in range(tw // 128):
                    ti_ = st["y"] % 2
                    st["y"] += 1
                    for hf in range(2):
                        b = next_bank()
                        for kk in range(4):
                            k = hf * 4 + kk
                            P.op("pe", lambda e, b=b, kk=kk, k=k, s_=s_: e.transpose(out=psum[b][:, kk * 128:(kk + 1) * 128],
                                                                                    in_=af(YFM_OFF + k * 512 + s_ * 128, 128), identity=ident[:]),
                                 reads=[yfmB[k], identB], writes=[psB[b]], signal=(kk == 3))
                        if hf == 0:
                            P.op("act", lambda e, b=b, ti_=ti_: e.activation(out=af(YTM_OFF[ti_], 512), in_=psum[b][:], func=AF.Copy),
                                 reads=[psB[b]], writes=[ytmB[ti_]])
                        else:
                            P.op("dve", lambda e, b=b, ti_=ti_: e.tensor_copy(out=af(YTM_OFF[ti_] + 512, 512), in_=psum[b][:]),
                                 reads=[psB[b], ytmB[ti_]], writes=[ytmB[ti_]])
                    col_ = toff + s_ * 128
                    dst = y_p[col_:col_ + 128, :] if col_ < SEQ else y_s
                    P.op("sp", lambda e, ti_=ti_, dst=dst: e.dma_start(out=dst, in_=af(YTM_OFF[ti_], 1024)), reads=[ytmB[ti_]], chan=st_ch[ti_])

            last = (l == DEPTH - 1)
            nhooks = []
            if not last:
                nA = do_group(l + 1, GROUPS[0])
                nhooks = nA[0]()
                pre_apply = nA[1]
            lctx = {}
            lsteps = norm_steps(NTL - 1, lctx, T=MT)
            for ti in range(NTL):
                fsteps, fctx = [], {}
                if last and ti >= 1:
                    fsteps = norm_steps(ti - 1, fctx, T=MT)
                for c in range(KT):
                    ple_unit(c, ti)
                    if ti == 0 and c in (0, 2, 4):
                        lsteps.pop(0)()
                        if c == 4:
                            toff_, tw_ = MT[NTL - 1]
                            norm_apply(NTL - 1, lctx["n"], lambda k, toff_=toff_, tw_=tw_: n_t[:, k, toff_:toff_ + tw_], [nB[k][NTL - 1] for k in range(KT)], V_NPLE + l * 8, T=MT)
                    if nhooks and ti >= 2 and c in (1, 3, 5):
                        nhooks.pop(0)()
                    if fsteps and c in (0, 2, 4):
                        fsteps.pop(0)()
                        if c == 4:
                            tw_ = MT[ti - 1][1]
                            norm_apply(ti - 1, fctx["n"], lambda k, tw_=tw_: af(YFM_OFF + k * 512, tw_), yfmB, V_NF, T=MT)
                if last and ti >= 1:
                    final_tile(ti - 1)
            ws_release(sg[0])
            ws_release(sg[1])
            while nhooks:
                nhooks.pop(0)()
            if last:
                final_norm(NTL - 1)
                final_tile(NTL - 1)

        assert ws["taken"] == len(wq) and ws["issued"] == len(wq), (ws["taken"], ws["issued"], len(wq))

        final_waits = [(c, c.count) for c in st_ch + so_ch + [d2d_ch] if c.count > 0]

        all_ch = list(P.engs.values()) + P.chans
        for c in all_ch:
            c.sem = es.enter_context(nc.semaphore(c.name))
        block = es.enter_context(nc.Block())

        def replay(name, e, tail=()):
            for waits, fn, sig in P.progs[name]:
                for c, v in waits:
                    e.wait_ge(c.sem, v)
                ins = fn(e)
                if sig is not None:
                    ins.then_inc(sig[0].sem, sig[1])
            for c, v in tail:
                e.wait_ge(c.sem, v)

        @block.tensor
        def _(e):
            replay("pe", e)

        @block.scalar
        def _(e):
            replay("act", e)

        @block.vector
        def _(e):
            replay("dve", e)

        @block.gpsimd
        def _(e):
            replay("pool", e)

        @block.sync
        def _(e):
            replay("sp", e, tail=final_waits)

    return nc


_NC = None


def kernel(x_prompt, x_sample, state_pool, state_conv, p_prompt, p_sample,
           norm_mix, w_in, pool_w, pool_scale, conv_w, conv_b, w_out,
           norm_mlp, w_up, w_down, norm_ple, w_ple_gate, w_ple_proj, norm_f):
    global _NC
    f = lambda a: np.ascontiguousarray(np.asarray(a, dtype=np.float32))
    if _NC is None:
        _NC = build()
    nc = _NC
    vecs = np.concatenate([
        f(norm_mix).reshape(16, 128), f(norm_mlp).reshape(16, 128), f(norm_ple).reshape(16, 128),
        f(norm_f).reshape(8, 128), f(pool_scale).reshape(8, 128), f(conv_w).reshape(24, 128),
        f(conv_b).reshape(8, 128)], axis=0)
    shared = dict(vecs=f(vecs), w_in=f(w_in), pool_w=f(pool_w), w_out=f(w_out), w_up=f(w_up),
                  w_down=f(w_down), w_gate=f(w_ple_gate), w_ple=f(w_ple_proj))
    x_prompt, x_sample = f(x_prompt), f(x_sample)
    state_pool, state_conv = f(state_pool), f(state_conv)
    p_prompt, p_sample = f(p_prompt), f(p_sample)
    in_maps = []
    for c in range(NCORES):
        ss = slice(c * NSEQ, (c + 1) * NSEQ)
        m = dict(shared)
        m["x_p"] = x_prompt[c]
        m["x_s"] = f(x_sample[ss].reshape(NS, D))
        m["sp_in"] = f(state_pool[:, ss].reshape(DEPTH, NSEQ * POOL_BUF, 512))
        m["sc_in"] = f(state_conv[:, ss].reshape(DEPTH, NSEQ * CONV_BUF, 512))
        m["pp"] = f(p_prompt[:, c])
        m["psm"] = f(p_sample[:, ss].reshape(DEPTH, NS, PLE))
        in_maps.append(m)
    res = run_bass_kernel_spmd(nc, in_maps, core_ids=list(range(NCORES)))
    R = res.results
    y_prompt = np.stack([R[c]["y_p"] for c in range(NCORES)], axis=0)
    y_sample = np.concatenate([R[c]["y_s"].reshape(NSEQ, DSEQ, D) for c in range(NCORES)], axis=0)
    new_pool_prompt = np.stack([R[c]["npp"] for c in range(NCORES)], axis=1)
    new_conv_prompt = np.stack([R[c]["ncp"] for c in range(NCORES)], axis=1)
    new_pool_sample = np.concatenate([R[c]["nps"] for c in range(NCORES)], axis=1)
    new_conv_sample = np.concatenate([R[c]["ncs"].reshape(DEPTH, NSEQ, CONV_BUF, 512) for c in range(NCORES)], axis=1)
    return (y_prompt.astype(np.float32), y_sample.astype(np.float32), new_pool_prompt.astype(np.float32),
            new_conv_prompt.astype(np.float32), new_pool_sample.astype(np.float32), new_conv_sample.astype(np.float32))
```
